# Optimizing a Trainium2 kernel written in Bass

```python
import jax, jax.numpy as jnp
from jax import lax
import numpy as np

D_MODEL = 1024
BATCH = 4
SEQ = 8192
DEPTH = 1

ATTN_HEADS = 8
HEAD_DIM = 64
ATTN_WIDTH = ATTN_HEADS * HEAD_DIM
MOBA_BLOCK = 256
MOBA_TOPK = 3
Q_CHUNK = 32
LRU_WIDTH = D_MODEL - ATTN_WIDTH
LRU_HEADS = 8
LRU_HEAD_DIM = LRU_WIDTH // LRU_HEADS
LRU_CONV = 4
LRU_C = 8.0
N_IN = 3 * ATTN_WIDTH + 2 * LRU_WIDTH
D_FF = 2816
FFN_CONV = 3
N_MOD = 6
EPS = 1e-6

kernel_name = "hymba_rglru_moba_convffn_adaln"


def rmsnorm(x, g):
    xf = x.astype(jnp.float32)
    y = xf * lax.rsqrt(jnp.mean(xf * xf, axis=-1, keepdims=True) + EPS)
    return (y * g.astype(jnp.float32)).astype(x.dtype)


def causal_dwconv(x, w, b):
    K = w.shape[0]
    S = x.shape[1]
    xp = jnp.pad(x, ((0, 0), (K - 1, 0), (0, 0)))
    y = xp[:, 0:S] * w[0]
    for j in range(1, K):
        y = y + xp[:, j:j + S] * w[j]
    return y + b


def rg_lru(xb, w_a, b_a, w_x, b_x, lam):
    B, S, _ = xb.shape
    xh = xb.reshape(B, S, LRU_HEADS, LRU_HEAD_DIM)
    r = jax.nn.sigmoid(jnp.einsum('bshi,hij->bshj', xh, w_a).reshape(B, S, LRU_WIDTH) + b_a)
    i = jax.nn.sigmoid(jnp.einsum('bshi,hij->bshj', xh, w_x).reshape(B, S, LRU_WIDTH) + b_x)
    log_a = -LRU_C * r.astype(jnp.float32) * jax.nn.softplus(-lam.astype(jnp.float32))
    a = jnp.exp(log_a)
    u = jnp.sqrt(-jnp.expm1(2.0 * log_a)) * (i * xb).astype(jnp.float32)

    def combine(left, right):
        a1, b1 = left
        a2, b2 = right
        return a1 * a2, a2 * b1 + b2

    _, h = lax.associative_scan(combine, (a, u), axis=1)
    return h.astype(xb.dtype)


def moba_attention(q, k, v):
    B, S, H, Dh = q.shape
    nb = -(-S // MOBA_BLOCK)
    s_pad = nb * MOBA_BLOCK
    pad = ((0, 0), (0, s_pad - S), (0, 0), (0, 0))
    q, k, v = [jnp.pad(t, pad).transpose(0, 2, 1, 3) for t in (q, k, v)]
    k_blk = k.reshape(B, H, nb, MOBA_BLOCK, Dh)
    v_blk = v.reshape(B, H, nb, MOBA_BLOCK, Dh)
    k_mean = jnp.mean(k_blk.astype(jnp.float32), axis=3)
    gate = jnp.einsum('bhsd,bhnd->bhsn', q.astype(jnp.float32), k_mean)
    q_blk_id = jnp.arange(s_pad) // MOBA_BLOCK
    past = jnp.arange(nb)[None, :] < q_blk_id[:, None]
    gate = jnp.where(past, gate, -jnp.inf)
    n_sel = min(MOBA_TOPK, nb)
    top_v, top_i = lax.top_k(gate, n_sel)
    sel_valid = jnp.isfinite(top_v)
    own = jnp.broadcast_to(q_blk_id[None, None, :, None], (B, H, s_pad, 1)).astype(top_i.dtype)
    blk_idx = jnp.concatenate([top_i, own], axis=-1)
    blk_valid = jnp.concatenate([sel_valid, jnp.ones(own.shape, dtype=bool)], axis=-1)

    nc = s_pad // Q_CHUNK

    def to_chunks(t):
        return jnp.moveaxis(t.reshape(B, H, nc, Q_CHUNK, *t.shape[3:]), 2, 0)

    bi = jnp.arange(B)[:, None, None, None]
    hi = jnp.arange(H)[None, :, None, None]
    offs = jnp.arange(MOBA_BLOCK)
    scale = Dh ** -0.5

    def chunk_attn(args):
        c_idx, qc, idx, valid = args
        kg = k_blk[bi, hi, idx]
        vg = v_blk[bi, hi, idx]
        s = jnp.einsum('bhqd,bhqnkd->bhqnk', qc, kg).astype(jnp.float32) * scale
        q_pos = c_idx * Q_CHUNK + jnp.arange(Q_CHUNK)
        k_pos = idx[..., None] * MOBA_BLOCK + offs
        mask = valid[..., None] & (k_pos <= q_pos[:, None, None])
        s = jnp.where(mask, s, -jnp.inf)
        p = jax.nn.softmax(s.reshape(B, H, Q_CHUNK, -1), axis=-1).reshape(s.shape)
        return jnp.einsum('bhqnk,bhqnkd->bhqd', p.astype(vg.dtype), vg)

    out = lax.map(chunk_attn, (jnp.arange(nc), to_chunks(q), to_chunks(blk_idx), to_chunks(blk_valid)))
    out = jnp.moveaxis(out, 0, 2).reshape(B, H, s_pad, Dh)[:, :, :S]
    return out.transpose(0, 2, 1, 3).reshape(B, S, H * Dh)


def setup_inputs(seed: int = 0) -> dict:
    key = jax.random.key(seed)
    ks = jax.random.split(key, 24)
    f32 = jnp.float32
    nrm = lambda k, shape, s: jax.random.normal(k, shape, f32) * s
    gain = lambda k, shape: 1.0 + 0.02 * jax.random.normal(k, shape, f32)
    u = jax.random.uniform(ks[14], (DEPTH, LRU_WIDTH), f32, 0.9, 0.999)
    a_base = u ** (1.0 / LRU_C)
    lru_lambda = jnp.log(a_base) - jnp.log1p(-a_base)
    return {
        "x": jax.random.normal(ks[0], (BATCH, SEQ, D_MODEL), f32),
        "c": jax.random.normal(ks[1], (BATCH, D_MODEL), f32),
        "w_ada": nrm(ks[2], (DEPTH, D_MODEL, N_MOD * D_MODEL), 0.5 * D_MODEL ** -0.5),
        "b_ada": nrm(ks[3], (DEPTH, N_MOD * D_MODEL), 0.02),
        "norm1_g": gain(ks[4], (DEPTH, D_MODEL)),
        "w_in": nrm(ks[5], (DEPTH, D_MODEL, N_IN), D_MODEL ** -0.5),
        "q_norm_g": gain(ks[6], (DEPTH, HEAD_DIM)),
        "k_norm_g": gain(ks[7], (DEPTH, HEAD_DIM)),
        "lru_conv_w": nrm(ks[8], (DEPTH, LRU_CONV, LRU_WIDTH), LRU_CONV ** -0.5),
        "lru_conv_b": nrm(ks[9], (DEPTH, LRU_WIDTH), 0.02),
        "lru_wa": nrm(ks[10], (DEPTH, LRU_HEADS, LRU_HEAD_DIM, LRU_HEAD_DIM), LRU_HEAD_DIM ** -0.5),
        "lru_ba": nrm(ks[11], (DEPTH, LRU_WIDTH), 0.02),
        "lru_wx": nrm(ks[12], (DEPTH, LRU_HEADS, LRU_HEAD_DIM, LRU_HEAD_DIM), LRU_HEAD_DIM ** -0.5),
        "lru_bx": nrm(ks[13], (DEPTH, LRU_WIDTH), 0.02),
        "lru_lambda": lru_lambda,
        "lru_out_g": gain(ks[15], (DEPTH, LRU_WIDTH)),
        "attn_out_g": gain(ks[16], (DEPTH, ATTN_WIDTH)),
        "w_out": nrm(ks[17], (DEPTH, D_MODEL, D_MODEL), D_MODEL ** -0.5),
        "norm2_g": gain(ks[18], (DEPTH, D_MODEL)),
        "w_up": nrm(ks[19], (DEPTH, D_MODEL, 2 * D_FF), D_MODEL ** -0.5),
        "ffn_conv_w": nrm(ks[20], (DEPTH, FFN_CONV, 2 * D_FF), FFN_CONV ** -0.5),
        "ffn_conv_b": nrm(ks[21], (DEPTH, 2 * D_FF), 0.02),
        "w_down": nrm(ks[22], (DEPTH, D_FF, D_MODEL), D_FF ** -0.5),
    }


def reference(x, c, w_ada, b_ada, norm1_g, w_in, q_norm_g, k_norm_g, lru_conv_w, lru_conv_b,
              lru_wa, lru_ba, lru_wx, lru_bx, lru_lambda, lru_out_g, attn_out_g, w_out,
              norm2_g, w_up, ffn_conv_w, ffn_conv_b, w_down):
    B, S, D = x.shape
    for l in range(DEPTH):
        mod = (c @ w_ada[l] + b_ada[l])[:, None, :]
        sh1, sc1, g1, sh2, sc2, g2 = jnp.split(mod, N_MOD, axis=-1)

        h = rmsnorm(x, norm1_g[l]) * (1.0 + sc1) + sh1
        z = h @ w_in[l]
        q, k, v, xr, gr = jnp.split(
            z, [ATTN_WIDTH, 2 * ATTN_WIDTH, 3 * ATTN_WIDTH, 3 * ATTN_WIDTH + LRU_WIDTH], axis=-1)
        q = rmsnorm(q.reshape(B, S, ATTN_HEADS, HEAD_DIM), q_norm_g[l])
        k = rmsnorm(k.reshape(B, S, ATTN_HEADS, HEAD_DIM), k_norm_g[l])
        v = v.reshape(B, S, ATTN_HEADS, HEAD_DIM)
        attn = moba_attention(q, k, v)
        xr = causal_dwconv(xr, lru_conv_w[l], lru_conv_b[l])
        lru = rg_lru(xr, lru_wa[l], lru_ba[l], lru_wx[l], lru_bx[l], lru_lambda[l]) * jax.nn.gelu(gr)
        mix = jnp.concatenate([rmsnorm(lru, lru_out_g[l]), rmsnorm(attn, attn_out_g[l])], axis=-1)
        x = x + g1 * (mix @ w_out[l])

        h2 = rmsnorm(x, norm2_g[l]) * (1.0 + sc2) + sh2
        up = causal_dwconv(h2 @ w_up[l], ffn_conv_w[l], ffn_conv_b[l])
        gate, val = jnp.split(up, 2, axis=-1)
        x = x + g2 * ((jax.nn.silu(gate) * val) @ w_down[l])
    return x
```

```python
import contextlib
import numpy as np
import ml_dtypes
import concourse.bass as bass
import concourse.mybir as mybir
from concourse.bass_utils import run_bass_kernel_spmd

F32 = mybir.dt.float32
BF16 = mybir.dt.bfloat16
AF = mybir.ActivationFunctionType
ALU = mybir.AluOpType
AX = mybir.AxisListType

PE, ACT, DVE, POOL, SP = "tensor", "scalar", "vector", "gpsimd", "sync"
COMPUTE = (PE, ACT, DVE, POOL)

D = 1024
S = 8192
NB = 32
OWN0 = 15
NOWN = NB - OWN0
NQ = NOWN * 256
DFF = 2816
NEG = -30000.0
EPS = 1e-6


class Buf:
    __slots__ = ("name", "w", "rs")

    def __init__(self, name=""):
        self.name = name
        self.w = None
        self.rs = []


class Op:
    __slots__ = ("eng", "fn", "deps", "dma", "sig", "sem", "need", "seg", "alld", "cost", "lat", "line")

    def __init__(self, eng, fn, dma, seg):
        self.eng = eng
        self.fn = fn
        self.dma = dma
        self.deps = set()
        self.sig = None
        self.sem = None
        self.need = False
        self.seg = seg


class Prog:
    def __init__(self, nc, stack, n_dma_sems=16):
        self.nc = nc
        self.n = n_dma_sems
        self.seg = 0
        self.ops = []
        self.esem = {e: stack.enter_context(nc.semaphore("s_" + e)) for e in COMPUTE}
        self.dsem = {q: [stack.enter_context(nc.semaphore("d_%s_%d" % (q, i))) for i in range(n_dma_sems)]
                     for q in (SP, POOL)}
        self.ecnt = {e: 0 for e in COMPUTE}
        self.dcnt = {q: [0] * n_dma_sems for q in self.dsem}
        self.dlast = {q: [None] * n_dma_sems for q in self.dsem}
        self.dnum = {q: 0 for q in self.dsem}
        self.finals = []
        self.rec = None
        self.sched = True
        self.est_ns = 0.0
        self.est_log = []

    def chain(self, fn):
        self.rec = []
        fn()
        lst, self.rec = self.rec, None
        return lst

    def merge(self, *chains):
        items = []
        for ci, ch in enumerate(chains):
            n = len(ch)
            for i, it in enumerate(ch):
                items.append(((i + 0.5) / n, ci, i, it))
        items.sort(key=lambda x: (x[0], x[1], x[2]))
        for _, _, _, it in items:
            self.op(*it)

    def op(self, eng, fn, reads=(), writes=(), dma=False):
        if self.rec is not None:
            self.rec.append((eng, fn, tuple(reads), tuple(writes), dma))
            return None
        o = Op(eng, fn, dma, self.seg)
        for b in reads:
            if b.w is not None:
                o.deps.add(b.w)
        for b in writes:
            if b.w is not None:
                o.deps.add(b.w)
            for r in b.rs:
                o.deps.add(r)
        for b in reads:
            b.rs.append(o)
        for b in writes:
            b.w = o
            b.rs = []
        o.deps.discard(o)
        import sys as _sys
        fr = _sys._getframe(1)
        if fr.f_code.co_name in ('dma', 'merge'):
            fr = fr.f_back
        o.line = fr.f_lineno
        n = getattr(fn, "n", 256)
        k = getattr(fn, "k", 0)
        if dma:
            o.cost = 60.0
            o.lat = 2200.0 + n / 60.0
        elif eng == PE:
            o.cost = 45.0 + 0.37 * n
            o.lat = o.cost + 200.0
        elif eng == ACT:
            o.cost = 130.0 + 0.83 * n + 120.0 * k
            o.lat = o.cost + 120.0
        elif eng == POOL:
            o.cost = 250.0 + 2.0 * n
            o.lat = o.cost + 150.0
        else:
            o.cost = 150.0 + (1.05 if k else 0.6) * n
            o.lat = o.cost + 120.0
        self.ops.append(o)
        return o

    def dma(self, eng, out, in_, reads=(), writes=()):
        nbytes = 1
        for d in out.shape:
            nbytes *= int(d)
        nbytes *= 2 if out.dtype == BF16 else 4
        f = lambda e: e.dma_start(out=out, in_=in_)
        f.n = nbytes
        return self.op(eng, f, reads, writes, dma=True)

    def schedule(self, ops):
        import bisect
        n = len(ops)
        pos = {id(o): i for i, o in enumerate(ops)}
        succ = [[] for _ in range(n)]
        indeg = [0] * n
        for i, o in enumerate(ops):
            for d in o.alld:
                j = pos.get(id(d))
                if j is None:
                    continue
                succ[j].append(i)
                indeg[i] += 1
        rt = [0.0] * n
        why = [None] * n
        stt = [0.0] * n
        lastop = {}
        avail = {e: [] for e in (PE, ACT, DVE, POOL, SP)}
        free = {e: 0.0 for e in avail}
        for i, o in enumerate(ops):
            if indeg[i] == 0:
                avail[o.eng].append(i)
        order = []
        WIN_ = 40
        done = 0
        while done < n:
            best = None
            for e, lst in avail.items():
                if not lst:
                    continue
                fe = free[e]
                cb = None
                for i in lst[:WIN_]:
                    st = rt[i] if rt[i] > fe else fe
                    if cb is None or st < cb[0]:
                        cb = (st, i)
                if best is None or cb < best[0:2]:
                    best = (cb[0], cb[1], e)
            st, i, e = best
            lst = avail[e]
            lst.pop(bisect.bisect_left(lst, i))
            o = ops[i]
            stt[i] = st
            if rt[i] < st and e in lastop:
                why[i] = ('eng', lastop[e])
            lastop[e] = i
            free[e] = st + o.cost
            fin = st + o.lat
            order.append(o)
            done += 1
            for k in succ[i]:
                if fin > rt[k]:
                    rt[k] = fin
                    if why[k] is None or why[k][0] == 'dep':
                        why[k] = ('dep', i)
                indeg[k] -= 1
                if indeg[k] == 0:
                    bisect.insort(avail[ops[k].eng], k)
        self.est_ns = max(free.values())
        busy = {}
        for o in ops:
            busy[o.eng] = busy.get(o.eng, 0.0) + o.cost
        self.est_busy = {k: round(v / 1e3) for k, v in busy.items()}
        return order

    def emit_segment(self, last=False):
        nc = self.nc
        seg = self.seg
        for o in self.ops:
            o.alld = [d for d in o.deps if d.seg == seg]
        ops = self.schedule(self.ops) if self.sched else self.ops
        self.est_log.append((seg, len(ops), round(self.est_ns / 1e3), self.est_busy))
        queues = {e: [] for e in (PE, ACT, DVE, POOL, SP)}
        for o in ops:
            queues[o.eng].append(o)
            keep = set()
            for d in o.deps:
                if d.seg != seg:
                    continue
                if d.dma:
                    keep.add(d)
                elif d.eng == o.eng and not o.dma:
                    if o.eng != PE:
                        keep.add(d)
                else:
                    keep.add(d)
            o.deps = keep
            for d in keep:
                d.need = True
        for e in COMPUTE:
            for o in reversed(queues[e]):
                if not o.dma:
                    o.need = True
                    break
        for o in ops:
            if o.dma:
                q = o.eng
                i = self.dnum[q] % self.n
                self.dnum[q] += 1
                prev = self.dlast[q][i]
                if prev is not None and prev.seg == seg:
                    o.deps.add(prev)
                self.dcnt[q][i] += 16
                o.sem = self.dsem[q][i]
                o.sig = self.dcnt[q][i]
                self.dlast[q][i] = o
            elif o.need:
                self.ecnt[o.eng] += 1
                o.sem = self.esem[o.eng]
                o.sig = self.ecnt[o.eng]
        with nc.Block() as block:
            def run(engname, eng):
                waited = {}
                for o in queues[engname]:
                    need = {}
                    for d in o.deps:
                        k = id(d.sem)
                        if waited.get(k, 0) >= d.sig:
                            continue
                        if k not in need or need[k][1] < d.sig:
                            need[k] = (d.sem, d.sig)
                    for k, (s, v) in need.items():
                        eng.wait_ge(s, v)
                        waited[k] = v
                    ins = o.fn(eng)
                    if o.sem is not None:
                        ins.then_inc(o.sem, 16 if o.dma else 1)
                for e in COMPUTE:
                    if e != engname and self.ecnt[e] > 0:
                        eng.wait_ge(self.esem[e], self.ecnt[e])
                for q in self.dsem:
                    for i in range(self.n):
                        if self.dcnt[q][i] > 0:
                            eng.wait_ge(self.dsem[q][i], self.dcnt[q][i])

            @block.sync
            def _(e):
                run(SP, e)

            @block.tensor
            def _(e):
                run(PE, e)

            @block.scalar
            def _(e):
                run(ACT, e)

            @block.vector
            def _(e):
                run(DVE, e)

            @block.gpsimd
            def _(e):
                run(POOL, e)
        self.ops = []
        self.seg += 1


def _fs(ap):
    n = 1
    for d in ap.shape[1:]:
        n *= int(d)
    return n


def _w(f, n):
    f.n = n
    return f


def A_(out, in_, func, **kw):
    f = _w(lambda e: e.activation(out=out, in_=in_, func=func, **kw), _fs(out))
    f.k = sum(1 for key in ("scale", "bias", "accum_out") if key in kw and not isinstance(kw[key], (int, float)))
    return f


def TS_(out, in0, s1, s2, op0, op1=None):
    if op1 is None:
        return _w(lambda e: e.tensor_scalar(out=out, in0=in0, scalar1=s1, scalar2=None, op0=op0), _fs(out))
    return _w(lambda e: e.tensor_scalar(out=out, in0=in0, scalar1=s1, scalar2=s2, op0=op0, op1=op1), _fs(out))


def TT_(out, in0, in1, op):
    f = _w(lambda e: e.tensor_tensor(out=out, in0=in0, in1=in1, op=op), _fs(out))
    f.k = 1
    return f


def STT_(out, in0, scalar, in1, op0, op1):
    f = _w(lambda e: e.scalar_tensor_tensor(out=out, in0=in0, scalar=scalar, in1=in1, op0=op0, op1=op1), _fs(out))
    f.k = 1
    return f


def CP_(out, in_):
    return _w(lambda e: e.tensor_copy(out, in_), _fs(out))


def MM_(out, lhsT, rhs, start, stop):
    return _w(lambda e: e.matmul(out, lhsT=lhsT, rhs=rhs, start=start, stop=stop), _fs(rhs))


def TR_(out, in_, ident):
    return _w(lambda e: e.transpose(out, in_, ident), 128)


def build(debug=False, stop_after=3):
    nc = bass.Bass("TRN2", target_bir_lowering=False)

    def din(name, shape, dt=F32):
        return nc.dram_tensor(name, list(shape), dt, kind="ExternalInput").ap()

    skind = "ExternalOutput" if debug else "Internal"

    def dscr(name, shape, dt):
        return nc.dram_tensor(name, list(shape), dt, kind=skind).ap()

    xs = din("xs", [S, D])
    cT = din("cT", [128, 8])
    w_ada = din("w_ada", [D, 6 * D])
    b_ada = din("b_ada", [1, 6 * D])
    n1g = din("n1g", [128, 8])
    n2g = din("n2g", [128, 8])
    mixg = din("mixg", [128, 8])
    w_in = din("w_in", [D, 2560])
    gqb = din("gqb", [128, 512])
    gkb = din("gkb", [128, 512])
    cw = din("cw", [128, 4, 4])
    cb = din("cb", [128, 4])
    wabd = din("wabd", [128, 4, 128])
    wxbd = din("wxbd", [128, 4, 128])
    lba = din("lba", [128, 4])
    lbx = din("lbx", [128, 4])
    lam = din("lam", [128, 4])
    w_out = din("w_out", [D, D])
    w_up = din("w_up", [D, 2 * DFF])
    fcw = din("fcw", [128, 44, 3])
    fcb = din("fcb", [128, 44])
    w_down = din("w_down", [DFF, D])
    flag = din("flag", [128, 1])
    pastb = din("pastb", [128, NOWN, 32])
    oh = din("oh", [32, S], BF16)
    identb = din("identb", [128, 128], BF16)
    identf = din("identf", [128, 128])
    onesb = din("onesb", [128, 128], BF16)
    maskd = din("maskd", [128, 2, 256], BF16)
    out = nc.dram_tensor("out", [4096, D], F32, kind="ExternalOutput").ap()

    KT_s = dscr("KT_s", [512, S], BF16)
    QT_s = dscr("QT_s", [512, NQ], BF16)
    NS_s = dscr("NS_s", [256, NQ], BF16)
    LR_s = dscr("LR_s", [512, NQ], BF16)
    V_s = dscr("V_s", [8, 64, 128, 64], BF16)
    AT_s = dscr("AT_s", [NQ, 512], F32)
    X1_s = dscr("X1_s", [NQ, D], F32) if debug else None

    with contextlib.ExitStack() as top:
        P = Prog(nc, top)

        def mk(stack):
            def sb(name, shape, dt=F32):
                return stack.enter_context(nc.sbuf_tensor(name, list(shape), dt))

            def ps(name, shape, dt=F32):
                return stack.enter_context(nc.psum_tensor(name, list(shape), dt))
            return sb, ps

        sbT, _ = mk(top)
        WOUT = sbT("WOUT", [128, 8, D], BF16); bWOUT = Buf()
        G2B = sbT("G2B", [128, D], BF16); bG2B = Buf()
        G1 = sbT("G1", [128, 8]); SH1 = sbT("SH1", [128, 8]); G2 = sbT("G2", [128, 8]); SH2 = sbT("SH2", [128, 8])
        bMOD = Buf()
        IDB = sbT("IDB", [128, 128], BF16); ONB = sbT("ONB", [128, 128], BF16); MKD = sbT("MKD", [128, 2, 256], BF16)
        FLG = sbT("FLG", [128, 1]); bCONST = Buf()
        CW = sbT("CW", [128, 4, 4]); CB = sbT("CB", [128, 4]); CL = sbT("CL", [128, 4]); CL2 = sbT("CL2", [128, 4])
        NBA = sbT("NBA", [128, 4]); NBX = sbT("NBX", [128, 4])
        WAB = sbT("WAB", [128, 4, 128], BF16); WXB = sbT("WXB", [128, 4, 128], BF16)
        FCW = sbT("FCW", [128, 44, 3]); FCB = sbT("FCB", [128, 44])

        sWI = contextlib.ExitStack()
        sbWI, _ = mk(sWI)
        WIN = sbWI("WIN", [128, 8, 2560], BF16); bWIN = Buf()
        with contextlib.ExitStack() as s0:
            sb, ps = mk(s0)
            WIS = [sb("WIS%d" % i, [128, 8, 320]) for i in range(2)]; bWIS = [Buf(), Buf()]
            CT = sb("CT", [128, 8]); CREP = sb("CREP", [128, 8, 128])
            WAS = [sb("WAS%d" % i, [128, 8, 512]) for i in range(3)]; bWAS = [Buf() for _ in range(3)]
            BB = [sb("BB%d" % i, [128, 512]) for i in range(3)]; bBB = [Buf() for _ in range(3)]
            MB = sb("MB", [128, 6 * D]); bMB = Buf()
            TMP = sb("TMP", [128, 48, 128]); bTMP = Buf()
            MT = sb("MT", [128, 48]); bMT = Buf()
            IDF = sb("IDF", [128, 128])
            N1G = sb("N1G", [128, 8]); N2G = sb("N2G", [128, 8]); MIXG = sb("MIXG", [128, 8])
            LBA = sb("LBA", [128, 4]); LBX = sb("LBX", [128, 4]); LAM = sb("LAM", [128, 4])
            WAF = sb("WAF", [128, 4, 128]); WXF = sb("WXF", [128, 4, 128])
            PS0 = [ps("PS0_%d" % i, [128, 512]) for i in range(2)]; bPS0 = [Buf(), Buf()]
            bC0 = Buf()
            for dst, src in ((CT, cT), (IDF, identf), (N1G, n1g), (N2G, n2g), (MIXG, mixg), (LBA, lba), (LBX, lbx),
                             (LAM, lam), (WAF, wabd), (WXF, wxbd), (IDB, identb), (ONB, onesb), (MKD, maskd),
                             (FLG, flag), (CW, cw), (CB, cb), (FCW, fcw), (FCB, fcb)):
                P.dma(SP, dst[:], src, writes=[bC0])
            P.op(DVE, CP_(CREP[:], CT[:, :].unsqueeze(2).to_broadcast([128, 8, 128])), reads=[bC0], writes=[bCONST])
            for g in range(12):
                i = g % 3
                P.dma(SP, WAS[i][:], w_ada[:, g * 512:(g + 1) * 512].rearrange("(kc p) n -> p kc n", p=128), writes=[bWAS[i]])
                P.dma(SP, BB[i][:], b_ada[0:1, g * 512:(g + 1) * 512].partition_broadcast(128)[:, 0, :], writes=[bBB[i]])
                for kc in range(8):
                    P.op(PE, MM_(PS0[g % 2][:], CREP[:, kc, :], WAS[i][:, kc, :], kc == 0, kc == 7),
                         reads=[bCONST, bWAS[i]], writes=[bPS0[g % 2]])
                P.op(DVE, TT_(MB[:, g * 512:(g + 1) * 512], PS0[g % 2][:], BB[i][:], ALU.add),
                     reads=[bPS0[g % 2], bBB[i]], writes=[bMB])
            for g in range(8):
                i = g % 2
                P.dma(SP, WIS[i][:], w_in[:, g * 320:(g + 1) * 320].rearrange("(kc p) n -> p kc n", p=128), writes=[bWIS[i]])
                P.op(DVE, CP_(WIN[:, :, g * 320:(g + 1) * 320], WIS[i][:]), reads=[bWIS[i]], writes=[bWIN])
            P.op(DVE, TT_(TMP[:], MB[:].rearrange("p (c j) -> p c j", j=128),
                          IDF[:, :].unsqueeze(1).to_broadcast([128, 48, 128]), ALU.mult),
                 reads=[bMB, bC0], writes=[bTMP])
            P.op(DVE, lambda e: e.tensor_reduce(out=MT[:], in_=TMP[:], axis=AX.X, op=ALU.add), reads=[bTMP], writes=[bMT])
            P.op(DVE, CP_(SH1[:], MT[:, 0:8]), reads=[bMT], writes=[bMOD])
            P.op(DVE, STT_(G1[:], MT[:, 8:16], 1.0, N1G[:], ALU.add, ALU.mult), reads=[bMT, bC0], writes=[bMOD])
            P.op(DVE, CP_(SH2[:], MT[:, 24:32]), reads=[bMT], writes=[bMOD])
            P.op(DVE, STT_(G2[:], MT[:, 32:40], 1.0, N2G[:], ALU.add, ALU.mult), reads=[bMT, bC0], writes=[bMOD])
            P.op(DVE, CP_(G2B[:], MB[:, 5 * D:6 * D]), reads=[bMB], writes=[bG2B])
            for nh in range(2):
                P.dma(SP, WAS[nh][:], w_out[:, nh * 512:(nh + 1) * 512].rearrange("(kc p) n -> p kc n", p=128), writes=[bWAS[nh]])
                for kc in range(8):
                    P.op(DVE, STT_(WOUT[:, kc, nh * 512:(nh + 1) * 512], WAS[nh][:, kc, :], MIXG[:, kc:kc + 1],
                                   MB[:, 2 * D + nh * 512:2 * D + (nh + 1) * 512], ALU.mult, ALU.mult),
                         reads=[bWAS[nh], bMB, bC0], writes=[bWOUT])
            P.op(ACT, A_(LAM[:], LAM[:], AF.Exp, scale=-1.0), reads=[bC0], writes=[bC0])
            P.op(ACT, A_(LAM[:], LAM[:], AF.Ln, bias=1.0), reads=[bC0], writes=[bC0])
            P.op(DVE, TS_(CL[:], LAM[:], -8.0, None, ALU.mult), reads=[bC0], writes=[bCONST])
            P.op(DVE, TS_(CL2[:], LAM[:], -16.0, None, ALU.mult), reads=[bC0], writes=[bCONST])
            P.op(DVE, TS_(NBA[:], LBA[:], -1.0, None, ALU.mult), reads=[bC0], writes=[bCONST])
            P.op(DVE, TS_(NBX[:], LBX[:], -1.0, None, ALU.mult), reads=[bC0], writes=[bCONST])
            P.op(DVE, CP_(WAB[:], WAF[:]), reads=[bC0], writes=[bCONST])
            P.op(DVE, CP_(WXB[:], WXF[:]), reads=[bC0], writes=[bCONST])
            P.emit_segment()

        with contextlib.ExitStack() as s1:
            if stop_after < 1:
                return nc
            sb, ps = mk(s1)
            PBT = sb("PBT", [128, NOWN, 32]); bPBT = Buf()
            GQB = sb("GQB", [128, 512]); GKB = sb("GKB", [128, 512])
            X = [sb("X%d" % i, [128, 2, D]) for i in range(2)]; bX = [Buf(), Buf()]
            J = sb("J", [128, D], BF16); bJ = Buf()
            ST = sb("ST", [128, 8]); bST = Buf()
            HN = sb("HN", [128, 2, D], BF16); bHN = Buf()
            HTd = [sb("HT%d" % i, [128, 8, 256], BF16) for i in range(2)]; bHTd = [Buf(), Buf()]
            SQ = sb("SQ", [128, 512]); bSQ = Buf()
            SK = sb("SK", [128, 24]); bSK = Buf()
            TK = sb("TK", [128, 512]); bTK = Buf()
            KN = sb("KN", [128, 512], BF16); bKN = Buf()
            KTt = sb("KTt", [128, 4, 256], BF16); bKTt = Buf()
            QTt = sb("QTt", [128, 4, 256], BF16); bQTt = Buf()
            KM = sb("KM", [128, 4, 2, 32], BF16); bKM = Buf()
            KMS = sb("KMS", [128, 4]); bKMS = Buf()
            VT = [sb("VT%d" % i, [128, 512], BF16) for i in range(2)]; bVT = [Buf(), Buf()]
            GB = sb("GB", [128, 8, 32]); bGB = Buf()
            M8 = sb("M8", [128, 8, 8]); bM8 = Buf()
            THR = sb("THR", [128, 8]); bTHR = Buf()
            NS = sb("NS", [128, 256], BF16); bNS = Buf()
            NSTt = sb("NSTt", [128, 2, 256], BF16); bNSTt = Buf()
            XR = sb("XR", [128, 4, 259]); bXR = Buf()
            XC = sb("XC", [128, 4, 256]); bXC = Buf()
            XCb = sb("XCb", [128, 4, 256], BF16); bXCb = Buf()
            R = sb("R", [128, 4, 256]); bR = Buf()
            I = sb("I", [128, 4, 256]); bI = Buf()
            AA = sb("AA", [128, 4, 256]); bAA = Buf()
            OM = sb("OM", [128, 4, 256]); bOM = Buf()
            H = sb("H", [128, 4, 256]); bH = Buf()
            HS = sb("HS", [128, 4]); bHS = Buf()
            T1 = sb("T1", [128, 4, 256]); bT1 = Buf()
            YSQ = sb("YSQ", [128, 4, 256], BF16); bYSQ = Buf()
            RS = sb("RS", [128, 256]); bRS = Buf()
            LRN = sb("LRN", [128, 4, 256], BF16); bLRN = Buf()
            PTd = [ps("PT1_%d" % i, [128, 4, 256], BF16) for i in range(2)]; bPTd = [Buf(), Buf()]
            PF4 = ps("PF4", [128, 4, 256])
            PF = [PF4[:, 0:2, :], PF4[:, 2:4, :]]; bPF = [Buf(), Buf()]
            PK = [ps("PK%d" % i, [128, 512]) for i in range(2)]; bPK = [Buf(), Buf()]
            PM0 = ps("PM0", [128, 4, 256], BF16); bPM0 = Buf()
            PM1 = ps("PM1", [128, 512]); bPM1 = Buf()

            P.dma(SP, PBT[:], pastb, writes=[bPBT])
            P.dma(SP, GQB[:], gqb, writes=[bPBT])
            P.dma(SP, GKB[:], gkb, writes=[bPBT])
            P.op(DVE, lambda e: e.memset(XR[:, :, 0:3], 0.0), writes=[bXR])
            P.op(DVE, lambda e: e.memset(HS[:], 0.0), writes=[bHS])
            P.op(DVE, lambda e: e.memset(KM[:], 0.0), writes=[bKM])

            def load_x(bi):
                P.dma(SP, X[bi % 2][:], xs[bi * 256:(bi + 1) * 256, :].rearrange("(t p) d -> p t d", p=128),
                      writes=[bX[bi % 2]])

            def norm_T(Xt, bXt, Gm, SHm, HT, bHT):
                for t in range(2):
                    P.op(ACT, A_(J[:], Xt[:, t, :], AF.Square, accum_out=ST[:, t:t + 1]), reads=[bXt], writes=[bJ, bST])
                P.op(ACT, A_(ST[:, 2:4], ST[:, 0:2], AF.Ln, scale=1.0 / D, bias=EPS), reads=[bST], writes=[bST])
                P.op(ACT, A_(ST[:, 4:6], ST[:, 2:4], AF.Exp, scale=-0.5), reads=[bST], writes=[bST])
                for t in range(2):
                    P.op(DVE, TS_(HN[:, t, :], Xt[:, t, :], ST[:, 4 + t:5 + t], None, ALU.mult), reads=[bXt, bST], writes=[bHN])
                for rnd in range(2):
                    PTr, bPTr = PTd[rnd], bPTd[rnd]
                    for k4 in range(4):
                        kc = rnd * 4 + k4
                        for t in range(2):
                            P.op(PE, TR_(PTr[:, k4, t * 128:(t + 1) * 128], HN[:, t, kc * 128:(kc + 1) * 128], IDB[:]),
                                 reads=[bHN], writes=[bPTr])
                    for k4 in range(4):
                        kc = rnd * 4 + k4
                        P.op(DVE, TS_(HT[:, kc, :], PTr[:, k4, :], Gm[:, kc:kc + 1], SHm[:, kc:kc + 1], ALU.mult, ALU.add),
                             reads=[bPTr, bMOD], writes=[bHT])

            def headnorm(src_ps, bsrc, Gb, dst, bdst):
                P.op(ACT, A_(SQ[:], src_ps, AF.Square), reads=[bsrc], writes=[bSQ])
                P.op(DVE, lambda e: e.tensor_reduce(out=SK[:, 0:8], in_=SQ[:].rearrange("p (h d) -> p h d", h=8), axis=AX.X, op=ALU.add),
                     reads=[bSQ], writes=[bSK])
                P.op(ACT, A_(SK[:, 8:16], SK[:, 0:8], AF.Ln, scale=1.0 / 64, bias=EPS), reads=[bSK], writes=[bSK])
                P.op(ACT, A_(SK[:, 16:24], SK[:, 8:16], AF.Exp, scale=-0.5), reads=[bSK], writes=[bSK])
                P.op(DVE, TT_(TK[:].rearrange("p (h d) -> p h d", h=8), src_ps.rearrange("p (h d) -> p h d", h=8),
                              SK[:, 16:24].unsqueeze(2).to_broadcast([128, 8, 64]), ALU.mult), reads=[bsrc, bSK], writes=[bTK])
                P.op(DVE, TT_(dst, TK[:], Gb[:], ALU.mult), reads=[bTK, bPBT], writes=[bdst])

            pkc = [0]

            def chain1(bi):
                own = bi >= OWN0
                qi = bi - OWN0
                Xt, bXt = X[bi % 2], bX[bi % 2]
                HT, bHT = HTd[bi % 2], bHTd[bi % 2]
                norm_T(Xt, bXt, G1, SH1, HT, bHT)
                for t in range(2):
                    tile_idx = bi * 2 + t
                    bk = pkc[0] % 2; pkc[0] += 1
                    for kc in range(8):
                        P.op(PE, MM_(PK[bk][:], HT[:, kc, t * 128:(t + 1) * 128], WIN[:, kc, 512:1024], kc == 0, kc == 7),
                             reads=[bWIN, bHT], writes=[bPK[bk]])
                    headnorm(PK[bk][:], bPK[bk], GKB, KN[:], bKN)
                    for pr in range(4):
                        P.op(PE, TR_(PM0[:, pr, t * 128:(t + 1) * 128], KN[:, pr * 128:(pr + 1) * 128], IDB[:]), reads=[bKN], writes=[bPM0])
                    P.op(DVE, CP_(KTt[:, :, t * 128:(t + 1) * 128], PM0[:, :, t * 128:(t + 1) * 128]), reads=[bPM0], writes=[bKTt])
                    bv = pkc[0] % 2; pkc[0] += 1
                    for kc in range(8):
                        P.op(PE, MM_(PK[bv][:], HT[:, kc, t * 128:(t + 1) * 128], WIN[:, kc, 1024:1536], kc == 0, kc == 7),
                             reads=[bWIN, bHT], writes=[bPK[bv]])
                    P.op(ACT, A_(VT[t][:], PK[bv][:], AF.Copy), reads=[bPK[bv]], writes=[bVT[t]])
                    P.dma(POOL, V_s[:, tile_idx, :, :].rearrange("h p d -> p h d"), VT[t][:].rearrange("p (h d) -> p h d", h=8),
                          reads=[bVT[t]])
                P.dma(POOL, KT_s[:, bi * 256:(bi + 1) * 256].rearrange("(pr p) n -> p pr n", p=128), KTt[:], reads=[bKTt])
                P.op(DVE, lambda e: e.tensor_reduce(out=KMS[:], in_=KTt[:], axis=AX.X, op=ALU.add), reads=[bKTt], writes=[bKMS])
                if own:
                    for t in range(2):
                        bq = pkc[0] % 2; pkc[0] += 1
                        for kc in range(8):
                            P.op(PE, MM_(PK[bq][:], HT[:, kc, t * 128:(t + 1) * 128], WIN[:, kc, 0:512], kc == 0, kc == 7),
                                 reads=[bWIN, bHT], writes=[bPK[bq]])
                        headnorm(PK[bq][:], bPK[bq], GQB, KN[:], bKN)
                        for pr in range(4):
                            P.op(PE, TR_(PM0[:, pr, t * 128:(t + 1) * 128], KN[:, pr * 128:(pr + 1) * 128], IDB[:]), reads=[bKN], writes=[bPM0])
                        P.op(DVE, CP_(QTt[:, :, t * 128:(t + 1) * 128], PM0[:, :, t * 128:(t + 1) * 128]), reads=[bPM0], writes=[bQTt])
                        for pr in range(4):
                            P.op(PE, MM_(PM1[:, pr * 64:(pr + 1) * 64], QTt[:, pr, t * 128:(t + 1) * 128],
                                         KM[:, pr, :, :].rearrange("p s j -> p (s j)"), True, True), reads=[bQTt, bKM], writes=[bPM1])
                        P.op(DVE, TT_(GB[:], PM1[:, 0:256].rearrange("p (h j) -> p h j", h=8),
                                      PBT[:, qi, :].unsqueeze(1).to_broadcast([128, 8, 32]), ALU.add), reads=[bPM1, bPBT], writes=[bGB])
                        for h in range(8):
                            P.op(DVE, lambda e, h=h: e.max(out=M8[:, h, :], in_=GB[:, h, :]), reads=[bGB], writes=[bM8])
                        P.op(DVE, TS_(THR[:], M8[:, :, 2], -1e29, None, ALU.max), reads=[bM8], writes=[bTHR])
                        P.op(DVE, TT_(NS[:].rearrange("p (h j) -> p h j", h=8), GB[:],
                                      THR[:, :].unsqueeze(2).to_broadcast([128, 8, 32]), ALU.is_lt), reads=[bGB, bTHR], writes=[bNS])
                        for g in range(2):
                            P.op(PE, TR_(PTd[1][:, g, t * 128:(t + 1) * 128], NS[:, g * 128:(g + 1) * 128], IDB[:]), reads=[bNS], writes=[bPTd[1]])
                        P.op(DVE, CP_(NSTt[:, :, t * 128:(t + 1) * 128], PTd[1][:, 0:2, t * 128:(t + 1) * 128]), reads=[bPTd[1]], writes=[bNSTt])
                    P.dma(POOL, QT_s[:, qi * 256:(qi + 1) * 256].rearrange("(pr p) n -> p pr n", p=128), QTt[:], reads=[bQTt])
                    P.dma(POOL, NS_s[:, qi * 256:(qi + 1) * 256].rearrange("(g p) n -> p g n", p=128), NSTt[:], reads=[bNSTt])
                P.op(DVE, TS_(KM[0:64, :, 0, bi], KMS[0:64, :], 1.0 / 256, None, ALU.mult), reads=[bKMS], writes=[bKM])
                P.op(DVE, TS_(KM[64:128, :, 1, bi], KMS[64:128, :], 1.0 / 256, None, ALU.mult), reads=[bKMS], writes=[bKM])

            def chain2(bi):
                own = bi >= OWN0
                qi = bi - OWN0
                HT, bHT = HTd[bi % 2], bHTd[bi % 2]
                for c in range(4):
                    for kc in range(8):
                        P.op(PE, MM_(PF[c // 2][:, c % 2, :], WIN[:, kc, 1536 + c * 128:1536 + (c + 1) * 128], HT[:, kc, :], kc == 0, kc == 7),
                             reads=[bWIN, bHT], writes=[bPF[c // 2]])
                if bi == OWN0 + 1:
                    P.op(DVE, TS_(XR[:, :, 0:3], XR[:, :, 0:3], FLG[:, 0:1], None, ALU.mult), reads=[bXR, bCONST, bC0], writes=[bXR])
                    P.op(DVE, TS_(HS[:], HS[:], FLG[:, 0:1], None, ALU.mult), reads=[bHS, bC0], writes=[bHS])
                P.op(ACT, A_(XR[:, :, 3:259], PF4[:], AF.Copy), reads=[bPF[0], bPF[1]], writes=[bXR])
                for c in range(4):
                    P.op(ACT, A_(XC[:, c, :], XR[:, c, 0:256], AF.Identity, scale=CW[:, c, 0:1], bias=CB[:, c:c + 1]),
                         reads=[bXR, bC0], writes=[bXC])
                    for j in range(1, 4):
                        P.op(DVE, STT_(XC[:, c, :], XR[:, c, j:j + 256], CW[:, c, j:j + 1], XC[:, c, :], ALU.mult, ALU.add),
                             reads=[bXR, bXC, bC0], writes=[bXC])
                P.op(DVE, CP_(XR[:, :, 0:3], XR[:, :, 256:259]), reads=[bXR], writes=[bXR])
                P.op(DVE, CP_(XCb[:], XC[:]), reads=[bXC], writes=[bXCb])
                for (Wb, NBb, dst, bdst) in ((WAB, NBA, R, bR), (WXB, NBX, I, bI)):
                    for c in range(4):
                        P.op(PE, MM_(PF[c // 2][:, c % 2, :], Wb[:, c, :], XCb[:, c, :], True, True),
                             reads=[bCONST, bXCb], writes=[bPF[c // 2]])
                    for c in range(4):
                        P.op(ACT, A_(dst[:, c, :], PF[c // 2][:, c % 2, :], AF.Exp, scale=-1.0, bias=NBb[:, c:c + 1]),
                             reads=[bPF[c // 2], bCONST], writes=[bdst])
                    P.op(ACT, A_(dst[:], dst[:], AF.Ln, bias=1.0), reads=[bdst], writes=[bdst])
                    P.op(ACT, A_(dst[:], dst[:], AF.Exp, scale=-1.0), reads=[bdst], writes=[bdst])
                for c in range(4):
                    P.op(ACT, A_(AA[:, c, :], R[:, c, :], AF.Exp, scale=CL[:, c:c + 1]), reads=[bR, bCONST], writes=[bAA])
                    P.op(ACT, A_(OM[:, c, :], R[:, c, :], AF.Exp, scale=CL2[:, c:c + 1]), reads=[bR, bCONST], writes=[bOM])
                P.op(DVE, TS_(OM[:], OM[:], -1.0, 1.0, ALU.mult, ALU.add), reads=[bOM], writes=[bOM])
                P.op(ACT, A_(OM[:], OM[:], AF.Ln), reads=[bOM], writes=[bOM])
                P.op(ACT, A_(OM[:], OM[:], AF.Exp, scale=0.5), reads=[bOM], writes=[bOM])
                P.op(DVE, TT_(OM[:], OM[:], I[:], ALU.mult), reads=[bOM, bI], writes=[bOM])
                P.op(DVE, TT_(OM[:], OM[:], XC[:], ALU.mult), reads=[bOM, bXC], writes=[bOM])
                for c in range(4):
                    P.op(DVE, lambda e, c=c: e.tensor_tensor_scan(out=H[:, c, :], data0=AA[:, c, :], data1=OM[:, c, :],
                                                                   initial=HS[:, c:c + 1], op0=ALU.mult, op1=ALU.add),
                         reads=[bAA, bOM, bHS], writes=[bH])
                P.op(DVE, CP_(HS[:], H[:, :, 255]), reads=[bH], writes=[bHS])
                if own:
                    for c in range(4):
                        for kc in range(8):
                            P.op(PE, MM_(PF[c // 2][:, c % 2, :], WIN[:, kc, 2048 + c * 128:2048 + (c + 1) * 128], HT[:, kc, :], kc == 0, kc == 7),
                                 reads=[bWIN, bHT], writes=[bPF[c // 2]])
                    for c2 in range(2):
                        cs = slice(2 * c2, 2 * c2 + 2)
                        P.op(ACT, A_(T1[:, cs, :], PF[c2][:], AF.Square), reads=[bPF[c2]], writes=[bT1])
                        P.op(DVE, TS_(T1[:, cs, :], T1[:, cs, :], 0.044715, 1.0, ALU.mult, ALU.add), reads=[bT1], writes=[bT1])
                        P.op(DVE, TT_(T1[:, cs, :], T1[:, cs, :], PF[c2][:], ALU.mult), reads=[bT1, bPF[c2]], writes=[bT1])
                        P.op(ACT, A_(T1[:, cs, :], T1[:, cs, :], AF.Exp, scale=-1.5957691216), reads=[bT1], writes=[bT1])
                        P.op(ACT, A_(T1[:, cs, :], T1[:, cs, :], AF.Ln, bias=1.0), reads=[bT1], writes=[bT1])
                        P.op(ACT, A_(T1[:, cs, :], T1[:, cs, :], AF.Exp, scale=-1.0), reads=[bT1], writes=[bT1])
                        P.op(DVE, TT_(T1[:, cs, :], T1[:, cs, :], PF[c2][:], ALU.mult), reads=[bT1, bPF[c2]], writes=[bT1])
                    P.op(DVE, TT_(T1[:], T1[:], H[:], ALU.mult), reads=[bT1, bH], writes=[bT1])
                    P.op(ACT, A_(YSQ[:], T1[:], AF.Square), reads=[bT1], writes=[bYSQ])
                    for c in range(4):
                        P.op(PE, MM_(PF4[:, 0, :], ONB[:], YSQ[:, c, :], c == 0, c == 3), reads=[bYSQ, bC0], writes=[bPF[0]])
                    P.op(ACT, A_(RS[:], PF4[:, 0, :], AF.Ln, scale=1.0 / 512, bias=EPS), reads=[bPF[0]], writes=[bRS])
                    P.op(ACT, A_(RS[:], RS[:], AF.Exp, scale=-0.5), reads=[bRS], writes=[bRS])
                    P.op(DVE, TT_(LRN[:], T1[:], RS[:, :].unsqueeze(1).to_broadcast([128, 4, 256]), ALU.mult), reads=[bT1, bRS], writes=[bLRN])
                    P.dma(POOL, LR_s[:, qi * 256:(qi + 1) * 256].rearrange("(c p) n -> p c n", p=128), LRN[:], reads=[bLRN])

            load_x(0)
            load_x(1)
            for bi in range(NB):
                chain1(bi)
                chain2(bi)
                if bi + 2 < NB:
                    load_x(bi + 2)
            P.emit_segment()
        sWI.close()

        sW = top.enter_context(contextlib.ExitStack())
        sbW, _ = mk(sW)
        WUP = sbW("WUP", [128, 8, 2 * DFF], BF16); bWUP = Buf()
        with contextlib.ExitStack() as s2:
            if stop_after < 2:
                return nc
            sb, ps = mk(s2)
            WSW = [sb("WSW%d" % i, [128, 512]) for i in range(4)]; bWSW = [Buf() for _ in range(4)]
            KX = [sb("KX%d" % i, [96, S], BF16) for i in range(2)]; bKX = [Buf(), Buf()]
            QX = [sb("QX%d" % i, [96, NQ], BF16) for i in range(2)]; bQX = [Buf(), Buf()]
            VX = [sb("VX%d" % i, [128, 64, 65], BF16) for i in range(2)]; bVX = [Buf(), Buf()]
            PP = [sb("PP%d" % i, [128, 2, 256], BF16) for i in range(3)]; bPP = [Buf() for _ in range(3)]
            AO = [sb("AO%d" % i, [128, 2, 64]) for i in range(2)]; bAO = [Buf(), Buf()]
            RD = sb("RD", [128, 4]); bRD = Buf()
            SB_ = [ps("SB%d" % i, [128, 2, 256]) for i in range(3)]; bSB = [Buf() for _ in range(3)]
            OB = [[ps("OB%d_%d" % (i, q), [128, 512]) for q in range(2)] for i in range(2)]
            bOB = [[Buf(), Buf()], [Buf(), Buf()]]
            for i in range(2):
                P.dma(SP, KX[i][64:96, :], oh, writes=[bKX[i]])
                P.op(DVE, lambda e, i=i: e.memset(VX[i][:, :, 64:65], 1.0), writes=[bVX[i]])

            def load_head(h):
                i = h % 2
                P.dma(SP, KX[i][0:64, :], KT_s[h * 64:(h + 1) * 64, :], writes=[bKX[i]])
                P.dma(SP, QX[i][0:64, :], QT_s[h * 64:(h + 1) * 64, :], writes=[bQX[i]])
                P.dma(SP, QX[i][64:96, :], NS_s[h * 32:(h + 1) * 32, :], writes=[bQX[i]])
                P.dma(SP, VX[i][:, :, 0:64], V_s[h].rearrange("t p d -> p t d"), writes=[bVX[i]])

            its = []
            oi = 0
            for h in range(8):
                for qi in range(NOWN):
                    bi = OWN0 + qi
                    ob = oi % 2; oi += 1
                    for j in range(bi + 1):
                        its.append((h, qi, bi, ob, j))

            def emit_qk(n):
                h, qi, bi, ob, j = its[n]
                hb = h % 2
                sbk = n % 3
                qc = slice(qi * 256, (qi + 1) * 256)
                diag = (j == bi)
                for kh in range(2):
                    kc_ = slice(j * 256 + kh * 128, j * 256 + (kh + 1) * 128)
                    if not diag:
                        P.op(PE, MM_(SB_[sbk][:, kh, :], KX[hb][0:96, kc_], QX[hb][0:96, qc], True, True),
                             reads=[bKX[hb], bQX[hb]], writes=[bSB[sbk]])
                    else:
                        P.op(PE, MM_(SB_[sbk][:, kh, :], KX[hb][0:64, kc_], QX[hb][0:64, qc], True, False),
                             reads=[bKX[hb], bQX[hb]], writes=[bSB[sbk]])
                        P.op(PE, MM_(SB_[sbk][:, kh, :], IDB[:], MKD[:, kh, :], False, True),
                             reads=[bC0], writes=[bSB[sbk]])

            def emit_rest(n):
                h, qi, bi, ob, j = its[n]
                hb = h % 2
                sbk = n % 3
                nj = bi + 1
                P.op(ACT, A_(PP[sbk][:], SB_[sbk][:], AF.Exp, scale=0.125), reads=[bSB[sbk]], writes=[bPP[sbk]])
                if n + 2 < len(its):
                    emit_qk(n + 2)
                for kh in range(2):
                    for qh in range(2):
                        P.op(PE, MM_(OB[ob][qh][:, 0:65], PP[sbk][:, kh, qh * 128:(qh + 1) * 128], VX[hb][:, j * 2 + kh, :],
                                     (j == 0 and kh == 0), (j == nj - 1 and kh == 1)),
                             reads=[bPP[sbk], bVX[hb]], writes=[bOB[ob][qh]])
                if j == nj - 1:
                    for qh in range(2):
                        P.op(DVE, lambda e, ob=ob, qh=qh: e.reciprocal(RD[:, ob * 2 + qh:ob * 2 + qh + 1], OB[ob][qh][:, 64:65]),
                             reads=[bOB[ob][qh]], writes=[bRD])
                        P.op(DVE, TS_(AO[ob][:, qh, :], OB[ob][qh][:, 0:64], RD[:, ob * 2 + qh:ob * 2 + qh + 1], None, ALU.mult),
                             reads=[bOB[ob][qh], bRD], writes=[bAO[ob]])
                    P.dma(POOL, AT_s[qi * 256:(qi + 1) * 256, h * 64:(h + 1) * 64].rearrange("(qh p) d -> p qh d", p=128), AO[ob][:],
                          reads=[bAO[ob]])

            load_head(0)
            load_head(1)
            def wup_piece(g):
                i = g % 4
                P.dma(SP, WSW[i][:].rearrange("p (kc n) -> p kc n", kc=8),
                      w_up[:, g * 64:(g + 1) * 64].rearrange("(kc p) n -> p kc n", p=128), writes=[bWSW[i]])
                P.op(DVE, CP_(WUP[:, :, g * 64:(g + 1) * 64], WSW[i][:].rearrange("p (kc n) -> p kc n", kc=8)),
                     reads=[bWSW[i]], writes=[bWUP])
            wnext = [0]
            emit_qk(0)
            emit_qk(1)
            for n in range(len(its)):
                h, qi, bi, ob, j = its[n]
                if n > 0 and its[n - 1][0] != h and h + 1 < 8:
                    load_head(h + 1)
                if n % 24 == 12 and wnext[0] < 88:
                    wup_piece(wnext[0]); wnext[0] += 1
                emit_rest(n)
            while wnext[0] < 88:
                wup_piece(wnext[0]); wnext[0] += 1
            P.sched = False
            P.emit_segment()
            P.sched = True

        with contextlib.ExitStack() as s3:
            if stop_after < 3:
                return nc
            sb, ps = mk(s3)
            WDN = sb("WDN", [128, 22, D], BF16); bWDN = Buf()
            XD = [sb("XT3_%d" % i, [128, 2, D]) for i in range(2)]; bXD = [Buf(), Buf()]
            AT = sb("AT", [128, 512]); bAT = Buf()
            ST = sb("ST3", [128, 8]); bST = Buf()
            AN = sb("AN", [128, 512], BF16); bAN = Buf()
            HTD = [sb("HTD%d" % i, [128, 8, 256], BF16) for i in range(2)]; bHTD = [Buf(), Buf()]
            HN = sb("HN3", [128, D], BF16); bHN = Buf()
            UB = [sb("UB%d" % i, [128, 2, 258]) for i in range(3)]; bUB = [Buf() for _ in range(3)]; bUBh = [Buf() for _ in range(3)]
            ACC = [sb("ACC%d" % i, [128, 2, 256]) for i in range(3)]; bACC = [Buf() for _ in range(3)]
            WS3 = [ACC[i][:].rearrange("p a n -> p (a n)") for i in range(3)]; bWS3 = bACC
            UH = sb("UH", [128, 44, 2]); bUH = Buf()
            ACTT = sb("ACTT", [128, 22, 256], BF16); bACTT = Buf(); bACTTm = [Buf() for _ in range(22)]
            PY = [ps("PY%d" % i, [128, 512]) for i in range(4)]; bPY = [Buf() for _ in range(4)]
            PT1 = ps("PT3", [128, 4, 256], BF16); bPT1 = Buf()
            PU = [ps("PU%d" % i, [128, 2, 256]) for i in range(3)]; bPU = [Buf() for _ in range(3)]

            ACTF = ACTT[:].rearrange("p m n -> p (m n)").bitcast(F32)
            STG = [(WS3[0], bACC[0]), (WS3[1], bACC[1]), (WS3[2], bACC[2])] + [(ACTF[:, k * 512:(k + 1) * 512], Buf()) for k in range(5)]
            wi = 0
            for m in range(22):
                for nh in range(2):
                    stg, bstg = STG[wi % len(STG)]; wi += 1
                    P.dma(SP, stg, w_down[m * 128:(m + 1) * 128, nh * 512:(nh + 1) * 512], writes=[bstg])
                    P.op(DVE, TT_(WDN[:, m, nh * 512:(nh + 1) * 512], stg, G2B[:, nh * 512:(nh + 1) * 512], ALU.mult),
                         reads=[bstg, bG2B], writes=[bWDN])
            P.op(DVE, _w(lambda e: e.memset(UH[:], 0.0), 88), reads=[b for _, b in STG[3:]], writes=[bUH] + bACTTm)

            up_i = 0
            for qi in range(NOWN):
                bi = OWN0 + qi
                Xt, bXt = XD[qi % 2], bXD[qi % 2]
                MIXT = HT = HTD[qi % 2]
                bMIXL = bMIXA = bHT = bHTD[qi % 2]
                P.dma(SP, Xt[:], xs[bi * 256:(bi + 1) * 256, :].rearrange("(t p) d -> p t d", p=128), writes=[bXt])
                P.dma(SP, MIXT[:, 0:4, :], LR_s[:, qi * 256:(qi + 1) * 256].rearrange("(c p) n -> p c n", p=128), writes=[bMIXL])
                for t in range(2):
                    P.dma(SP, AT[:], AT_s[qi * 256 + t * 128:qi * 256 + (t + 1) * 128, :], writes=[bAT])
                    P.op(ACT, A_(AN[:], AT[:], AF.Square, accum_out=ST[:, 0:1]), reads=[bAT], writes=[bAN, bST])
                    P.op(ACT, A_(ST[:, 2:3], ST[:, 0:1], AF.Ln, scale=1.0 / 512, bias=EPS), reads=[bST], writes=[bST])
                    P.op(ACT, A_(ST[:, 4:5], ST[:, 2:3], AF.Exp, scale=-0.5), reads=[bST], writes=[bST])
                    P.op(DVE, TS_(AN[:], AT[:], ST[:, 4:5], None, ALU.mult), reads=[bAT, bST], writes=[bAN])
                    for c in range(4):
                        P.op(PE, TR_(PT1[:, c, t * 128:(t + 1) * 128], AN[:, c * 128:(c + 1) * 128], IDB[:]), reads=[bAN], writes=[bPT1])
                P.op(DVE, CP_(MIXT[:, 4:8, :], PT1[:]), reads=[bPT1], writes=[bMIXA])
                for t in range(2):
                    for nh in range(2):
                        pb = t * 2 + nh
                        for kc in range(8):
                            P.op(PE, MM_(PY[pb][:], MIXT[:, kc, t * 128:(t + 1) * 128], WOUT[:, kc, nh * 512:(nh + 1) * 512], kc == 0, kc == 7),
                                 reads=[bMIXL, bMIXA, bWOUT], writes=[bPY[pb]])
                        P.op(DVE, TT_(Xt[:, t, nh * 512:(nh + 1) * 512], PY[pb][:], Xt[:, t, nh * 512:(nh + 1) * 512], ALU.add),
                             reads=[bPY[pb], bXt], writes=[bXt])
                if debug:
                    P.dma(POOL, X1_s[qi * 256:(qi + 1) * 256, :].rearrange("(t p) d -> p t d", p=128), Xt[:], reads=[bXt])
                for t in range(2):
                    P.op(ACT, A_(HN[:], Xt[:, t, :], AF.Square, accum_out=ST[:, t:t + 1]), reads=[bXt], writes=[bHN, bST])
                P.op(ACT, A_(ST[:, 2:4], ST[:, 0:2], AF.Ln, scale=1.0 / D, bias=EPS), reads=[bST], writes=[bST])
                P.op(ACT, A_(ST[:, 4:6], ST[:, 2:4], AF.Exp, scale=-0.5), reads=[bST], writes=[bST])
                HT2 = HT[:].rearrange("p k (t n) -> p k t n", t=2)
                for t in range(2):
                    P.op(DVE, TS_(HN[:], Xt[:, t, :], ST[:, 4 + t:5 + t], None, ALU.mult), reads=[bXt, bST], writes=[bHN])
                    for rnd in range(2):
                        for k4 in range(4):
                            kc = rnd * 4 + k4
                            P.op(PE, TR_(PT1[:, k4, 0:128], HN[:, kc * 128:(kc + 1) * 128], IDB[:]), reads=[bHN], writes=[bPT1])
                        for k4 in range(4):
                            kc = rnd * 4 + k4
                            P.op(DVE, TS_(HT[:, kc, t * 128:(t + 1) * 128], PT1[:, k4, 0:128], G2[:, kc:kc + 1], SH2[:, kc:kc + 1], ALU.mult, ALU.add),
                                 reads=[bPT1, bMOD], writes=[bHT])
                if qi == 1:
                    P.op(DVE, TS_(UH[:], UH[:], FLG[:, 0:1], None, ALU.mult), reads=[bUH, bC0], writes=[bUH])
                for m in range(22):
                    u = up_i % 3; up_i += 1
                    bPU_, bUB_, bACC_ = bPU, bUB, bACC
                    for half, mm in ((0, m), (1, m + 22)):
                        for kc in range(8):
                            P.op(PE, MM_(PU[u][:, half, :], WUP[:, kc, mm * 128:(mm + 1) * 128], HT[:, kc, :], kc == 0, kc == 7),
                                 reads=[bWUP, bHT], writes=[bPU_[u]])
                    if qi > 0:
                        for half, mm in ((0, m), (1, m + 22)):
                            P.op(POOL, CP_(UB[u][:, half, 0:2], UH[:, mm, :]), reads=[bUH], writes=[bUBh[u]])
                    P.op(ACT, A_(UB[u][:, :, 2:258], PU[u][:], AF.Copy), reads=[bPU_[u]], writes=[bUB_[u]])
                    for half, mm in ((0, m), (1, m + 22)):
                        P.op(POOL, CP_(UH[:, mm, :], UB[u][:, half, 256:258]), reads=[bUB_[u]], writes=[bUH])
                    if qi == 0:
                        continue
                    for half, mm in ((0, m), (1, m + 22)):
                        P.op(ACT, A_(ACC[u][:, half, :], UB[u][:, half, 0:256], AF.Identity, scale=FCW[:, mm, 0:1], bias=FCB[:, mm:mm + 1]),
                             reads=[bUB_[u], bUBh[u], bC0], writes=[bACC_[u]])
                        for jj in (1, 2):
                            P.op(DVE, STT_(ACC[u][:, half, :], UB[u][:, half, jj:jj + 256], FCW[:, mm, jj:jj + 1], ACC[u][:, half, :], ALU.mult, ALU.add),
                                 reads=[bUB_[u], bUBh[u], bACC_[u], bC0], writes=[bACC_[u]])
                    P.op(ACT, A_(ACC[u][:, 0, :], ACC[u][:, 0, :], AF.Silu), reads=[bACC_[u]], writes=[bACC_[u]])
                    P.op(POOL, TT_(ACTT[:, m, :], ACC[u][:, 0, :], ACC[u][:, 1, :], ALU.mult), reads=[bACC_[u]], writes=[bACTTm[m]])
                if qi == 0:
                    continue
                for m in range(22):
                    for t in range(2):
                        for nh in range(2):
                            pb = t * 2 + nh
                            P.op(PE, MM_(PY[pb][:], ACTT[:, m, t * 128:(t + 1) * 128], WDN[:, m, nh * 512:(nh + 1) * 512], m == 0, m == 21),
                                 reads=[bACTTm[m], bWDN], writes=[bPY[pb]])
                for t in range(2):
                    for nh in range(2):
                        pb = t * 2 + nh
                        P.op(DVE, TT_(Xt[:, t, nh * 512:(nh + 1) * 512], PY[pb][:], Xt[:, t, nh * 512:(nh + 1) * 512], ALU.add),
                             reads=[bPY[pb], bXt], writes=[bXt])
                P.dma(POOL, out[(qi - 1) * 256:qi * 256, :].rearrange("(t p) d -> p t d", p=128), Xt[:], reads=[bXt])
            P.emit_segment(last=True)
            nc._est_log = P.est_log
    return nc


def make_in_maps(x, c, w_ada, b_ada, norm1_g, w_in, q_norm_g, k_norm_g, lru_conv_w, lru_conv_b,
                 lru_wa, lru_ba, lru_wx, lru_bx, lru_lambda, lru_out_g, attn_out_g, w_out,
                 norm2_g, w_up, ffn_conv_w, ffn_conv_b, w_down):
    f = lambda a: np.ascontiguousarray(np.asarray(a, dtype=np.float32))
    x = f(x); c = f(c)

    def fm(v, nch):
        return f(np.asarray(v, np.float32).reshape(nch, 128).T)

    def bd(w):
        w = np.asarray(w, np.float32)
        o = np.zeros((128, 4, 128), np.float32)
        for ch in range(4):
            for s_ in range(2):
                o[s_ * 64:(s_ + 1) * 64, ch, s_ * 64:(s_ + 1) * 64] = w[ch * 2 + s_]
        return o

    shared = {
        "w_ada": f(w_ada[0]), "b_ada": f(b_ada[0]).reshape(1, -1),
        "n1g": fm(norm1_g[0], 8), "n2g": fm(norm2_g[0], 8),
        "mixg": fm(np.concatenate([np.asarray(lru_out_g[0]), np.asarray(attn_out_g[0])]), 8),
        "w_in": f(w_in[0]),
        "gqb": f(np.broadcast_to(np.tile(np.asarray(q_norm_g[0], np.float32), 8), (128, 512))),
        "gkb": f(np.broadcast_to(np.tile(np.asarray(k_norm_g[0], np.float32), 8), (128, 512))),
        "cw": f(np.asarray(lru_conv_w[0], np.float32).reshape(4, 4, 128).transpose(2, 1, 0)),
        "cb": fm(lru_conv_b[0], 4),
        "wabd": bd(lru_wa[0]), "wxbd": bd(lru_wx[0]),
        "lba": fm(lru_ba[0], 4), "lbx": fm(lru_bx[0], 4), "lam": fm(lru_lambda[0], 4),
        "w_out": f(w_out[0]), "w_up": f(w_up[0]),
        "fcw": f(np.asarray(ffn_conv_w[0], np.float32).reshape(3, 44, 128).transpose(2, 1, 0)),
        "fcb": fm(ffn_conv_b[0], 44),
        "w_down": f(w_down[0]),
        "identb": np.eye(128, dtype=np.float32).astype(ml_dtypes.bfloat16),
        "identf": np.eye(128, dtype=np.float32),
        "onesb": np.ones((128, 128), np.float32).astype(ml_dtypes.bfloat16),
    }
    ohm = np.zeros((32, S), np.float32)
    for j in range(32):
        ohm[j, j * 256:(j + 1) * 256] = NEG
    shared["oh"] = ohm.astype(ml_dtypes.bfloat16)
    kk = np.arange(128)[:, None]
    qq = np.arange(128)[None, :]
    tri = np.where(kk <= qq, 0.0, NEG).astype(np.float32)
    md = np.zeros((128, 2, 256), np.float32)
    md[:, 0, 0:128] = tri
    md[:, 1, 0:128] = NEG
    md[:, 1, 128:256] = tri
    shared["maskd"] = md.astype(ml_dtypes.bfloat16)
    in_maps = []
    for core in range(8):
        b, half = core // 2, core % 2
        if half == 1:
            xsv = x[b]
        else:
            xsv = np.concatenate([np.zeros((4096, D), np.float32), x[b, :4096]], axis=0)
        pb = np.full((NOWN, 32), -1e30, np.float32)
        for qi in range(NOWN):
            bi = OWN0 + qi
            lo = 16 if half == 0 else 0
            if bi > lo:
                pb[qi, lo:bi] = 0.0
        m = dict(shared)
        m["xs"] = f(xsv)
        m["cT"] = fm(c[b], 8)
        m["flag"] = np.full((128, 1), float(half), np.float32)
        m["pastb"] = f(np.broadcast_to(pb[None], (128, NOWN, 32)))
        in_maps.append(m)
    return in_maps


_NC = {}


def kernel(**inputs):
    in_maps = make_in_maps(**inputs)
    if "nc" not in _NC:
        _NC["nc"] = build(False)
    res = run_bass_kernel_spmd(_NC["nc"], in_maps, core_ids=list(range(8)))
    outp = np.zeros((4, S, D), np.float32)
    for core in range(8):
        b, half = core // 2, core % 2
        outp[b, half * 4096:(half + 1) * 4096] = res.results[core]["out"]
    return outp
```

```python
import contextlib
import numpy as np
import ml_dtypes
import concourse.bass as bass
import concourse.mybir as mybir
from concourse.bass_utils import run_bass_kernel_spmd

F32 = mybir.dt.float32
BF16 = mybir.dt.bfloat16
AF = mybir.ActivationFunctionType
ALU = mybir.AluOpType
AX = mybir.AxisListType

PE, ACT, DVE, POOL, SP = "tensor", "scalar", "vector", "gpsimd", "sync"
COMPUTE = (PE, ACT, DVE, POOL)

D = 1024
S = 8192
NB = 32
OWN0 = 15
NOWN = NB - OWN0
NQ = NOWN * 256
DFF = 2816
NEG = -30000.0
EPS = 1e-6


class Buf:
    __slots__ = ("name", "w", "rs")

    def __init__(self, name=""):
        self.name = name
        self.w = None
        self.rs = []


class Op:
    __slots__ = ("eng", "fn", "deps", "dma", "sig", "sem", "need", "seg", "alld", "cost", "lat", "line")

    def __init__(self, eng, fn, dma, seg):
        self.eng = eng
        self.fn = fn
        self.dma = dma
        self.deps = set()
        self.sig = None
        self.sem = None
        self.need = False
        self.seg = seg


class Prog:
    def __init__(self, nc, stack, n_dma_sems=16):
        self.nc = nc
        self.n = n_dma_sems
        self.seg = 0
        self.ops = []
        self.esem = {e: stack.enter_context(nc.semaphore("s_" + e)) for e in COMPUTE}
        self.dsem = {q: [stack.enter_context(nc.semaphore("d_%s_%d" % (q, i))) for i in range(n_dma_sems)]
                     for q in (SP, POOL)}
        self.ecnt = {e: 0 for e in COMPUTE}
        self.dcnt = {q: [0] * n_dma_sems for q in self.dsem}
        self.dlast = {q: [None] * n_dma_sems for q in self.dsem}
        self.dnum = {q: 0 for q in self.dsem}
        self.finals = []
        self.rec = None
        self.sched = True
        self.est_ns = 0.0
        self.est_log = []

    def chain(self, fn):
        self.rec = []
        fn()
        lst, self.rec = self.rec, None
        return lst

    def merge(self, *chains):
        items = []
        for ci, ch in enumerate(chains):
            n = len(ch)
            for i, it in enumerate(ch):
                items.append(((i + 0.5) / n, ci, i, it))
        items.sort(key=lambda x: (x[0], x[1], x[2]))
        for _, _, _, it in items:
            self.op(*it)

    def op(self, eng, fn, reads=(), writes=(), dma=False):
        if self.rec is not None:
            self.rec.append((eng, fn, tuple(reads), tuple(writes), dma))
            return None
        o = Op(eng, fn, dma, self.seg)
        for b in reads:
            if b.w is not None:
                o.deps.add(b.w)
        for b in writes:
            if b.w is not None:
                o.deps.add(b.w)
            for r in b.rs:
                o.deps.add(r)
        for b in reads:
            b.rs.append(o)
        for b in writes:
            b.w = o
            b.rs = []
        o.deps.discard(o)
        import sys as _sys
        fr = _sys._getframe(1)
        if fr.f_code.co_name in ('dma', 'merge'):
            fr = fr.f_back
        o.line = fr.f_lineno
        n = getattr(fn, "n", 256)
        k = getattr(fn, "k", 0)
        if dma:
            o.cost = 60.0
            o.lat = 2200.0 + n / 60.0
        elif eng == PE:
            o.cost = 45.0 + 0.37 * n
            o.lat = o.cost + 200.0
        elif eng == ACT:
            o.cost = 130.0 + 0.83 * n + 120.0 * k
            o.lat = o.cost + 120.0
        elif eng == POOL:
            o.cost = 250.0 + 2.0 * n
            o.lat = o.cost + 150.0
        else:
            o.cost = 150.0 + (1.05 if k else 0.6) * n
            o.lat = o.cost + 120.0
        self.ops.append(o)
        return o

    def dma(self, eng, out, in_, reads=(), writes=()):
        nbytes = 1
        for d in out.shape:
            nbytes *= int(d)
        nbytes *= 2 if out.dtype == BF16 else 4
        f = lambda e: e.dma_start(out=out, in_=in_)
        f.n = nbytes
        return self.op(eng, f, reads, writes, dma=True)

    def schedule(self, ops):
        import bisect
        n = len(ops)
        pos = {id(o): i for i, o in enumerate(ops)}
        succ = [[] for _ in range(n)]
        indeg = [0] * n
        for i, o in enumerate(ops):
            for d in o.alld:
                j = pos.get(id(d))
                if j is None:
                    continue
                succ[j].append(i)
                indeg[i] += 1
        rt = [0.0] * n
        why = [None] * n
        stt = [0.0] * n
        lastop = {}
        avail = {e: [] for e in (PE, ACT, DVE, POOL, SP)}
        free = {e: 0.0 for e in avail}
        for i, o in enumerate(ops):
            if indeg[i] == 0:
                avail[o.eng].append(i)
        order = []
        WIN_ = 40
        done = 0
        while done < n:
            best = None
            for e, lst in avail.items():
                if not lst:
                    continue
                fe = free[e]
                cb = None
                for i in lst[:WIN_]:
                    st = rt[i] if rt[i] > fe else fe
                    if cb is None or st < cb[0]:
                        cb = (st, i)
                if best is None or cb < best[0:2]:
                    best = (cb[0], cb[1], e)
            st, i, e = best
            lst = avail[e]
            lst.pop(bisect.bisect_left(lst, i))
            o = ops[i]
            stt[i] = st
            if rt[i] < st and e in lastop:
                why[i] = ('eng', lastop[e])
            lastop[e] = i
            free[e] = st + o.cost
            fin = st + o.lat
            order.append(o)
            done += 1
            for k in succ[i]:
                if fin > rt[k]:
                    rt[k] = fin
                    if why[k] is None or why[k][0] == 'dep':
                        why[k] = ('dep', i)
                indeg[k] -= 1
                if indeg[k] == 0:
                    bisect.insort(avail[ops[k].eng], k)
        self.est_ns = max(free.values())
        busy = {}
        for o in ops:
            busy[o.eng] = busy.get(o.eng, 0.0) + o.cost
        self.est_busy = {k: round(v / 1e3) for k, v in busy.items()}
        return order

    def emit_segment(self, last=False):
        nc = self.nc
        seg = self.seg
        for o in self.ops:
            o.alld = [d for d in o.deps if d.seg == seg]
        ops = self.schedule(self.ops) if self.sched else self.ops
        self.est_log.append((seg, len(ops), round(self.est_ns / 1e3), self.est_busy))
        queues = {e: [] for e in (PE, ACT, DVE, POOL, SP)}
        for o in ops:
            queues[o.eng].append(o)
            keep = set()
            for d in o.deps:
                if d.seg != seg:
                    continue
                if d.dma:
                    keep.add(d)
                elif d.eng == o.eng and not o.dma:
                    if o.eng != PE:
                        keep.add(d)
                else:
                    keep.add(d)
            o.deps = keep
            for d in keep:
                d.need = True
        for e in COMPUTE:
            for o in reversed(queues[e]):
                if not o.dma:
                    o.need = True
                    break
        for o in ops:
            if o.dma:
                q = o.eng
                i = self.dnum[q] % self.n
                self.dnum[q] += 1
                prev = self.dlast[q][i]
                if prev is not None and prev.seg == seg:
                    o.deps.add(prev)
                self.dcnt[q][i] += 16
                o.sem = self.dsem[q][i]
                o.sig = self.dcnt[q][i]
                self.dlast[q][i] = o
            elif o.need:
                self.ecnt[o.eng] += 1
                o.sem = self.esem[o.eng]
                o.sig = self.ecnt[o.eng]
        with nc.Block() as block:
            def run(engname, eng):
                waited = {}
                for o in queues[engname]:
                    need = {}
                    for d in o.deps:
                        k = id(d.sem)
                        if waited.get(k, 0) >= d.sig:
                            continue
                        if k not in need or need[k][1] < d.sig:
                            need[k] = (d.sem, d.sig)
                    for k, (s, v) in need.items():
                        eng.wait_ge(s, v)
                        waited[k] = v
                    ins = o.fn(eng)
                    if o.sem is not None:
                        ins.then_inc(o.sem, 16 if o.dma else 1)
                for e in COMPUTE:
                    if e != engname and self.ecnt[e] > 0:
                        eng.wait_ge(self.esem[e], self.ecnt[e])
                for q in self.dsem:
                    for i in range(self.n):
                        if self.dcnt[q][i] > 0:
                            eng.wait_ge(self.dsem[q][i], self.dcnt[q][i])

            @block.sync
            def _(e):
                run(SP, e)

            @block.tensor
            def _(e):
                run(PE, e)

            @block.scalar
            def _(e):
                run(ACT, e)

            @block.vector
            def _(e):
                run(DVE, e)

            @block.gpsimd
            def _(e):
                run(POOL, e)
        self.ops = []
        self.seg += 1


def _fs(ap):
    n = 1
    for d in ap.shape[1:]:
        n *= int(d)
    return n


def _w(f, n):
    f.n = n
    return f


def A_(out, in_, func, **kw):
    f = _w(lambda e: e.activation(out=out, in_=in_, func=func, **kw), _fs(out))
    f.k = sum(1 for key in ("scale", "bias", "accum_out") if key in kw and not isinstance(kw[key], (int, float)))
    return f


def TS_(out, in0, s1, s2, op0, op1=None):
    if op1 is None:
        return _w(lambda e: e.tensor_scalar(out=out, in0=in0, scalar1=s1, scalar2=None, op0=op0), _fs(out))
    return _w(lambda e: e.tensor_scalar(out=out, in0=in0, scalar1=s1, scalar2=s2, op0=op0, op1=op1), _fs(out))


def TT_(out, in0, in1, op):
    f = _w(lambda e: e.tensor_tensor(out=out, in0=in0, in1=in1, op=op), _fs(out))
    f.k = 1
    return f


def STT_(out, in0, scalar, in1, op0, op1):
    f = _w(lambda e: e.scalar_tensor_tensor(out=out, in0=in0, scalar=scalar, in1=in1, op0=op0, op1=op1), _fs(out))
    f.k = 1
    return f


def CP_(out, in_):
    return _w(lambda e: e.tensor_copy(out, in_), _fs(out))


def MM_(out, lhsT, rhs, start, stop):
    return _w(lambda e: e.matmul(out, lhsT=lhsT, rhs=rhs, start=start, stop=stop), _fs(rhs))


def TR_(out, in_, ident):
    return _w(lambda e: e.transpose(out, in_, ident), 128)


def build(debug=False, stop_after=3):
    nc = bass.Bass("TRN2", target_bir_lowering=False)

    def din(name, shape, dt=F32):
        return nc.dram_tensor(name, list(shape), dt, kind="ExternalInput").ap()

    skind = "ExternalOutput" if debug else "Internal"

    def dscr(name, shape, dt):
        return nc.dram_tensor(name, list(shape), dt, kind=skind).ap()

    xs = din("xs", [S, D])
    cT = din("cT", [128, 8])
    w_ada = din("w_ada", [D, 6 * D])
    b_ada = din("b_ada", [1, 6 * D])
    n1g = din("n1g", [128, 8])
    n2g = din("n2g", [128, 8])
    mixg = din("mixg", [128, 8])
    w_in = din("w_in", [D, 2560])
    gqb = din("gqb", [128, 512])
    gkb = din("gkb", [128, 512])
    cw = din("cw", [128, 4, 4])
    cb = din("cb", [128, 4])
    wabd = din("wabd", [128, 4, 128])
    wxbd = din("wxbd", [128, 4, 128])
    lba = din("lba", [128, 4])
    lbx = din("lbx", [128, 4])
    lam = din("lam", [128, 4])
    w_out = din("w_out", [D, D])
    w_up = din("w_up", [D, 2 * DFF])
    fcw = din("fcw", [128, 44, 3])
    fcb = din("fcb", [128, 44])
    w_down = din("w_down", [DFF, D])
    flag = din("flag", [128, 1])
    pastb = din("pastb", [128, NOWN, 32])
    oh = din("oh", [32, S], BF16)
    identb = din("identb", [128, 128], BF16)
    identf = din("identf", [128, 128])
    onesb = din("onesb", [128, 128], BF16)
    maskd = din("maskd", [128, 2, 256], BF16)
    out = nc.dram_tensor("out", [4096, D], F32, kind="ExternalOutput").ap()

    KT_s = dscr("KT_s", [512, S], BF16)
    QT_s = dscr("QT_s", [512, NQ], BF16)
    NS_s = dscr("NS_s", [256, NQ], BF16)
    LR_s = dscr("LR_s", [512, NQ], BF16)
    V_s = dscr("V_s", [8, 64, 128, 64], BF16)
    AT_s = dscr("AT_s", [NQ, 512], F32)
    X1_s = dscr("X1_s", [NQ, D], F32) if debug else None

    with contextlib.ExitStack() as top:
        P = Prog(nc, top)

        def mk(stack):
            def sb(name, shape, dt=F32):
                return stack.enter_context(nc.sbuf_tensor(name, list(shape), dt))

            def ps(name, shape, dt=F32):
                return stack.enter_context(nc.psum_tensor(name, list(shape), dt))
            return sb, ps

        sbT, _ = mk(top)
        WOUT = sbT("WOUT", [128, 8, D], BF16); bWOUT = Buf()
        G2B = sbT("G2B", [128, D], BF16); bG2B = Buf()
        G1 = sbT("G1", [128, 8]); SH1 = sbT("SH1", [128, 8]); G2 = sbT("G2", [128, 8]); SH2 = sbT("SH2", [128, 8])
        bMOD = Buf()
        IDB = sbT("IDB", [128, 128], BF16); ONB = sbT("ONB", [128, 128], BF16); MKD = sbT("MKD", [128, 2, 256], BF16)
        FLG = sbT("FLG", [128, 1]); bCONST = Buf()
        CW = sbT("CW", [128, 4, 4]); CB = sbT("CB", [128, 4]); CL = sbT("CL", [128, 4]); CL2 = sbT("CL2", [128, 4])
        NBA = sbT("NBA", [128, 4]); NBX = sbT("NBX", [128, 4])
        WAB = sbT("WAB", [128, 4, 128], BF16); WXB = sbT("WXB", [128, 4, 128], BF16)
        FCW = sbT("FCW", [128, 44, 3]); FCB = sbT("FCB", [128, 44])

        sWI = contextlib.ExitStack()
        sbWI, _ = mk(sWI)
        WIN = sbWI("WIN", [128, 8, 2560], BF16); bWIN = Buf()
        with contextlib.ExitStack() as s0:
            sb, ps = mk(s0)
            WIS = [sb("WIS%d" % i, [128, 8, 320]) for i in range(2)]; bWIS = [Buf(), Buf()]
            CT = sb("CT", [128, 8]); CREP = sb("CREP", [128, 8, 128])
            WAS = [sb("WAS%d" % i, [128, 8, 512]) for i in range(3)]; bWAS = [Buf() for _ in range(3)]
            BB = [sb("BB%d" % i, [128, 512]) for i in range(3)]; bBB = [Buf() for _ in range(3)]
            MB = sb("MB", [128, 6 * D]); bMB = Buf()
            TMP = sb("TMP", [128, 48, 128]); bTMP = Buf()
            MT = sb("MT", [128, 48]); bMT = Buf()
            IDF = sb("IDF", [128, 128])
            N1G = sb("N1G", [128, 8]); N2G = sb("N2G", [128, 8]); MIXG = sb("MIXG", [128, 8])
            LBA = sb("LBA", [128, 4]); LBX = sb("LBX", [128, 4]); LAM = sb("LAM", [128, 4])
            WAF = sb("WAF", [128, 4, 128]); WXF = sb("WXF", [128, 4, 128])
            PS0 = [ps("PS0_%d" % i, [128, 512]) for i in range(2)]; bPS0 = [Buf(), Buf()]
            bC0 = Buf()
            for dst, src in ((CT, cT), (IDF, identf), (N1G, n1g), (N2G, n2g), (MIXG, mixg), (LBA, lba), (LBX, lbx),
                             (LAM, lam), (WAF, wabd), (WXF, wxbd), (IDB, identb), (ONB, onesb), (MKD, maskd),
                             (FLG, flag), (CW, cw), (CB, cb), (FCW, fcw), (FCB, fcb)):
                P.dma(SP, dst[:], src, writes=[bC0])
            P.op(DVE, CP_(CREP[:], CT[:, :].unsqueeze(2).to_broadcast([128, 8, 128])), reads=[bC0], writes=[bCONST])
            for g in range(12):
                i = g % 3
                P.dma(SP, WAS[i][:], w_ada[:, g * 512:(g + 1) * 512].rearrange("(kc p) n -> p kc n", p=128), writes=[bWAS[i]])
                P.dma(SP, BB[i][:], b_ada[0:1, g * 512:(g + 1) * 512].partition_broadcast(128)[:, 0, :], writes=[bBB[i]])
                for kc in range(8):
                    P.op(PE, MM_(PS0[g % 2][:], CREP[:, kc, :], WAS[i][:, kc, :], kc == 0, kc == 7),
                         reads=[bCONST, bWAS[i]], writes=[bPS0[g % 2]])
                P.op(DVE, TT_(MB[:, g * 512:(g + 1) * 512], PS0[g % 2][:], BB[i][:], ALU.add),
                     reads=[bPS0[g % 2], bBB[i]], writes=[bMB])
            for g in range(8):
                i = g % 2
                P.dma(SP, WIS[i][:], w_in[:, g * 320:(g + 1) * 320].rearrange("(kc p) n -> p kc n", p=128), writes=[bWIS[i]])
                P.op(DVE, CP_(WIN[:, :, g * 320:(g + 1) * 320], WIS[i][:]), reads=[bWIS[i]], writes=[bWIN])
            P.op(DVE, TT_(TMP[:], MB[:].rearrange("p (c j) -> p c j", j=128),
                          IDF[:, :].unsqueeze(1).to_broadcast([128, 48, 128]), ALU.mult),
                 reads=[bMB, bC0], writes=[bTMP])
            P.op(DVE, lambda e: e.tensor_reduce(out=MT[:], in_=TMP[:], axis=AX.X, op=ALU.add), reads=[bTMP], writes=[bMT])
            P.op(DVE, CP_(SH1[:], MT[:, 0:8]), reads=[bMT], writes=[bMOD])
            P.op(DVE, STT_(G1[:], MT[:, 8:16], 1.0, N1G[:], ALU.add, ALU.mult), reads=[bMT, bC0], writes=[bMOD])
            P.op(DVE, CP_(SH2[:], MT[:, 24:32]), reads=[bMT], writes=[bMOD])
            P.op(DVE, STT_(G2[:], MT[:, 32:40], 1.0, N2G[:], ALU.add, ALU.mult), reads=[bMT, bC0], writes=[bMOD])
            P.op(DVE, CP_(G2B[:], MB[:, 5 * D:6 * D]), reads=[bMB], writes=[bG2B])
            for nh in range(2):
                P.dma(SP, WAS[nh][:], w_out[:, nh * 512:(nh + 1) * 512].rearrange("(kc p) n -> p kc n", p=128), writes=[bWAS[nh]])
                for kc in range(8):
                    P.op(DVE, STT_(WOUT[:, kc, nh * 512:(nh + 1) * 512], WAS[nh][:, kc, :], MIXG[:, kc:kc + 1],
                                   MB[:, 2 * D + nh * 512:2 * D + (nh + 1) * 512], ALU.mult, ALU.mult),
                         reads=[bWAS[nh], bMB, bC0], writes=[bWOUT])
            P.op(ACT, A_(LAM[:], LAM[:], AF.Exp, scale=-1.0), reads=[bC0], writes=[bC0])
            P.op(ACT, A_(LAM[:], LAM[:], AF.Ln, bias=1.0), reads=[bC0], writes=[bC0])
            P.op(DVE, TS_(CL[:], LAM[:], -8.0, None, ALU.mult), reads=[bC0], writes=[bCONST])
            P.op(DVE, TS_(CL2[:], LAM[:], -16.0, None, ALU.mult), reads=[bC0], writes=[bCONST])
            P.op(DVE, TS_(NBA[:], LBA[:], -1.0, None, ALU.mult), reads=[bC0], writes=[bCONST])
            P.op(DVE, TS_(NBX[:], LBX[:], -1.0, None, ALU.mult), reads=[bC0], writes=[bCONST])
            P.op(DVE, CP_(WAB[:], WAF[:]), reads=[bC0], writes=[bCONST])
            P.op(DVE, CP_(WXB[:], WXF[:]), reads=[bC0], writes=[bCONST])
            P.emit_segment()

        with contextlib.ExitStack() as s1:
            if stop_after < 1:
                return nc
            sb, ps = mk(s1)
            PBT = sb("PBT", [128, NOWN, 32]); bPBT = Buf()
            GQB = sb("GQB", [128, 512]); GKB = sb("GKB", [128, 512])
            X = [sb("X%d" % i, [128, 2, D]) for i in range(2)]; bX = [Buf(), Buf()]
            J = sb("J", [128, D], BF16); bJ = Buf()
            ST = sb("ST", [128, 8]); bST = Buf()
            HN = sb("HN", [128, 2, D], BF16); bHN = Buf()
            HTd = [sb("HT%d" % i, [128, 8, 256], BF16) for i in range(2)]; bHTd = [Buf(), Buf()]
            SQ = sb("SQ", [128, 512]); bSQ = Buf()
            SK = sb("SK", [128, 24]); bSK = Buf()
            TK = sb("TK", [128, 512]); bTK = Buf()
            KN = sb("KN", [128, 512], BF16); bKN = Buf()
            KTt = sb("KTt", [128, 4, 256], BF16); bKTt = Buf()
            QTt = sb("QTt", [128, 4, 256], BF16); bQTt = Buf()
            KM = sb("KM", [128, 4, 2, 32], BF16); bKM = Buf()
            KMS = sb("KMS", [128, 4]); bKMS = Buf()
            VT = [sb("VT%d" % i, [128, 512], BF16) for i in range(2)]; bVT = [Buf(), Buf()]
            GB = sb("GB", [128, 8, 32]); bGB = Buf()
            M8 = sb("M8", [128, 8, 8]); bM8 = Buf()
            THR = sb("THR", [128, 8]); bTHR = Buf()
            NS = sb("NS", [128, 256], BF16); bNS = Buf()
            NSTt = sb("NSTt", [128, 2, 256], BF16); bNSTt = Buf()
            XR = sb("XR", [128, 4, 259]); bXR = Buf()
            XC = sb("XC", [128, 4, 256]); bXC = Buf()
            XCb = sb("XCb", [128, 4, 256], BF16); bXCb = Buf()
            R = sb("R", [128, 4, 256]); bR = Buf()
            I = sb("I", [128, 4, 256]); bI = Buf()
            AA = sb("AA", [128, 4, 256]); bAA = Buf()
            OM = sb("OM", [128, 4, 256]); bOM = Buf()
            H = sb("H", [128, 4, 256]); bH = Buf()
            HS = sb("HS", [128, 4]); bHS = Buf()
            T1 = sb("T1", [128, 4, 256]); bT1 = Buf()
            YSQ = sb("YSQ", [128, 4, 256], BF16); bYSQ = Buf()
            RS = sb("RS", [128, 256]); bRS = Buf()
            LRN = sb("LRN", [128, 4, 256], BF16); bLRN = Buf()
            PT1 = ps("PT1", [128, 4, 256], BF16); bPT1 = Buf()
            PF4 = ps("PF4", [128, 4, 256])
            PF = [PF4[:, 0:2, :], PF4[:, 2:4, :]]; bPF = [Buf(), Buf()]
            PK = [ps("PK%d" % i, [128, 512]) for i in range(2)]; bPK = [Buf(), Buf()]
            PM0 = ps("PM0", [128, 4, 256], BF16); bPM0 = Buf()
            PM1 = ps("PM1", [128, 512]); bPM1 = Buf()
            PRS = ps("PRS", [128, 512]); bPRS = Buf()

            P.dma(SP, PBT[:], pastb, writes=[bPBT])
            P.dma(SP, GQB[:], gqb, writes=[bPBT])
            P.dma(SP, GKB[:], gkb, writes=[bPBT])
            P.op(DVE, lambda e: e.memset(XR[:, :, 0:3], 0.0), writes=[bXR])
            P.op(DVE, lambda e: e.memset(HS[:], 0.0), writes=[bHS])
            P.op(DVE, lambda e: e.memset(KM[:], 0.0), writes=[bKM])

            def load_x(bi):
                P.dma(SP, X[bi % 2][:], xs[bi * 256:(bi + 1) * 256, :].rearrange("(t p) d -> p t d", p=128),
                      writes=[bX[bi % 2]])

            def norm_T(Xt, bXt, Gm, SHm, HT, bHT):
                for t in range(2):
                    P.op(ACT, A_(J[:], Xt[:, t, :], AF.Square, accum_out=ST[:, t:t + 1]), reads=[bXt], writes=[bJ, bST])
                P.op(ACT, A_(ST[:, 2:4], ST[:, 0:2], AF.Ln, scale=1.0 / D, bias=EPS), reads=[bST], writes=[bST])
                P.op(ACT, A_(ST[:, 4:6], ST[:, 2:4], AF.Exp, scale=-0.5), reads=[bST], writes=[bST])
                for t in range(2):
                    P.op(DVE, TS_(HN[:, t, :], Xt[:, t, :], ST[:, 4 + t:5 + t], None, ALU.mult), reads=[bXt, bST], writes=[bHN])
                for rnd in range(2):
                    for k4 in range(4):
                        kc = rnd * 4 + k4
                        for t in range(2):
                            P.op(PE, TR_(PT1[:, k4, t * 128:(t + 1) * 128], HN[:, t, kc * 128:(kc + 1) * 128], IDB[:]),
                                 reads=[bHN], writes=[bPT1])
                    for k4 in range(4):
                        kc = rnd * 4 + k4
                        P.op(DVE, TS_(HT[:, kc, :], PT1[:, k4, :], Gm[:, kc:kc + 1], SHm[:, kc:kc + 1], ALU.mult, ALU.add),
                             reads=[bPT1, bMOD], writes=[bHT])

            def headnorm(src_ps, bsrc, Gb, dst, bdst):
                P.op(ACT, A_(SQ[:], src_ps, AF.Square), reads=[bsrc], writes=[bSQ])
                P.op(DVE, lambda e: e.tensor_reduce(out=SK[:, 0:8], in_=SQ[:].rearrange("p (h d) -> p h d", h=8), axis=AX.X, op=ALU.add),
                     reads=[bSQ], writes=[bSK])
                P.op(ACT, A_(SK[:, 8:16], SK[:, 0:8], AF.Ln, scale=1.0 / 64, bias=EPS), reads=[bSK], writes=[bSK])
                P.op(ACT, A_(SK[:, 16:24], SK[:, 8:16], AF.Exp, scale=-0.5), reads=[bSK], writes=[bSK])
                P.op(DVE, TT_(TK[:].rearrange("p (h d) -> p h d", h=8), src_ps.rearrange("p (h d) -> p h d", h=8),
                              SK[:, 16:24].unsqueeze(2).to_broadcast([128, 8, 64]), ALU.mult), reads=[bsrc, bSK], writes=[bTK])
                P.op(DVE, TT_(dst, TK[:], Gb[:], ALU.mult), reads=[bTK, bPBT], writes=[bdst])

            pkc = [0]

            def chain1(bi):
                own = bi >= OWN0
                qi = bi - OWN0
                Xt, bXt = X[bi % 2], bX[bi % 2]
                HT, bHT = HTd[bi % 2], bHTd[bi % 2]
                norm_T(Xt, bXt, G1, SH1, HT, bHT)
                for t in range(2):
                    tile_idx = bi * 2 + t
                    bk = pkc[0] % 2; pkc[0] += 1
                    for kc in range(8):
                        P.op(PE, MM_(PK[bk][:], HT[:, kc, t * 128:(t + 1) * 128], WIN[:, kc, 512:1024], kc == 0, kc == 7),
                             reads=[bWIN, bHT], writes=[bPK[bk]])
                    headnorm(PK[bk][:], bPK[bk], GKB, KN[:], bKN)
                    for pr in range(4):
                        P.op(PE, TR_(PM0[:, pr, t * 128:(t + 1) * 128], KN[:, pr * 128:(pr + 1) * 128], IDB[:]), reads=[bKN], writes=[bPM0])
                    P.op(DVE, CP_(KTt[:, :, t * 128:(t + 1) * 128], PM0[:, :, t * 128:(t + 1) * 128]), reads=[bPM0], writes=[bKTt])
                    bv = pkc[0] % 2; pkc[0] += 1
                    for kc in range(8):
                        P.op(PE, MM_(PK[bv][:], HT[:, kc, t * 128:(t + 1) * 128], WIN[:, kc, 1024:1536], kc == 0, kc == 7),
                             reads=[bWIN, bHT], writes=[bPK[bv]])
                    P.op(ACT, A_(VT[t][:], PK[bv][:], AF.Copy), reads=[bPK[bv]], writes=[bVT[t]])
                    P.dma(POOL, V_s[:, tile_idx, :, :].rearrange("h p d -> p h d"), VT[t][:].rearrange("p (h d) -> p h d", h=8),
                          reads=[bVT[t]])
                P.dma(POOL, KT_s[:, bi * 256:(bi + 1) * 256].rearrange("(pr p) n -> p pr n", p=128), KTt[:], reads=[bKTt])
                P.op(DVE, lambda e: e.tensor_reduce(out=KMS[:], in_=KTt[:], axis=AX.X, op=ALU.add), reads=[bKTt], writes=[bKMS])
                if own:
                    for t in range(2):
                        bq = pkc[0] % 2; pkc[0] += 1
                        for kc in range(8):
                            P.op(PE, MM_(PK[bq][:], HT[:, kc, t * 128:(t + 1) * 128], WIN[:, kc, 0:512], kc == 0, kc == 7),
                                 reads=[bWIN, bHT], writes=[bPK[bq]])
                        headnorm(PK[bq][:], bPK[bq], GQB, KN[:], bKN)
                        for pr in range(4):
                            P.op(PE, TR_(PM0[:, pr, t * 128:(t + 1) * 128], KN[:, pr * 128:(pr + 1) * 128], IDB[:]), reads=[bKN], writes=[bPM0])
                        P.op(DVE, CP_(QTt[:, :, t * 128:(t + 1) * 128], PM0[:, :, t * 128:(t + 1) * 128]), reads=[bPM0], writes=[bQTt])
                        for pr in range(4):
                            P.op(PE, MM_(PM1[:, pr * 64:(pr + 1) * 64], QTt[:, pr, t * 128:(t + 1) * 128],
                                         KM[:, pr, :, :].rearrange("p s j -> p (s j)"), True, True), reads=[bQTt, bKM], writes=[bPM1])
                        P.op(DVE, TT_(GB[:], PM1[:, 0:256].rearrange("p (h j) -> p h j", h=8),
                                      PBT[:, qi, :].unsqueeze(1).to_broadcast([128, 8, 32]), ALU.add), reads=[bPM1, bPBT], writes=[bGB])
                        for h in range(8):
                            P.op(DVE, lambda e, h=h: e.max(out=M8[:, h, :], in_=GB[:, h, :]), reads=[bGB], writes=[bM8])
                        P.op(DVE, TS_(THR[:], M8[:, :, 2], -1e29, None, ALU.max), reads=[bM8], writes=[bTHR])
                        P.op(DVE, TT_(NS[:].rearrange("p (h j) -> p h j", h=8), GB[:],
                                      THR[:, :].unsqueeze(2).to_broadcast([128, 8, 32]), ALU.is_lt), reads=[bGB, bTHR], writes=[bNS])
                        for g in range(2):
                            P.op(PE, TR_(PT1[:, g, t * 128:(t + 1) * 128], NS[:, g * 128:(g + 1) * 128], IDB[:]), reads=[bNS], writes=[bPT1])
                        P.op(DVE, CP_(NSTt[:, :, t * 128:(t + 1) * 128], PT1[:, 0:2, t * 128:(t + 1) * 128]), reads=[bPT1], writes=[bNSTt])
                    P.dma(POOL, QT_s[:, qi * 256:(qi + 1) * 256].rearrange("(pr p) n -> p pr n", p=128), QTt[:], reads=[bQTt])
                    P.dma(POOL, NS_s[:, qi * 256:(qi + 1) * 256].rearrange("(g p) n -> p g n", p=128), NSTt[:], reads=[bNSTt])
                P.op(DVE, TS_(KM[0:64, :, 0, bi], KMS[0:64, :], 1.0 / 256, None, ALU.mult), reads=[bKMS], writes=[bKM])
                P.op(DVE, TS_(KM[64:128, :, 1, bi], KMS[64:128, :], 1.0 / 256, None, ALU.mult), reads=[bKMS], writes=[bKM])

            def chain2(bi):
                own = bi >= OWN0
                qi = bi - OWN0
                HT, bHT = HTd[bi % 2], bHTd[bi % 2]
                for c in range(4):
                    for kc in range(8):
                        P.op(PE, MM_(PF[c // 2][:, c % 2, :], WIN[:, kc, 1536 + c * 128:1536 + (c + 1) * 128], HT[:, kc, :], kc == 0, kc == 7),
                             reads=[bWIN, bHT], writes=[bPF[c // 2]])
                if bi == OWN0 + 1:
                    P.op(DVE, TS_(XR[:, :, 0:3], XR[:, :, 0:3], FLG[:, 0:1], None, ALU.mult), reads=[bXR, bCONST, bC0], writes=[bXR])
                    P.op(DVE, TS_(HS[:], HS[:], FLG[:, 0:1], None, ALU.mult), reads=[bHS, bC0], writes=[bHS])
                P.op(ACT, A_(XR[:, :, 3:259], PF4[:], AF.Copy), reads=[bPF[0], bPF[1]], writes=[bXR])
                for c in range(4):
                    P.op(ACT, A_(XC[:, c, :], XR[:, c, 0:256], AF.Identity, scale=CW[:, c, 0:1], bias=CB[:, c:c + 1]),
                         reads=[bXR, bC0], writes=[bXC])
                    for j in range(1, 4):
                        P.op(DVE, STT_(XC[:, c, :], XR[:, c, j:j + 256], CW[:, c, j:j + 1], XC[:, c, :], ALU.mult, ALU.add),
                             reads=[bXR, bXC, bC0], writes=[bXC])
                P.op(DVE, CP_(XR[:, :, 0:3], XR[:, :, 256:259]), reads=[bXR], writes=[bXR])
                P.op(DVE, CP_(XCb[:], XC[:]), reads=[bXC], writes=[bXCb])
                for (Wb, NBb, dst, bdst) in ((WAB, NBA, R, bR), (WXB, NBX, I, bI)):
                    for c in range(4):
                        P.op(PE, MM_(PF[c // 2][:, c % 2, :], Wb[:, c, :], XCb[:, c, :], True, True),
                             reads=[bCONST, bXCb], writes=[bPF[c // 2]])
                    for c in range(4):
                        P.op(ACT, A_(dst[:, c, :], PF[c // 2][:, c % 2, :], AF.Exp, scale=-1.0, bias=NBb[:, c:c + 1]),
                             reads=[bPF[c // 2], bCONST], writes=[bdst])
                    P.op(ACT, A_(dst[:], dst[:], AF.Ln, bias=1.0), reads=[bdst], writes=[bdst])
                    P.op(ACT, A_(dst[:], dst[:], AF.Exp, scale=-1.0), reads=[bdst], writes=[bdst])
                for c in range(4):
                    P.op(ACT, A_(AA[:, c, :], R[:, c, :], AF.Exp, scale=CL[:, c:c + 1]), reads=[bR, bCONST], writes=[bAA])
                    P.op(ACT, A_(OM[:, c, :], R[:, c, :], AF.Exp, scale=CL2[:, c:c + 1]), reads=[bR, bCONST], writes=[bOM])
                P.op(DVE, TS_(OM[:], OM[:], -1.0, 1.0, ALU.mult, ALU.add), reads=[bOM], writes=[bOM])
                P.op(ACT, A_(OM[:], OM[:], AF.Ln), reads=[bOM], writes=[bOM])
                P.op(ACT, A_(OM[:], OM[:], AF.Exp, scale=0.5), reads=[bOM], writes=[bOM])
                P.op(DVE, TT_(OM[:], OM[:], I[:], ALU.mult), reads=[bOM, bI], writes=[bOM])
                P.op(DVE, TT_(OM[:], OM[:], XC[:], ALU.mult), reads=[bOM, bXC], writes=[bOM])
                for c in range(4):
                    P.op(DVE, lambda e, c=c: e.tensor_tensor_scan(out=H[:, c, :], data0=AA[:, c, :], data1=OM[:, c, :],
                                                                   initial=HS[:, c:c + 1], op0=ALU.mult, op1=ALU.add),
                         reads=[bAA, bOM, bHS], writes=[bH])
                P.op(DVE, CP_(HS[:], H[:, :, 255]), reads=[bH], writes=[bHS])
                if own:
                    for c in range(4):
                        for kc in range(8):
                            P.op(PE, MM_(PF[c // 2][:, c % 2, :], WIN[:, kc, 2048 + c * 128:2048 + (c + 1) * 128], HT[:, kc, :], kc == 0, kc == 7),
                                 reads=[bWIN, bHT], writes=[bPF[c // 2]])
                    for c2 in range(2):
                        cs = slice(2 * c2, 2 * c2 + 2)
                        P.op(ACT, A_(T1[:, cs, :], PF[c2][:], AF.Square), reads=[bPF[c2]], writes=[bT1])
                        P.op(DVE, TS_(T1[:, cs, :], T1[:, cs, :], 0.044715, 1.0, ALU.mult, ALU.add), reads=[bT1], writes=[bT1])
                        P.op(DVE, TT_(T1[:, cs, :], T1[:, cs, :], PF[c2][:], ALU.mult), reads=[bT1, bPF[c2]], writes=[bT1])
                        P.op(ACT, A_(T1[:, cs, :], T1[:, cs, :], AF.Exp, scale=-1.5957691216), reads=[bT1], writes=[bT1])
                        P.op(ACT, A_(T1[:, cs, :], T1[:, cs, :], AF.Ln, bias=1.0), reads=[bT1], writes=[bT1])
                        P.op(ACT, A_(T1[:, cs, :], T1[:, cs, :], AF.Exp, scale=-1.0), reads=[bT1], writes=[bT1])
                        P.op(DVE, TT_(T1[:, cs, :], T1[:, cs, :], PF[c2][:], ALU.mult), reads=[bT1, bPF[c2]], writes=[bT1])
                    P.op(DVE, TT_(T1[:], T1[:], H[:], ALU.mult), reads=[bT1, bH], writes=[bT1])
                    P.op(ACT, A_(YSQ[:], T1[:], AF.Square), reads=[bT1], writes=[bYSQ])
                    for c in range(4):
                        P.op(PE, MM_(PRS[:, 0:256], ONB[:], YSQ[:, c, :], c == 0, c == 3), reads=[bYSQ, bC0], writes=[bPRS])
                    P.op(ACT, A_(RS[:], PRS[:, 0:256], AF.Ln, scale=1.0 / 512, bias=EPS), reads=[bPRS], writes=[bRS])
                    P.op(ACT, A_(RS[:], RS[:], AF.Exp, scale=-0.5), reads=[bRS], writes=[bRS])
                    P.op(DVE, TT_(LRN[:], T1[:], RS[:, :].unsqueeze(1).to_broadcast([128, 4, 256]), ALU.mult), reads=[bT1, bRS], writes=[bLRN])
                    P.dma(POOL, LR_s[:, qi * 256:(qi + 1) * 256].rearrange("(c p) n -> p c n", p=128), LRN[:], reads=[bLRN])

            load_x(0)
            load_x(1)
            for bi in range(NB):
                chain1(bi)
                chain2(bi)
                if bi + 2 < NB:
                    load_x(bi + 2)
            P.emit_segment()
        sWI.close()

        sW = top.enter_context(contextlib.ExitStack())
        sbW, _ = mk(sW)
        WUP = sbW("WUP", [128, 8, 2 * DFF], BF16); bWUP = Buf()
        with contextlib.ExitStack() as s2:
            if stop_after < 2:
                return nc
            sb, ps = mk(s2)
            WSW = [sb("WSW%d" % i, [128, 512]) for i in range(4)]; bWSW = [Buf() for _ in range(4)]
            KX = [sb("KX%d" % i, [96, S], BF16) for i in range(2)]; bKX = [Buf(), Buf()]
            QX = [sb("QX%d" % i, [96, NQ], BF16) for i in range(2)]; bQX = [Buf(), Buf()]
            VX = [sb("VX%d" % i, [128, 64, 65], BF16) for i in range(2)]; bVX = [Buf(), Buf()]
            PP = [sb("PP%d" % i, [128, 2, 256], BF16) for i in range(3)]; bPP = [Buf() for _ in range(3)]
            AO = [sb("AO%d" % i, [128, 2, 64]) for i in range(2)]; bAO = [Buf(), Buf()]
            RD = sb("RD", [128, 4]); bRD = Buf()
            SB_ = [ps("SB%d" % i, [128, 2, 256]) for i in range(3)]; bSB = [Buf() for _ in range(3)]
            OB = [[ps("OB%d_%d" % (i, q), [128, 512]) for q in range(2)] for i in range(2)]
            bOB = [[Buf(), Buf()], [Buf(), Buf()]]
            for i in range(2):
                P.dma(SP, KX[i][64:96, :], oh, writes=[bKX[i]])
                P.op(DVE, lambda e, i=i: e.memset(VX[i][:, :, 64:65], 1.0), writes=[bVX[i]])

            def load_head(h):
                i = h % 2
                P.dma(SP, KX[i][0:64, :], KT_s[h * 64:(h + 1) * 64, :], writes=[bKX[i]])
                P.dma(SP, QX[i][0:64, :], QT_s[h * 64:(h + 1) * 64, :], writes=[bQX[i]])
                P.dma(SP, QX[i][64:96, :], NS_s[h * 32:(h + 1) * 32, :], writes=[bQX[i]])
                P.dma(SP, VX[i][:, :, 0:64], V_s[h].rearrange("t p d -> p t d"), writes=[bVX[i]])

            its = []
            oi = 0
            for h in range(8):
                for qi in range(NOWN):
                    bi = OWN0 + qi
                    ob = oi % 2; oi += 1
                    for j in range(bi + 1):
                        its.append((h, qi, bi, ob, j))

            def emit_qk(n):
                h, qi, bi, ob, j = its[n]
                hb = h % 2
                sbk = n % 3
                qc = slice(qi * 256, (qi + 1) * 256)
                diag = (j == bi)
                for kh in range(2):
                    kc_ = slice(j * 256 + kh * 128, j * 256 + (kh + 1) * 128)
                    if not diag:
                        P.op(PE, MM_(SB_[sbk][:, kh, :], KX[hb][0:96, kc_], QX[hb][0:96, qc], True, True),
                             reads=[bKX[hb], bQX[hb]], writes=[bSB[sbk]])
                    else:
                        P.op(PE, MM_(SB_[sbk][:, kh, :], KX[hb][0:64, kc_], QX[hb][0:64, qc], True, False),
                             reads=[bKX[hb], bQX[hb]], writes=[bSB[sbk]])
                        P.op(PE, MM_(SB_[sbk][:, kh, :], IDB[:], MKD[:, kh, :], False, True),
                             reads=[bC0], writes=[bSB[sbk]])

            def emit_rest(n):
                h, qi, bi, ob, j = its[n]
                hb = h % 2
                sbk = n % 3
                nj = bi + 1
                P.op(ACT, A_(PP[sbk][:], SB_[sbk][:], AF.Exp, scale=0.125), reads=[bSB[sbk]], writes=[bPP[sbk]])
                if n + 2 < len(its):
                    emit_qk(n + 2)
                for kh in range(2):
                    for qh in range(2):
                        P.op(PE, MM_(OB[ob][qh][:, 0:65], PP[sbk][:, kh, qh * 128:(qh + 1) * 128], VX[hb][:, j * 2 + kh, :],
                                     (j == 0 and kh == 0), (j == nj - 1 and kh == 1)),
                             reads=[bPP[sbk], bVX[hb]], writes=[bOB[ob][qh]])
                if j == nj - 1:
                    for qh in range(2):
                        P.op(DVE, lambda e, ob=ob, qh=qh: e.reciprocal(RD[:, ob * 2 + qh:ob * 2 + qh + 1], OB[ob][qh][:, 64:65]),
                             reads=[bOB[ob][qh]], writes=[bRD])
                        P.op(DVE, TS_(AO[ob][:, qh, :], OB[ob][qh][:, 0:64], RD[:, ob * 2 + qh:ob * 2 + qh + 1], None, ALU.mult),
                             reads=[bOB[ob][qh], bRD], writes=[bAO[ob]])
                    P.dma(POOL, AT_s[qi * 256:(qi + 1) * 256, h * 64:(h + 1) * 64].rearrange("(qh p) d -> p qh d", p=128), AO[ob][:],
                          reads=[bAO[ob]])

            load_head(0)
            load_head(1)
            def wup_piece(g):
                i = g % 4
                P.dma(SP, WSW[i][:].rearrange("p (kc n) -> p kc n", kc=8),
                      w_up[:, g * 64:(g + 1) * 64].rearrange("(kc p) n -> p kc n", p=128), writes=[bWSW[i]])
                P.op(DVE, CP_(WUP[:, :, g * 64:(g + 1) * 64], WSW[i][:].rearrange("p (kc n) -> p kc n", kc=8)),
                     reads=[bWSW[i]], writes=[bWUP])
            wnext = [0]
            emit_qk(0)
            emit_qk(1)
            for n in range(len(its)):
                h, qi, bi, ob, j = its[n]
                if n > 0 and its[n - 1][0] != h and h + 1 < 8:
                    load_head(h + 1)
                if n % 24 == 12 and wnext[0] < 88:
                    wup_piece(wnext[0]); wnext[0] += 1
                emit_rest(n)
            while wnext[0] < 88:
                wup_piece(wnext[0]); wnext[0] += 1
            P.sched = False
            P.emit_segment()
            P.sched = True

        with contextlib.ExitStack() as s3:
            if stop_after < 3:
                return nc
            sb, ps = mk(s3)
            WDN = sb("WDN", [128, 22, D], BF16); bWDN = Buf()
            XD = [sb("XT3_%d" % i, [128, 2, D]) for i in range(2)]; bXD = [Buf(), Buf()]
            AT = sb("AT", [128, 512]); bAT = Buf()
            ST = sb("ST3", [128, 8]); bST = Buf()
            AN = sb("AN", [128, 512], BF16); bAN = Buf()
            HTD = [sb("HTD%d" % i, [128, 8, 256], BF16) for i in range(2)]; bHTD = [Buf(), Buf()]
            HN = sb("HN3", [128, D], BF16); bHN = Buf()
            UB = [sb("UB%d" % i, [128, 2, 258]) for i in range(3)]; bUB = [Buf() for _ in range(3)]; bUBh = [Buf() for _ in range(3)]
            ACC = [sb("ACC%d" % i, [128, 2, 256]) for i in range(3)]; bACC = [Buf() for _ in range(3)]
            WS3 = [ACC[i][:].rearrange("p a n -> p (a n)") for i in range(3)]; bWS3 = bACC
            UH = sb("UH", [128, 44, 2]); bUH = Buf()
            ACTT = sb("ACTT", [128, 22, 256], BF16); bACTT = Buf(); bACTTm = [Buf() for _ in range(22)]
            PY = [ps("PY%d" % i, [128, 512]) for i in range(4)]; bPY = [Buf() for _ in range(4)]
            PT1 = ps("PT3", [128, 4, 256], BF16); bPT1 = Buf()
            PU = [ps("PU%d" % i, [128, 2, 256]) for i in range(3)]; bPU = [Buf() for _ in range(3)]

            ACTF = ACTT[:].rearrange("p m n -> p (m n)").bitcast(F32)
            STG = [(WS3[0], bACC[0]), (WS3[1], bACC[1]), (WS3[2], bACC[2])] + [(ACTF[:, k * 512:(k + 1) * 512], Buf()) for k in range(5)]
            wi = 0
            for m in range(22):
                for nh in range(2):
                    stg, bstg = STG[wi % len(STG)]; wi += 1
                    P.dma(SP, stg, w_down[m * 128:(m + 1) * 128, nh * 512:(nh + 1) * 512], writes=[bstg])
                    P.op(DVE, TT_(WDN[:, m, nh * 512:(nh + 1) * 512], stg, G2B[:, nh * 512:(nh + 1) * 512], ALU.mult),
                         reads=[bstg, bG2B], writes=[bWDN])
            P.op(DVE, _w(lambda e: e.memset(UH[:], 0.0), 88), reads=[b for _, b in STG[3:]], writes=[bUH] + bACTTm)

            upc = [0]

            def front(qi):
                bi = OWN0 + qi
                Xt, bXt = XD[qi % 2], bXD[qi % 2]
                MIXT = HT = HTD[qi % 2]
                bMIXL = bMIXA = bHT = bHTD[qi % 2]
                P.dma(SP, Xt[:], xs[bi * 256:(bi + 1) * 256, :].rearrange("(t p) d -> p t d", p=128), writes=[bXt])
                P.dma(SP, MIXT[:, 0:4, :], LR_s[:, qi * 256:(qi + 1) * 256].rearrange("(c p) n -> p c n", p=128), writes=[bMIXL])
                for t in range(2):
                    P.dma(SP, AT[:], AT_s[qi * 256 + t * 128:qi * 256 + (t + 1) * 128, :], writes=[bAT])
                    P.op(ACT, A_(AN[:], AT[:], AF.Square, accum_out=ST[:, 0:1]), reads=[bAT], writes=[bAN, bST])
                    P.op(ACT, A_(ST[:, 2:3], ST[:, 0:1], AF.Ln, scale=1.0 / 512, bias=EPS), reads=[bST], writes=[bST])
                    P.op(ACT, A_(ST[:, 4:5], ST[:, 2:3], AF.Exp, scale=-0.5), reads=[bST], writes=[bST])
                    P.op(DVE, TS_(AN[:], AT[:], ST[:, 4:5], None, ALU.mult), reads=[bAT, bST], writes=[bAN])
                    for c in range(4):
                        P.op(PE, TR_(PT1[:, c, t * 128:(t + 1) * 128], AN[:, c * 128:(c + 1) * 128], IDB[:]), reads=[bAN], writes=[bPT1])
                P.op(DVE, CP_(MIXT[:, 4:8, :], PT1[:]), reads=[bPT1], writes=[bMIXA])
                for t in range(2):
                    for nh in range(2):
                        pb = t * 2 + nh
                        for kc in range(8):
                            P.op(PE, MM_(PY[pb][:], MIXT[:, kc, t * 128:(t + 1) * 128], WOUT[:, kc, nh * 512:(nh + 1) * 512], kc == 0, kc == 7),
                                 reads=[bMIXL, bMIXA, bWOUT], writes=[bPY[pb]])
                        P.op(DVE, TT_(Xt[:, t, nh * 512:(nh + 1) * 512], PY[pb][:], Xt[:, t, nh * 512:(nh + 1) * 512], ALU.add),
                             reads=[bPY[pb], bXt], writes=[bXt])
                if debug:
                    P.dma(POOL, X1_s[qi * 256:(qi + 1) * 256, :].rearrange("(t p) d -> p t d", p=128), Xt[:], reads=[bXt])
                for t in range(2):
                    P.op(ACT, A_(HN[:], Xt[:, t, :], AF.Square, accum_out=ST[:, t:t + 1]), reads=[bXt], writes=[bHN, bST])
                P.op(ACT, A_(ST[:, 2:4], ST[:, 0:2], AF.Ln, scale=1.0 / D, bias=EPS), reads=[bST], writes=[bST])
                P.op(ACT, A_(ST[:, 4:6], ST[:, 2:4], AF.Exp, scale=-0.5), reads=[bST], writes=[bST])
                HT2 = HT[:].rearrange("p k (t n) -> p k t n", t=2)
                for t in range(2):
                    P.op(DVE, TS_(HN[:], Xt[:, t, :], ST[:, 4 + t:5 + t], None, ALU.mult), reads=[bXt, bST], writes=[bHN])
                    for rnd in range(2):
                        for k4 in range(4):
                            kc = rnd * 4 + k4
                            P.op(PE, TR_(PT1[:, k4, 0:128], HN[:, kc * 128:(kc + 1) * 128], IDB[:]), reads=[bHN], writes=[bPT1])
                        for k4 in range(4):
                            kc = rnd * 4 + k4
                            P.op(DVE, TS_(HT[:, kc, t * 128:(t + 1) * 128], PT1[:, k4, 0:128], G2[:, kc:kc + 1], SH2[:, kc:kc + 1], ALU.mult, ALU.add),
                                 reads=[bPT1, bMOD], writes=[bHT])

            def rest(qi):
                bi = OWN0 + qi
                Xt, bXt = XD[qi % 2], bXD[qi % 2]
                HT = HTD[qi % 2]
                bHT = bHTD[qi % 2]
                if qi == 1:
                    P.op(DVE, TS_(UH[:], UH[:], FLG[:, 0:1], None, ALU.mult), reads=[bUH, bC0], writes=[bUH])
                for m in range(22):
                    u = upc[0] % 3; upc[0] += 1
                    bPU_, bUB_, bACC_ = bPU, bUB, bACC
                    for half, mm in ((0, m), (1, m + 22)):
                        for kc in range(8):
                            P.op(PE, MM_(PU[u][:, half, :], WUP[:, kc, mm * 128:(mm + 1) * 128], HT[:, kc, :], kc == 0, kc == 7),
                                 reads=[bWUP, bHT], writes=[bPU_[u]])
                    if qi > 0:
                        for half, mm in ((0, m), (1, m + 22)):
                            P.op(POOL, CP_(UB[u][:, half, 0:2], UH[:, mm, :]), reads=[bUH], writes=[bUBh[u]])
                    P.op(ACT, A_(UB[u][:, :, 2:258], PU[u][:], AF.Copy), reads=[bPU_[u]], writes=[bUB_[u]])
                    for half, mm in ((0, m), (1, m + 22)):
                        P.op(POOL, CP_(UH[:, mm, :], UB[u][:, half, 256:258]), reads=[bUB_[u]], writes=[bUH])
                    if qi == 0:
                        continue
                    for half, mm in ((0, m), (1, m + 22)):
                        P.op(ACT, A_(ACC[u][:, half, :], UB[u][:, half, 0:256], AF.Identity, scale=FCW[:, mm, 0:1], bias=FCB[:, mm:mm + 1]),
                             reads=[bUB_[u], bUBh[u], bC0], writes=[bACC_[u]])
                        for jj in (1, 2):
                            P.op(DVE, STT_(ACC[u][:, half, :], UB[u][:, half, jj:jj + 256], FCW[:, mm, jj:jj + 1], ACC[u][:, half, :], ALU.mult, ALU.add),
                                 reads=[bUB_[u], bUBh[u], bACC_[u], bC0], writes=[bACC_[u]])
                    P.op(ACT, A_(ACC[u][:, 0, :], ACC[u][:, 0, :], AF.Silu), reads=[bACC_[u]], writes=[bACC_[u]])
                    P.op(POOL, TT_(ACTT[:, m, :], ACC[u][:, 0, :], ACC[u][:, 1, :], ALU.mult), reads=[bACC_[u]], writes=[bACTTm[m]])
                if qi == 0:
                    return
                for m in range(22):
                    for t in range(2):
                        for nh in range(2):
                            pb = t * 2 + nh
                            P.op(PE, MM_(PY[pb][:], ACTT[:, m, t * 128:(t + 1) * 128], WDN[:, m, nh * 512:(nh + 1) * 512], m == 0, m == 21),
                                 reads=[bACTTm[m], bWDN], writes=[bPY[pb]])
                for t in range(2):
                    for nh in range(2):
                        pb = t * 2 + nh
                        P.op(DVE, TT_(Xt[:, t, nh * 512:(nh + 1) * 512], PY[pb][:], Xt[:, t, nh * 512:(nh + 1) * 512], ALU.add),
                             reads=[bPY[pb], bXt], writes=[bXt])
                P.dma(POOL, out[(qi - 1) * 256:qi * 256, :].rearrange("(t p) d -> p t d", p=128), Xt[:], reads=[bXt])

            front(0)
            for qi in range(NOWN):
                if qi + 1 < NOWN:
                    front(qi + 1)
                rest(qi)
            P.emit_segment(last=True)
            nc._est_log = P.est_log
    return nc


def make_in_maps(x, c, w_ada, b_ada, norm1_g, w_in, q_norm_g, k_norm_g, lru_conv_w, lru_conv_b,
                 lru_wa, lru_ba, lru_wx, lru_bx, lru_lambda, lru_out_g, attn_out_g, w_out,
                 norm2_g, w_up, ffn_conv_w, ffn_conv_b, w_down):
    f = lambda a: np.ascontiguousarray(np.asarray(a, dtype=np.float32))
    x = f(x); c = f(c)

    def fm(v, nch):
        return f(np.asarray(v, np.float32).reshape(nch, 128).T)

    def bd(w):
        w = np.asarray(w, np.float32)
        o = np.zeros((128, 4, 128), np.float32)
        for ch in range(4):
            for s_ in range(2):
                o[s_ * 64:(s_ + 1) * 64, ch, s_ * 64:(s_ + 1) * 64] = w[ch * 2 + s_]
        return o

    shared = {
        "w_ada": f(w_ada[0]), "b_ada": f(b_ada[0]).reshape(1, -1),
        "n1g": fm(norm1_g[0], 8), "n2g": fm(norm2_g[0], 8),
        "mixg": fm(np.concatenate([np.asarray(lru_out_g[0]), np.asarray(attn_out_g[0])]), 8),
        "w_in": f(w_in[0]),
        "gqb": f(np.broadcast_to(np.tile(np.asarray(q_norm_g[0], np.float32), 8), (128, 512))),
        "gkb": f(np.broadcast_to(np.tile(np.asarray(k_norm_g[0], np.float32), 8), (128, 512))),
        "cw": f(np.asarray(lru_conv_w[0], np.float32).reshape(4, 4, 128).transpose(2, 1, 0)),
        "cb": fm(lru_conv_b[0], 4),
        "wabd": bd(lru_wa[0]), "wxbd": bd(lru_wx[0]),
        "lba": fm(lru_ba[0], 4), "lbx": fm(lru_bx[0], 4), "lam": fm(lru_lambda[0], 4),
        "w_out": f(w_out[0]), "w_up": f(w_up[0]),
        "fcw": f(np.asarray(ffn_conv_w[0], np.float32).reshape(3, 44, 128).transpose(2, 1, 0)),
        "fcb": fm(ffn_conv_b[0], 44),
        "w_down": f(w_down[0]),
        "identb": np.eye(128, dtype=np.float32).astype(ml_dtypes.bfloat16),
        "identf": np.eye(128, dtype=np.float32),
        "onesb": np.ones((128, 128), np.float32).astype(ml_dtypes.bfloat16),
    }
    ohm = np.zeros((32, S), np.float32)
    for j in range(32):
        ohm[j, j * 256:(j + 1) * 256] = NEG
    shared["oh"] = ohm.astype(ml_dtypes.bfloat16)
    kk = np.arange(128)[:, None]
    qq = np.arange(128)[None, :]
    tri = np.where(kk <= qq, 0.0, NEG).astype(np.float32)
    md = np.zeros((128, 2, 256), np.float32)
    md[:, 0, 0:128] = tri
    md[:, 1, 0:128] = NEG
    md[:, 1, 128:256] = tri
    shared["maskd"] = md.astype(ml_dtypes.bfloat16)
    in_maps = []
    for core in range(8):
        b, half = core // 2, core % 2
        if half == 1:
            xsv = x[b]
        else:
            xsv = np.concatenate([np.zeros((4096, D), np.float32), x[b, :4096]], axis=0)
        pb = np.full((NOWN, 32), -1e30, np.float32)
        for qi in range(NOWN):
            bi = OWN0 + qi
            lo = 16 if half == 0 else 0
            if bi > lo:
                pb[qi, lo:bi] = 0.0
        m = dict(shared)
        m["xs"] = f(xsv)
        m["cT"] = fm(c[b], 8)
        m["flag"] = np.full((128, 1), float(half), np.float32)
        m["pastb"] = f(np.broadcast_to(pb[None], (128, NOWN, 32)))
        in_maps.append(m)
    return in_maps


_NC = {}


def kernel(**inputs):
    in_maps = make_in_maps(**inputs)
    if "nc" not in _NC:
        _NC["nc"] = build(False)
    res = run_bass_kernel_spmd(_NC["nc"], in_maps, core_ids=list(range(8)))
    outp = np.zeros((4, S, D), np.float32)
    for core in range(8):
        b, half = core // 2, core % 2
        outp[b, half * 4096:(half + 1) * 4096] = res.results[core]["out"]
    return outp
```

```python
import contextlib
import numpy as np
import ml_dtypes
import concourse.bass as bass
import concourse.mybir as mybir
from concourse.bass_utils import run_bass_kernel_spmd

F32 = mybir.dt.float32
BF16 = mybir.dt.bfloat16
AF = mybir.ActivationFunctionType
ALU = mybir.AluOpType
AX = mybir.AxisListType

PE, ACT, DVE, POOL, SP = "tensor", "scalar", "vector", "gpsimd", "sync"
COMPUTE = (PE, ACT, DVE, POOL)

D = 1024
S = 8192
NB = 32
OWN0 = 15
NOWN = NB - OWN0
NQ = NOWN * 256
DFF = 2816
NEG = -30000.0
EPS = 1e-6


class Buf:
    __slots__ = ("name", "w", "rs")

    def __init__(self, name=""):
        self.name = name
        self.w = None
        self.rs = []


class Op:
    __slots__ = ("eng", "fn", "deps", "dma", "sig", "sem", "need", "seg", "alld", "cost", "lat", "line")

    def __init__(self, eng, fn, dma, seg):
        self.eng = eng
        self.fn = fn
        self.dma = dma
        self.deps = set()
        self.sig = None
        self.sem = None
        self.need = False
        self.seg = seg


class Prog:
    def __init__(self, nc, stack, n_dma_sems=16):
        self.nc = nc
        self.n = n_dma_sems
        self.seg = 0
        self.ops = []
        self.esem = {e: stack.enter_context(nc.semaphore("s_" + e)) for e in COMPUTE}
        self.dsem = {q: [stack.enter_context(nc.semaphore("d_%s_%d" % (q, i))) for i in range(n_dma_sems)]
                     for q in (SP, POOL)}
        self.ecnt = {e: 0 for e in COMPUTE}
        self.dcnt = {q: [0] * n_dma_sems for q in self.dsem}
        self.dlast = {q: [None] * n_dma_sems for q in self.dsem}
        self.dnum = {q: 0 for q in self.dsem}
        self.finals = []
        self.rec = None
        self.sched = True
        self.est_ns = 0.0
        self.est_log = []

    def chain(self, fn):
        self.rec = []
        fn()
        lst, self.rec = self.rec, None
        return lst

    def merge(self, *chains):
        items = []
        for ci, ch in enumerate(chains):
            n = len(ch)
            for i, it in enumerate(ch):
                items.append(((i + 0.5) / n, ci, i, it))
        items.sort(key=lambda x: (x[0], x[1], x[2]))
        for _, _, _, it in items:
            self.op(*it)

    def op(self, eng, fn, reads=(), writes=(), dma=False):
        if self.rec is not None:
            self.rec.append((eng, fn, tuple(reads), tuple(writes), dma))
            return None
        o = Op(eng, fn, dma, self.seg)
        for b in reads:
            if b.w is not None:
                o.deps.add(b.w)
        for b in writes:
            if b.w is not None:
                o.deps.add(b.w)
            for r in b.rs:
                o.deps.add(r)
        for b in reads:
            b.rs.append(o)
        for b in writes:
            b.w = o
            b.rs = []
        o.deps.discard(o)
        import sys as _sys
        fr = _sys._getframe(1)
        if fr.f_code.co_name in ('dma', 'merge'):
            fr = fr.f_back
        o.line = fr.f_lineno
        n = getattr(fn, "n", 256)
        k = getattr(fn, "k", 0)
        if dma:
            o.cost = 60.0
            o.lat = 2200.0 + n / 60.0
        elif eng == PE:
            o.cost = 45.0 + 0.37 * n
            o.lat = o.cost + 200.0
        elif eng == ACT:
            o.cost = 130.0 + 0.83 * n + 120.0 * k
            o.lat = o.cost + 120.0
        elif eng == POOL:
            o.cost = 250.0 + 2.0 * n
            o.lat = o.cost + 150.0
        else:
            o.cost = 150.0 + (1.05 if k else 0.6) * n
            o.lat = o.cost + 120.0
        self.ops.append(o)
        return o

    def dma(self, eng, out, in_, reads=(), writes=()):
        nbytes = 1
        for d in out.shape:
            nbytes *= int(d)
        nbytes *= 2 if out.dtype == BF16 else 4
        f = lambda e: e.dma_start(out=out, in_=in_)
        f.n = nbytes
        return self.op(eng, f, reads, writes, dma=True)

    def schedule(self, ops):
        import bisect
        n = len(ops)
        pos = {id(o): i for i, o in enumerate(ops)}
        succ = [[] for _ in range(n)]
        indeg = [0] * n
        for i, o in enumerate(ops):
            for d in o.alld:
                j = pos.get(id(d))
                if j is None:
                    continue
                succ[j].append(i)
                indeg[i] += 1
        rt = [0.0] * n
        why = [None] * n
        stt = [0.0] * n
        lastop = {}
        avail = {e: [] for e in (PE, ACT, DVE, POOL, SP)}
        free = {e: 0.0 for e in avail}
        for i, o in enumerate(ops):
            if indeg[i] == 0:
                avail[o.eng].append(i)
        order = []
        WIN_ = 40
        done = 0
        while done < n:
            best = None
            for e, lst in avail.items():
                if not lst:
                    continue
                fe = free[e]
                cb = None
                for i in lst[:WIN_]:
                    st = rt[i] if rt[i] > fe else fe
                    if cb is None or st < cb[0]:
                        cb = (st, i)
                if best is None or cb < best[0:2]:
                    best = (cb[0], cb[1], e)
            st, i, e = best
            lst = avail[e]
            lst.pop(bisect.bisect_left(lst, i))
            o = ops[i]
            stt[i] = st
            if rt[i] < st and e in lastop:
                why[i] = ('eng', lastop[e])
            lastop[e] = i
            free[e] = st + o.cost
            fin = st + o.lat
            order.append(o)
            done += 1
            for k in succ[i]:
                if fin > rt[k]:
                    rt[k] = fin
                    if why[k] is None or why[k][0] == 'dep':
                        why[k] = ('dep', i)
                indeg[k] -= 1
                if indeg[k] == 0:
                    bisect.insort(avail[ops[k].eng], k)
        self.est_ns = max(free.values())
        busy = {}
        for o in ops:
            busy[o.eng] = busy.get(o.eng, 0.0) + o.cost
        self.est_busy = {k: round(v / 1e3) for k, v in busy.items()}
        return order

    def emit_segment(self, last=False):
        nc = self.nc
        seg = self.seg
        for o in self.ops:
            o.alld = [d for d in o.deps if d.seg == seg]
        ops = self.schedule(self.ops) if self.sched else self.ops
        self.est_log.append((seg, len(ops), round(self.est_ns / 1e3), self.est_busy))
        queues = {e: [] for e in (PE, ACT, DVE, POOL, SP)}
        for o in ops:
            queues[o.eng].append(o)
            keep = set()
            for d in o.deps:
                if d.seg != seg:
                    continue
                if d.dma:
                    keep.add(d)
                elif d.eng == o.eng and not o.dma:
                    if o.eng != PE:
                        keep.add(d)
                else:
                    keep.add(d)
            o.deps = keep
            for d in keep:
                d.need = True
        for e in COMPUTE:
            for o in reversed(queues[e]):
                if not o.dma:
                    o.need = True
                    break
        for o in ops:
            if o.dma:
                q = o.eng
                i = self.dnum[q] % self.n
                self.dnum[q] += 1
                prev = self.dlast[q][i]
                if prev is not None and prev.seg == seg:
                    o.deps.add(prev)
                self.dcnt[q][i] += 16
                o.sem = self.dsem[q][i]
                o.sig = self.dcnt[q][i]
                self.dlast[q][i] = o
            elif o.need:
                self.ecnt[o.eng] += 1
                o.sem = self.esem[o.eng]
                o.sig = self.ecnt[o.eng]
        with nc.Block() as block:
            def run(engname, eng):
                waited = {}
                for o in queues[engname]:
                    need = {}
                    for d in o.deps:
                        k = id(d.sem)
                        if waited.get(k, 0) >= d.sig:
                            continue
                        if k not in need or need[k][1] < d.sig:
                            need[k] = (d.sem, d.sig)
                    for k, (s, v) in need.items():
                        eng.wait_ge(s, v)
                        waited[k] = v
                    ins = o.fn(eng)
                    if o.sem is not None:
                        ins.then_inc(o.sem, 16 if o.dma else 1)
                for e in COMPUTE:
                    if e != engname and self.ecnt[e] > 0:
                        eng.wait_ge(self.esem[e], self.ecnt[e])
                for q in self.dsem:
                    for i in range(self.n):
                        if self.dcnt[q][i] > 0:
                            eng.wait_ge(self.dsem[q][i], self.dcnt[q][i])

            @block.sync
            def _(e):
                run(SP, e)

            @block.tensor
            def _(e):
                run(PE, e)

            @block.scalar
            def _(e):
                run(ACT, e)

            @block.vector
            def _(e):
                run(DVE, e)

            @block.gpsimd
            def _(e):
                run(POOL, e)
        self.ops = []
        self.seg += 1


def _fs(ap):
    n = 1
    for d in ap.shape[1:]:
        n *= int(d)
    return n


def _w(f, n):
    f.n = n
    return f


def A_(out, in_, func, **kw):
    f = _w(lambda e: e.activation(out=out, in_=in_, func=func, **kw), _fs(out))
    f.k = sum(1 for key in ("scale", "bias", "accum_out") if key in kw and not isinstance(kw[key], (int, float)))
    return f


def TS_(out, in0, s1, s2, op0, op1=None):
    if op1 is None:
        return _w(lambda e: e.tensor_scalar(out=out, in0=in0, scalar1=s1, scalar2=None, op0=op0), _fs(out))
    return _w(lambda e: e.tensor_scalar(out=out, in0=in0, scalar1=s1, scalar2=s2, op0=op0, op1=op1), _fs(out))


def TT_(out, in0, in1, op):
    f = _w(lambda e: e.tensor_tensor(out=out, in0=in0, in1=in1, op=op), _fs(out))
    f.k = 1
    return f


def STT_(out, in0, scalar, in1, op0, op1):
    f = _w(lambda e: e.scalar_tensor_tensor(out=out, in0=in0, scalar=scalar, in1=in1, op0=op0, op1=op1), _fs(out))
    f.k = 1
    return f


def CP_(out, in_):
    return _w(lambda e: e.tensor_copy(out, in_), _fs(out))


def MM_(out, lhsT, rhs, start, stop):
    return _w(lambda e: e.matmul(out, lhsT=lhsT, rhs=rhs, start=start, stop=stop), _fs(rhs))


def TR_(out, in_, ident):
    return _w(lambda e: e.transpose(out, in_, ident), 128)


def build(debug=False, stop_after=3):
    nc = bass.Bass("TRN2", target_bir_lowering=False)

    def din(name, shape, dt=F32):
        return nc.dram_tensor(name, list(shape), dt, kind="ExternalInput").ap()

    skind = "ExternalOutput" if debug else "Internal"

    def dscr(name, shape, dt):
        return nc.dram_tensor(name, list(shape), dt, kind=skind).ap()

    xs = din("xs", [S, D])
    cT = din("cT", [128, 8])
    w_ada = din("w_ada", [D, 6 * D])
    b_ada = din("b_ada", [1, 6 * D])
    n1g = din("n1g", [128, 8])
    n2g = din("n2g", [128, 8])
    mixg = din("mixg", [128, 8])
    w_in = din("w_in", [D, 2560])
    gqb = din("gqb", [128, 512])
    gkb = din("gkb", [128, 512])
    cw = din("cw", [128, 4, 4])
    cb = din("cb", [128, 4])
    wabd = din("wabd", [128, 4, 128])
    wxbd = din("wxbd", [128, 4, 128])
    lba = din("lba", [128, 4])
    lbx = din("lbx", [128, 4])
    lam = din("lam", [128, 4])
    w_out = din("w_out", [D, D])
    w_up = din("w_up", [D, 2 * DFF])
    fcw = din("fcw", [128, 44, 3])
    fcb = din("fcb", [128, 44])
    w_down = din("w_down", [DFF, D])
    flag = din("flag", [128, 1])
    pastb = din("pastb", [128, NOWN, 32])
    oh = din("oh", [32, S], BF16)
    identb = din("identb", [128, 128], BF16)
    identf = din("identf", [128, 128])
    onesb = din("onesb", [128, 128], BF16)
    maskd = din("maskd", [128, 2, 256], BF16)
    out = nc.dram_tensor("out", [4096, D], F32, kind="ExternalOutput").ap()

    KT_s = dscr("KT_s", [512, S], BF16)
    QT_s = dscr("QT_s", [512, NQ], BF16)
    NS_s = dscr("NS_s", [256, NQ], BF16)
    LR_s = dscr("LR_s", [512, NQ], BF16)
    V_s = dscr("V_s", [8, 64, 128, 64], BF16)
    AT_s = dscr("AT_s", [NQ, 512], F32)
    X1_s = dscr("X1_s", [NQ, D], F32) if debug else None

    with contextlib.ExitStack() as top:
        P = Prog(nc, top)

        def mk(stack):
            def sb(name, shape, dt=F32):
                return stack.enter_context(nc.sbuf_tensor(name, list(shape), dt))

            def ps(name, shape, dt=F32):
                return stack.enter_context(nc.psum_tensor(name, list(shape), dt))
            return sb, ps

        sbT, _ = mk(top)
        WOUT = sbT("WOUT", [128, 8, D], BF16); bWOUT = Buf()
        G2B = sbT("G2B", [128, D], BF16); bG2B = Buf()
        G1 = sbT("G1", [128, 8]); SH1 = sbT("SH1", [128, 8]); G2 = sbT("G2", [128, 8]); SH2 = sbT("SH2", [128, 8])
        bMOD = Buf()
        IDB = sbT("IDB", [128, 128], BF16); ONB = sbT("ONB", [128, 128], BF16); MKD = sbT("MKD", [128, 2, 256], BF16)
        FLG = sbT("FLG", [128, 1]); bCONST = Buf()
        CW = sbT("CW", [128, 4, 4]); CB = sbT("CB", [128, 4]); CL = sbT("CL", [128, 4]); CL2 = sbT("CL2", [128, 4])
        NBA = sbT("NBA", [128, 4]); NBX = sbT("NBX", [128, 4])
        WAB = sbT("WAB", [128, 4, 128], BF16); WXB = sbT("WXB", [128, 4, 128], BF16)
        FCW = sbT("FCW", [128, 44, 3]); FCB = sbT("FCB", [128, 44])

        sWI = contextlib.ExitStack()
        sbWI, _ = mk(sWI)
        WIN = sbWI("WIN", [128, 8, 2560], BF16); bWIN = Buf()
        with contextlib.ExitStack() as s0:
            sb, ps = mk(s0)
            WIS = [sb("WIS%d" % i, [128, 8, 320]) for i in range(2)]; bWIS = [Buf(), Buf()]
            CT = sb("CT", [128, 8]); CREP = sb("CREP", [128, 8, 128])
            WAS = [sb("WAS%d" % i, [128, 8, 512]) for i in range(3)]; bWAS = [Buf() for _ in range(3)]
            BB = [sb("BB%d" % i, [128, 512]) for i in range(3)]; bBB = [Buf() for _ in range(3)]
            MB = sb("MB", [128, 6 * D]); bMB = Buf()
            TMP = sb("TMP", [128, 48, 128]); bTMP = Buf()
            MT = sb("MT", [128, 48]); bMT = Buf()
            IDF = sb("IDF", [128, 128])
            N1G = sb("N1G", [128, 8]); N2G = sb("N2G", [128, 8]); MIXG = sb("MIXG", [128, 8])
            LBA = sb("LBA", [128, 4]); LBX = sb("LBX", [128, 4]); LAM = sb("LAM", [128, 4])
            WAF = sb("WAF", [128, 4, 128]); WXF = sb("WXF", [128, 4, 128])
            PS0 = [ps("PS0_%d" % i, [128, 512]) for i in range(2)]; bPS0 = [Buf(), Buf()]
            bC0 = Buf()
            for dst, src in ((CT, cT), (IDF, identf), (N1G, n1g), (N2G, n2g), (MIXG, mixg), (LBA, lba), (LBX, lbx),
                             (LAM, lam), (WAF, wabd), (WXF, wxbd), (IDB, identb), (ONB, onesb), (MKD, maskd),
                             (FLG, flag), (CW, cw), (CB, cb), (FCW, fcw), (FCB, fcb)):
                P.dma(SP, dst[:], src, writes=[bC0])
            P.op(DVE, CP_(CREP[:], CT[:, :].unsqueeze(2).to_broadcast([128, 8, 128])), reads=[bC0], writes=[bCONST])
            for g in range(12):
                i = g % 3
                P.dma(SP, WAS[i][:], w_ada[:, g * 512:(g + 1) * 512].rearrange("(kc p) n -> p kc n", p=128), writes=[bWAS[i]])
                P.dma(SP, BB[i][:], b_ada[0:1, g * 512:(g + 1) * 512].partition_broadcast(128)[:, 0, :], writes=[bBB[i]])
                for kc in range(8):
                    P.op(PE, MM_(PS0[g % 2][:], CREP[:, kc, :], WAS[i][:, kc, :], kc == 0, kc == 7),
                         reads=[bCONST, bWAS[i]], writes=[bPS0[g % 2]])
                P.op(DVE, TT_(MB[:, g * 512:(g + 1) * 512], PS0[g % 2][:], BB[i][:], ALU.add),
                     reads=[bPS0[g % 2], bBB[i]], writes=[bMB])
            for g in range(8):
                i = g % 2
                P.dma(SP, WIS[i][:], w_in[:, g * 320:(g + 1) * 320].rearrange("(kc p) n -> p kc n", p=128), writes=[bWIS[i]])
                P.op(DVE, CP_(WIN[:, :, g * 320:(g + 1) * 320], WIS[i][:]), reads=[bWIS[i]], writes=[bWIN])
            P.op(DVE, TT_(TMP[:], MB[:].rearrange("p (c j) -> p c j", j=128),
                          IDF[:, :].unsqueeze(1).to_broadcast([128, 48, 128]), ALU.mult),
                 reads=[bMB, bC0], writes=[bTMP])
            P.op(DVE, lambda e: e.tensor_reduce(out=MT[:], in_=TMP[:], axis=AX.X, op=ALU.add), reads=[bTMP], writes=[bMT])
            P.op(DVE, CP_(SH1[:], MT[:, 0:8]), reads=[bMT], writes=[bMOD])
            P.op(DVE, STT_(G1[:], MT[:, 8:16], 1.0, N1G[:], ALU.add, ALU.mult), reads=[bMT, bC0], writes=[bMOD])
            P.op(DVE, CP_(SH2[:], MT[:, 24:32]), reads=[bMT], writes=[bMOD])
            P.op(DVE, STT_(G2[:], MT[:, 32:40], 1.0, N2G[:], ALU.add, ALU.mult), reads=[bMT, bC0], writes=[bMOD])
            P.op(DVE, CP_(G2B[:], MB[:, 5 * D:6 * D]), reads=[bMB], writes=[bG2B])
            for nh in range(2):
                P.dma(SP, WAS[nh][:], w_out[:, nh * 512:(nh + 1) * 512].rearrange("(kc p) n -> p kc n", p=128), writes=[bWAS[nh]])
                for kc in range(8):
                    P.op(DVE, STT_(WOUT[:, kc, nh * 512:(nh + 1) * 512], WAS[nh][:, kc, :], MIXG[:, kc:kc + 1],
                                   MB[:, 2 * D + nh * 512:2 * D + (nh + 1) * 512], ALU.mult, ALU.mult),
                         reads=[bWAS[nh], bMB, bC0], writes=[bWOUT])
            P.op(ACT, A_(LAM[:], LAM[:], AF.Exp, scale=-1.0), reads=[bC0], writes=[bC0])
            P.op(ACT, A_(LAM[:], LAM[:], AF.Ln, bias=1.0), reads=[bC0], writes=[bC0])
            P.op(DVE, TS_(CL[:], LAM[:], -8.0, None, ALU.mult), reads=[bC0], writes=[bCONST])
            P.op(DVE, TS_(CL2[:], LAM[:], -16.0, None, ALU.mult), reads=[bC0], writes=[bCONST])
            P.op(DVE, TS_(NBA[:], LBA[:], -1.0, None, ALU.mult), reads=[bC0], writes=[bCONST])
            P.op(DVE, TS_(NBX[:], LBX[:], -1.0, None, ALU.mult), reads=[bC0], writes=[bCONST])
            P.op(DVE, CP_(WAB[:], WAF[:]), reads=[bC0], writes=[bCONST])
            P.op(DVE, CP_(WXB[:], WXF[:]), reads=[bC0], writes=[bCONST])
            P.emit_segment()

        with contextlib.ExitStack() as s1:
            if stop_after < 1:
                return nc
            sb, ps = mk(s1)
            PBT = sb("PBT", [128, NOWN, 32]); bPBT = Buf()
            GQB = sb("GQB", [128, 512]); GKB = sb("GKB", [128, 512])
            X = [sb("X%d" % i, [128, 2, D]) for i in range(2)]; bX = [Buf(), Buf()]
            J = sb("J", [128, D], BF16); bJ = Buf()
            ST = sb("ST", [128, 8]); bST = Buf()
            HN = sb("HN", [128, 2, D], BF16); bHN = Buf()
            HTd = [sb("HT%d" % i, [128, 8, 256], BF16) for i in range(2)]; bHTd = [Buf(), Buf()]
            SQ = sb("SQ", [128, 512]); bSQ = Buf()
            SK = sb("SK", [128, 24]); bSK = Buf()
            TK = sb("TK", [128, 512]); bTK = Buf()
            KN = sb("KN", [128, 512], BF16); bKN = Buf()
            KTt = sb("KTt", [128, 4, 256], BF16); bKTt = Buf()
            QTt = sb("QTt", [128, 4, 256], BF16); bQTt = Buf()
            KM = sb("KM", [128, 4, 2, 32], BF16); bKM = Buf()
            KMS = sb("KMS", [128, 4]); bKMS = Buf()
            VT = [sb("VT%d" % i, [128, 512], BF16) for i in range(2)]; bVT = [Buf(), Buf()]
            GB = sb("GB", [128, 8, 32]); bGB = Buf()
            M8 = sb("M8", [128, 8, 8]); bM8 = Buf()
            THR = sb("THR", [128, 8]); bTHR = Buf()
            NS = sb("NS", [128, 256], BF16); bNS = Buf()
            NSTt = sb("NSTt", [128, 2, 256], BF16); bNSTt = Buf()
            XR = sb("XR", [128, 4, 259]); bXR = Buf()
            XC = sb("XC", [128, 4, 256]); bXC = Buf()
            XCb = sb("XCb", [128, 4, 256], BF16); bXCb = Buf()
            R = sb("R", [128, 4, 256]); bR = Buf()
            I = sb("I", [128, 4, 256]); bI = Buf()
            AA = sb("AA", [128, 4, 256]); bAA = Buf()
            OM = sb("OM", [128, 4, 256]); bOM = Buf()
            H = sb("H", [128, 4, 256]); bH = Buf()
            HS = sb("HS", [128, 4]); bHS = Buf()
            T1 = sb("T1", [128, 4, 256]); bT1 = Buf()
            YSQ = sb("YSQ", [128, 4, 256], BF16); bYSQ = Buf()
            RS = sb("RS", [128, 256]); bRS = Buf()
            LRN = sb("LRN", [128, 4, 256], BF16); bLRN = Buf()
            PT1 = ps("PT1", [128, 4, 256], BF16); bPT1 = Buf()
            PF4 = ps("PF4", [128, 4, 256])
            PF = [PF4[:, 0:2, :], PF4[:, 2:4, :]]; bPF = [Buf(), Buf()]
            PK = [ps("PK%d" % i, [128, 512]) for i in range(2)]; bPK = [Buf(), Buf()]
            PM0 = ps("PM0", [128, 4, 256], BF16); bPM0 = Buf()
            PM1 = ps("PM1", [128, 512]); bPM1 = Buf()
            PRS = ps("PRS", [128, 512]); bPRS = Buf()

            P.dma(SP, PBT[:], pastb, writes=[bPBT])
            P.dma(SP, GQB[:], gqb, writes=[bPBT])
            P.dma(SP, GKB[:], gkb, writes=[bPBT])
            P.op(DVE, lambda e: e.memset(XR[:, :, 0:3], 0.0), writes=[bXR])
            P.op(DVE, lambda e: e.memset(HS[:], 0.0), writes=[bHS])
            P.op(DVE, lambda e: e.memset(KM[:], 0.0), writes=[bKM])

            def load_x(bi):
                P.dma(SP, X[bi % 2][:], xs[bi * 256:(bi + 1) * 256, :].rearrange("(t p) d -> p t d", p=128),
                      writes=[bX[bi % 2]])

            def norm_T(Xt, bXt, Gm, SHm, HT, bHT):
                for t in range(2):
                    P.op(ACT, A_(J[:], Xt[:, t, :], AF.Square, accum_out=ST[:, t:t + 1]), reads=[bXt], writes=[bJ, bST])
                P.op(ACT, A_(ST[:, 2:4], ST[:, 0:2], AF.Ln, scale=1.0 / D, bias=EPS), reads=[bST], writes=[bST])
                P.op(ACT, A_(ST[:, 4:6], ST[:, 2:4], AF.Exp, scale=-0.5), reads=[bST], writes=[bST])
                for t in range(2):
                    P.op(DVE, TS_(HN[:, t, :], Xt[:, t, :], ST[:, 4 + t:5 + t], None, ALU.mult), reads=[bXt, bST], writes=[bHN])
                for rnd in range(2):
                    for k4 in range(4):
                        kc = rnd * 4 + k4
                        for t in range(2):
                            P.op(PE, TR_(PT1[:, k4, t * 128:(t + 1) * 128], HN[:, t, kc * 128:(kc + 1) * 128], IDB[:]),
                                 reads=[bHN], writes=[bPT1])
                    for k4 in range(4):
                        kc = rnd * 4 + k4
                        P.op(DVE, TS_(HT[:, kc, :], PT1[:, k4, :], Gm[:, kc:kc + 1], SHm[:, kc:kc + 1], ALU.mult, ALU.add),
                             reads=[bPT1, bMOD], writes=[bHT])

            def headnorm(src_ps, bsrc, Gb, dst, bdst):
                P.op(ACT, A_(SQ[:], src_ps, AF.Square), reads=[bsrc], writes=[bSQ])
                P.op(DVE, lambda e: e.tensor_reduce(out=SK[:, 0:8], in_=SQ[:].rearrange("p (h d) -> p h d", h=8), axis=AX.X, op=ALU.add),
                     reads=[bSQ], writes=[bSK])
                P.op(ACT, A_(SK[:, 8:16], SK[:, 0:8], AF.Ln, scale=1.0 / 64, bias=EPS), reads=[bSK], writes=[bSK])
                P.op(ACT, A_(SK[:, 16:24], SK[:, 8:16], AF.Exp, scale=-0.5), reads=[bSK], writes=[bSK])
                P.op(DVE, TT_(TK[:].rearrange("p (h d) -> p h d", h=8), src_ps.rearrange("p (h d) -> p h d", h=8),
                              SK[:, 16:24].unsqueeze(2).to_broadcast([128, 8, 64]), ALU.mult), reads=[bsrc, bSK], writes=[bTK])
                P.op(DVE, TT_(dst, TK[:], Gb[:], ALU.mult), reads=[bTK, bPBT], writes=[bdst])

            pkc = [0]

            def chain1(bi, part=0):
                own = bi >= OWN0
                qi = bi - OWN0
                Xt, bXt = X[bi % 2], bX[bi % 2]
                HT, bHT = HTd[bi % 2], bHTd[bi % 2]
                if part in (0, 1):
                    norm_T(Xt, bXt, G1, SH1, HT, bHT)
                if part == 1:
                    return
                for t in range(2):
                    tile_idx = bi * 2 + t
                    bk = pkc[0] % 2; pkc[0] += 1
                    for kc in range(8):
                        P.op(PE, MM_(PK[bk][:], HT[:, kc, t * 128:(t + 1) * 128], WIN[:, kc, 512:1024], kc == 0, kc == 7),
                             reads=[bWIN, bHT], writes=[bPK[bk]])
                    headnorm(PK[bk][:], bPK[bk], GKB, KN[:], bKN)
                    for pr in range(4):
                        P.op(PE, TR_(PM0[:, pr, t * 128:(t + 1) * 128], KN[:, pr * 128:(pr + 1) * 128], IDB[:]), reads=[bKN], writes=[bPM0])
                    P.op(DVE, CP_(KTt[:, :, t * 128:(t + 1) * 128], PM0[:, :, t * 128:(t + 1) * 128]), reads=[bPM0], writes=[bKTt])
                    bv = pkc[0] % 2; pkc[0] += 1
                    for kc in range(8):
                        P.op(PE, MM_(PK[bv][:], HT[:, kc, t * 128:(t + 1) * 128], WIN[:, kc, 1024:1536], kc == 0, kc == 7),
                             reads=[bWIN, bHT], writes=[bPK[bv]])
                    P.op(ACT, A_(VT[t][:], PK[bv][:], AF.Copy), reads=[bPK[bv]], writes=[bVT[t]])
                    P.dma(POOL, V_s[:, tile_idx, :, :].rearrange("h p d -> p h d"), VT[t][:].rearrange("p (h d) -> p h d", h=8),
                          reads=[bVT[t]])
                P.dma(POOL, KT_s[:, bi * 256:(bi + 1) * 256].rearrange("(pr p) n -> p pr n", p=128), KTt[:], reads=[bKTt])
                P.op(DVE, lambda e: e.tensor_reduce(out=KMS[:], in_=KTt[:], axis=AX.X, op=ALU.add), reads=[bKTt], writes=[bKMS])
                if own:
                    for t in range(2):
                        bq = pkc[0] % 2; pkc[0] += 1
                        for kc in range(8):
                            P.op(PE, MM_(PK[bq][:], HT[:, kc, t * 128:(t + 1) * 128], WIN[:, kc, 0:512], kc == 0, kc == 7),
                                 reads=[bWIN, bHT], writes=[bPK[bq]])
                        headnorm(PK[bq][:], bPK[bq], GQB, KN[:], bKN)
                        for pr in range(4):
                            P.op(PE, TR_(PM0[:, pr, t * 128:(t + 1) * 128], KN[:, pr * 128:(pr + 1) * 128], IDB[:]), reads=[bKN], writes=[bPM0])
                        P.op(DVE, CP_(QTt[:, :, t * 128:(t + 1) * 128], PM0[:, :, t * 128:(t + 1) * 128]), reads=[bPM0], writes=[bQTt])
                        for pr in range(4):
                            P.op(PE, MM_(PM1[:, pr * 64:(pr + 1) * 64], QTt[:, pr, t * 128:(t + 1) * 128],
                                         KM[:, pr, :, :].rearrange("p s j -> p (s j)"), True, True), reads=[bQTt, bKM], writes=[bPM1])
                        P.op(DVE, TT_(GB[:], PM1[:, 0:256].rearrange("p (h j) -> p h j", h=8),
                                      PBT[:, qi, :].unsqueeze(1).to_broadcast([128, 8, 32]), ALU.add), reads=[bPM1, bPBT], writes=[bGB])
                        for h in range(8):
                            P.op(DVE, lambda e, h=h: e.max(out=M8[:, h, :], in_=GB[:, h, :]), reads=[bGB], writes=[bM8])
                        P.op(DVE, TS_(THR[:], M8[:, :, 2], -1e29, None, ALU.max), reads=[bM8], writes=[bTHR])
                        P.op(DVE, TT_(NS[:].rearrange("p (h j) -> p h j", h=8), GB[:],
                                      THR[:, :].unsqueeze(2).to_broadcast([128, 8, 32]), ALU.is_lt), reads=[bGB, bTHR], writes=[bNS])
                        for g in range(2):
                            P.op(PE, TR_(PT1[:, g, t * 128:(t + 1) * 128], NS[:, g * 128:(g + 1) * 128], IDB[:]), reads=[bNS], writes=[bPT1])
                        P.op(DVE, CP_(NSTt[:, :, t * 128:(t + 1) * 128], PT1[:, 0:2, t * 128:(t + 1) * 128]), reads=[bPT1], writes=[bNSTt])
                    P.dma(POOL, QT_s[:, qi * 256:(qi + 1) * 256].rearrange("(pr p) n -> p pr n", p=128), QTt[:], reads=[bQTt])
                    P.dma(POOL, NS_s[:, qi * 256:(qi + 1) * 256].rearrange("(g p) n -> p g n", p=128), NSTt[:], reads=[bNSTt])
                P.op(DVE, TS_(KM[0:64, :, 0, bi], KMS[0:64, :], 1.0 / 256, None, ALU.mult), reads=[bKMS], writes=[bKM])
                P.op(DVE, TS_(KM[64:128, :, 1, bi], KMS[64:128, :], 1.0 / 256, None, ALU.mult), reads=[bKMS], writes=[bKM])

            def chain2(bi):
                own = bi >= OWN0
                qi = bi - OWN0
                HT, bHT = HTd[bi % 2], bHTd[bi % 2]
                for c in range(4):
                    for kc in range(8):
                        P.op(PE, MM_(PF[c // 2][:, c % 2, :], WIN[:, kc, 1536 + c * 128:1536 + (c + 1) * 128], HT[:, kc, :], kc == 0, kc == 7),
                             reads=[bWIN, bHT], writes=[bPF[c // 2]])
                if bi == OWN0 + 1:
                    P.op(DVE, TS_(XR[:, :, 0:3], XR[:, :, 0:3], FLG[:, 0:1], None, ALU.mult), reads=[bXR, bCONST, bC0], writes=[bXR])
                    P.op(DVE, TS_(HS[:], HS[:], FLG[:, 0:1], None, ALU.mult), reads=[bHS, bC0], writes=[bHS])
                P.op(ACT, A_(XR[:, :, 3:259], PF4[:], AF.Copy), reads=[bPF[0], bPF[1]], writes=[bXR])
                for c in range(4):
                    P.op(ACT, A_(XC[:, c, :], XR[:, c, 0:256], AF.Identity, scale=CW[:, c, 0:1], bias=CB[:, c:c + 1]),
                         reads=[bXR, bC0], writes=[bXC])
                    for j in range(1, 4):
                        P.op(DVE, STT_(XC[:, c, :], XR[:, c, j:j + 256], CW[:, c, j:j + 1], XC[:, c, :], ALU.mult, ALU.add),
                             reads=[bXR, bXC, bC0], writes=[bXC])
                P.op(DVE, CP_(XR[:, :, 0:3], XR[:, :, 256:259]), reads=[bXR], writes=[bXR])
                P.op(DVE, CP_(XCb[:], XC[:]), reads=[bXC], writes=[bXCb])
                for (Wb, NBb, dst, bdst) in ((WAB, NBA, R, bR), (WXB, NBX, I, bI)):
                    for c in range(4):
                        P.op(PE, MM_(PF[c // 2][:, c % 2, :], Wb[:, c, :], XCb[:, c, :], True, True),
                             reads=[bCONST, bXCb], writes=[bPF[c // 2]])
                    for c in range(4):
                        P.op(ACT, A_(dst[:, c, :], PF[c // 2][:, c % 2, :], AF.Exp, scale=-1.0, bias=NBb[:, c:c + 1]),
                             reads=[bPF[c // 2], bCONST], writes=[bdst])
                    P.op(ACT, A_(dst[:], dst[:], AF.Ln, bias=1.0), reads=[bdst], writes=[bdst])
                    P.op(ACT, A_(dst[:], dst[:], AF.Exp, scale=-1.0), reads=[bdst], writes=[bdst])
                for c in range(4):
                    P.op(ACT, A_(AA[:, c, :], R[:, c, :], AF.Exp, scale=CL[:, c:c + 1]), reads=[bR, bCONST], writes=[bAA])
                    P.op(ACT, A_(OM[:, c, :], R[:, c, :], AF.Exp, scale=CL2[:, c:c + 1]), reads=[bR, bCONST], writes=[bOM])
                P.op(DVE, TS_(OM[:], OM[:], -1.0, 1.0, ALU.mult, ALU.add), reads=[bOM], writes=[bOM])
                P.op(ACT, A_(OM[:], OM[:], AF.Ln), reads=[bOM], writes=[bOM])
                P.op(ACT, A_(OM[:], OM[:], AF.Exp, scale=0.5), reads=[bOM], writes=[bOM])
                P.op(DVE, TT_(OM[:], OM[:], I[:], ALU.mult), reads=[bOM, bI], writes=[bOM])
                P.op(DVE, TT_(OM[:], OM[:], XC[:], ALU.mult), reads=[bOM, bXC], writes=[bOM])
                for c in range(4):
                    P.op(DVE, lambda e, c=c: e.tensor_tensor_scan(out=H[:, c, :], data0=AA[:, c, :], data1=OM[:, c, :],
                                                                   initial=HS[:, c:c + 1], op0=ALU.mult, op1=ALU.add),
                         reads=[bAA, bOM, bHS], writes=[bH])
                P.op(DVE, CP_(HS[:], H[:, :, 255]), reads=[bH], writes=[bHS])
                if own:
                    for c in range(4):
                        for kc in range(8):
                            P.op(PE, MM_(PF[c // 2][:, c % 2, :], WIN[:, kc, 2048 + c * 128:2048 + (c + 1) * 128], HT[:, kc, :], kc == 0, kc == 7),
                                 reads=[bWIN, bHT], writes=[bPF[c // 2]])
                    for c2 in range(2):
                        cs = slice(2 * c2, 2 * c2 + 2)
                        P.op(ACT, A_(T1[:, cs, :], PF[c2][:], AF.Square), reads=[bPF[c2]], writes=[bT1])
                        P.op(DVE, TS_(T1[:, cs, :], T1[:, cs, :], 0.044715, 1.0, ALU.mult, ALU.add), reads=[bT1], writes=[bT1])
                        P.op(DVE, TT_(T1[:, cs, :], T1[:, cs, :], PF[c2][:], ALU.mult), reads=[bT1, bPF[c2]], writes=[bT1])
                        P.op(ACT, A_(T1[:, cs, :], T1[:, cs, :], AF.Exp, scale=-1.5957691216), reads=[bT1], writes=[bT1])
                        P.op(ACT, A_(T1[:, cs, :], T1[:, cs, :], AF.Ln, bias=1.0), reads=[bT1], writes=[bT1])
                        P.op(ACT, A_(T1[:, cs, :], T1[:, cs, :], AF.Exp, scale=-1.0), reads=[bT1], writes=[bT1])
                        P.op(DVE, TT_(T1[:, cs, :], T1[:, cs, :], PF[c2][:], ALU.mult), reads=[bT1, bPF[c2]], writes=[bT1])
                    P.op(DVE, TT_(T1[:], T1[:], H[:], ALU.mult), reads=[bT1, bH], writes=[bT1])
                    P.op(ACT, A_(YSQ[:], T1[:], AF.Square), reads=[bT1], writes=[bYSQ])
                    for c in range(4):
                        P.op(PE, MM_(PRS[:, 0:256], ONB[:], YSQ[:, c, :], c == 0, c == 3), reads=[bYSQ, bC0], writes=[bPRS])
                    P.op(ACT, A_(RS[:], PRS[:, 0:256], AF.Ln, scale=1.0 / 512, bias=EPS), reads=[bPRS], writes=[bRS])
                    P.op(ACT, A_(RS[:], RS[:], AF.Exp, scale=-0.5), reads=[bRS], writes=[bRS])
                    P.op(DVE, TT_(LRN[:], T1[:], RS[:, :].unsqueeze(1).to_broadcast([128, 4, 256]), ALU.mult), reads=[bT1, bRS], writes=[bLRN])
                    P.dma(POOL, LR_s[:, qi * 256:(qi + 1) * 256].rearrange("(c p) n -> p c n", p=128), LRN[:], reads=[bLRN])

            load_x(0)
            load_x(1)
            chain1(0, 1)
            for bi in range(NB):
                if bi + 1 < NB:
                    chain1(bi + 1, 1)
                chain2(bi)
                chain1(bi, 2)
                if bi + 2 < NB:
                    load_x(bi + 2)
            P.emit_segment()
        sWI.close()

        sW = top.enter_context(contextlib.ExitStack())
        sbW, _ = mk(sW)
        WUP = sbW("WUP", [128, 8, 2 * DFF], BF16); bWUP = Buf()
        with contextlib.ExitStack() as s2:
            if stop_after < 2:
                return nc
            sb, ps = mk(s2)
            WSW = [sb("WSW%d" % i, [128, 512]) for i in range(4)]; bWSW = [Buf() for _ in range(4)]
            KX = [sb("KX%d" % i, [96, S], BF16) for i in range(2)]; bKX = [Buf(), Buf()]
            QX = [sb("QX%d" % i, [96, NQ], BF16) for i in range(2)]; bQX = [Buf(), Buf()]
            VX = [sb("VX%d" % i, [128, 64, 65], BF16) for i in range(2)]; bVX = [Buf(), Buf()]
            PP = [sb("PP%d" % i, [128, 2, 256], BF16) for i in range(3)]; bPP = [Buf() for _ in range(3)]
            AO = [sb("AO%d" % i, [128, 2, 64]) for i in range(2)]; bAO = [Buf(), Buf()]
            RD = sb("RD", [128, 4]); bRD = Buf()
            SB_ = [ps("SB%d" % i, [128, 2, 256]) for i in range(3)]; bSB = [Buf() for _ in range(3)]
            OB = [[ps("OB%d_%d" % (i, q), [128, 512]) for q in range(2)] for i in range(2)]
            bOB = [[Buf(), Buf()], [Buf(), Buf()]]
            for i in range(2):
                P.dma(SP, KX[i][64:96, :], oh, writes=[bKX[i]])
                P.op(DVE, lambda e, i=i: e.memset(VX[i][:, :, 64:65], 1.0), writes=[bVX[i]])

            def load_head(h):
                i = h % 2
                P.dma(SP, KX[i][0:64, :], KT_s[h * 64:(h + 1) * 64, :], writes=[bKX[i]])
                P.dma(SP, QX[i][0:64, :], QT_s[h * 64:(h + 1) * 64, :], writes=[bQX[i]])
                P.dma(SP, QX[i][64:96, :], NS_s[h * 32:(h + 1) * 32, :], writes=[bQX[i]])
                P.dma(SP, VX[i][:, :, 0:64], V_s[h].rearrange("t p d -> p t d"), writes=[bVX[i]])

            its = []
            oi = 0
            for h in range(8):
                for qi in range(NOWN):
                    bi = OWN0 + qi
                    ob = oi % 2; oi += 1
                    for j in range(bi + 1):
                        its.append((h, qi, bi, ob, j))

            def emit_qk(n):
                h, qi, bi, ob, j = its[n]
                hb = h % 2
                sbk = n % 3
                qc = slice(qi * 256, (qi + 1) * 256)
                diag = (j == bi)
                for kh in range(2):
                    kc_ = slice(j * 256 + kh * 128, j * 256 + (kh + 1) * 128)
                    if not diag:
                        P.op(PE, MM_(SB_[sbk][:, kh, :], KX[hb][0:96, kc_], QX[hb][0:96, qc], True, True),
                             reads=[bKX[hb], bQX[hb]], writes=[bSB[sbk]])
                    else:
                        P.op(PE, MM_(SB_[sbk][:, kh, :], KX[hb][0:64, kc_], QX[hb][0:64, qc], True, False),
                             reads=[bKX[hb], bQX[hb]], writes=[bSB[sbk]])
                        P.op(PE, MM_(SB_[sbk][:, kh, :], IDB[:], MKD[:, kh, :], False, True),
                             reads=[bC0], writes=[bSB[sbk]])

            def emit_rest(n):
                h, qi, bi, ob, j = its[n]
                hb = h % 2
                sbk = n % 3
                nj = bi + 1
                P.op(ACT, A_(PP[sbk][:], SB_[sbk][:], AF.Exp, scale=0.125), reads=[bSB[sbk]], writes=[bPP[sbk]])
                if n + 2 < len(its):
                    emit_qk(n + 2)
                for kh in range(2):
                    for qh in range(2):
                        P.op(PE, MM_(OB[ob][qh][:, 0:65], PP[sbk][:, kh, qh * 128:(qh + 1) * 128], VX[hb][:, j * 2 + kh, :],
                                     (j == 0 and kh == 0), (j == nj - 1 and kh == 1)),
                             reads=[bPP[sbk], bVX[hb]], writes=[bOB[ob][qh]])
                if j == nj - 1:
                    for qh in range(2):
                        P.op(DVE, lambda e, ob=ob, qh=qh: e.reciprocal(RD[:, ob * 2 + qh:ob * 2 + qh + 1], OB[ob][qh][:, 64:65]),
                             reads=[bOB[ob][qh]], writes=[bRD])
                        P.op(DVE, TS_(AO[ob][:, qh, :], OB[ob][qh][:, 0:64], RD[:, ob * 2 + qh:ob * 2 + qh + 1], None, ALU.mult),
                             reads=[bOB[ob][qh], bRD], writes=[bAO[ob]])
                    P.dma(POOL, AT_s[qi * 256:(qi + 1) * 256, h * 64:(h + 1) * 64].rearrange("(qh p) d -> p qh d", p=128), AO[ob][:],
                          reads=[bAO[ob]])

            load_head(0)
            load_head(1)
            def wup_piece(g):
                i = g % 4
                P.dma(SP, WSW[i][:].rearrange("p (kc n) -> p kc n", kc=8),
                      w_up[:, g * 64:(g + 1) * 64].rearrange("(kc p) n -> p kc n", p=128), writes=[bWSW[i]])
                P.op(DVE, CP_(WUP[:, :, g * 64:(g + 1) * 64], WSW[i][:].rearrange("p (kc n) -> p kc n", kc=8)),
                     reads=[bWSW[i]], writes=[bWUP])
            wnext = [0]
            emit_qk(0)
            emit_qk(1)
            for n in range(len(its)):
                h, qi, bi, ob, j = its[n]
                if n > 0 and its[n - 1][0] != h and h + 1 < 8:
                    load_head(h + 1)
                if n % 24 == 12 and wnext[0] < 88:
                    wup_piece(wnext[0]); wnext[0] += 1
                emit_rest(n)
            while wnext[0] < 88:
                wup_piece(wnext[0]); wnext[0] += 1
            P.sched = False
            P.emit_segment()
            P.sched = True

        with contextlib.ExitStack() as s3:
            if stop_after < 3:
                return nc
            sb, ps = mk(s3)
            WDN = sb("WDN", [128, 22, D], BF16); bWDN = Buf()
            XD = [sb("XT3_%d" % i, [128, 2, D]) for i in range(2)]; bXD = [Buf(), Buf()]
            AT = sb("AT", [128, 512]); bAT = Buf()
            ST = sb("ST3", [128, 8]); bST = Buf()
            AN = sb("AN", [128, 512], BF16); bAN = Buf()
            HTD = [sb("HTD%d" % i, [128, 8, 256], BF16) for i in range(2)]; bHTD = [Buf(), Buf()]
            HN = sb("HN3", [128, D], BF16); bHN = Buf()
            UB = [sb("UB%d" % i, [128, 2, 258]) for i in range(3)]; bUB = [Buf() for _ in range(3)]; bUBh = [Buf() for _ in range(3)]
            ACC = [sb("ACC%d" % i, [128, 2, 256]) for i in range(3)]; bACC = [Buf() for _ in range(3)]
            WS3 = [ACC[i][:].rearrange("p a n -> p (a n)") for i in range(3)]; bWS3 = bACC
            UH = sb("UH", [128, 44, 2]); bUH = Buf()
            ACTT = sb("ACTT", [128, 22, 256], BF16); bACTT = Buf(); bACTTm = [Buf() for _ in range(22)]
            PY = [ps("PY%d" % i, [128, 512]) for i in range(4)]; bPY = [Buf() for _ in range(4)]
            PT1 = ps("PT3", [128, 4, 256], BF16); bPT1 = Buf()
            PU = [ps("PU%d" % i, [128, 2, 256]) for i in range(3)]; bPU = [Buf() for _ in range(3)]

            ACTF = ACTT[:].rearrange("p m n -> p (m n)").bitcast(F32)
            STG = [(WS3[0], bACC[0]), (WS3[1], bACC[1]), (WS3[2], bACC[2])] + [(ACTF[:, k * 512:(k + 1) * 512], Buf()) for k in range(5)]
            wi = 0
            for m in range(22):
                for nh in range(2):
                    stg, bstg = STG[wi % len(STG)]; wi += 1
                    P.dma(SP, stg, w_down[m * 128:(m + 1) * 128, nh * 512:(nh + 1) * 512], writes=[bstg])
                    P.op(DVE, TT_(WDN[:, m, nh * 512:(nh + 1) * 512], stg, G2B[:, nh * 512:(nh + 1) * 512], ALU.mult),
                         reads=[bstg, bG2B], writes=[bWDN])
            P.op(DVE, _w(lambda e: e.memset(UH[:], 0.0), 88), reads=[b for _, b in STG[3:]], writes=[bUH] + bACTTm)

            upc = [0]

            def front(qi):
                bi = OWN0 + qi
                Xt, bXt = XD[qi % 2], bXD[qi % 2]
                MIXT = HT = HTD[qi % 2]
                bMIXL = bMIXA = bHT = bHTD[qi % 2]
                P.dma(SP, Xt[:], xs[bi * 256:(bi + 1) * 256, :].rearrange("(t p) d -> p t d", p=128), writes=[bXt])
                P.dma(SP, MIXT[:, 0:4, :], LR_s[:, qi * 256:(qi + 1) * 256].rearrange("(c p) n -> p c n", p=128), writes=[bMIXL])
                for t in range(2):
                    P.dma(SP, AT[:], AT_s[qi * 256 + t * 128:qi * 256 + (t + 1) * 128, :], writes=[bAT])
                    P.op(ACT, A_(AN[:], AT[:], AF.Square, accum_out=ST[:, 0:1]), reads=[bAT], writes=[bAN, bST])
                    P.op(ACT, A_(ST[:, 2:3], ST[:, 0:1], AF.Ln, scale=1.0 / 512, bias=EPS), reads=[bST], writes=[bST])
                    P.op(ACT, A_(ST[:, 4:5], ST[:, 2:3], AF.Exp, scale=-0.5), reads=[bST], writes=[bST])
                    P.op(DVE, TS_(AN[:], AT[:], ST[:, 4:5], None, ALU.mult), reads=[bAT, bST], writes=[bAN])
                    for c in range(4):
                        P.op(PE, TR_(PT1[:, c, t * 128:(t + 1) * 128], AN[:, c * 128:(c + 1) * 128], IDB[:]), reads=[bAN], writes=[bPT1])
                P.op(DVE, CP_(MIXT[:, 4:8, :], PT1[:]), reads=[bPT1], writes=[bMIXA])
                for t in range(2):
                    for nh in range(2):
                        pb = t * 2 + nh
                        for kc in range(8):
                            P.op(PE, MM_(PY[pb][:], MIXT[:, kc, t * 128:(t + 1) * 128], WOUT[:, kc, nh * 512:(nh + 1) * 512], kc == 0, kc == 7),
                                 reads=[bMIXL, bMIXA, bWOUT], writes=[bPY[pb]])
                        P.op(DVE, TT_(Xt[:, t, nh * 512:(nh + 1) * 512], PY[pb][:], Xt[:, t, nh * 512:(nh + 1) * 512], ALU.add),
                             reads=[bPY[pb], bXt], writes=[bXt])
                if debug:
                    P.dma(POOL, X1_s[qi * 256:(qi + 1) * 256, :].rearrange("(t p) d -> p t d", p=128), Xt[:], reads=[bXt])
                for t in range(2):
                    P.op(ACT, A_(HN[:], Xt[:, t, :], AF.Square, accum_out=ST[:, t:t + 1]), reads=[bXt], writes=[bHN, bST])
                P.op(ACT, A_(ST[:, 2:4], ST[:, 0:2], AF.Ln, scale=1.0 / D, bias=EPS), reads=[bST], writes=[bST])
                P.op(ACT, A_(ST[:, 4:6], ST[:, 2:4], AF.Exp, scale=-0.5), reads=[bST], writes=[bST])
                HT2 = HT[:].rearrange("p k (t n) -> p k t n", t=2)
                for t in range(2):
                    P.op(DVE, TS_(HN[:], Xt[:, t, :], ST[:, 4 + t:5 + t], None, ALU.mult), reads=[bXt, bST], writes=[bHN])
                    for rnd in range(2):
                        for k4 in range(4):
                            kc = rnd * 4 + k4
                            P.op(PE, TR_(PT1[:, k4, 0:128], HN[:, kc * 128:(kc + 1) * 128], IDB[:]), reads=[bHN], writes=[bPT1])
                        for k4 in range(4):
                            kc = rnd * 4 + k4
                            P.op(DVE, TS_(HT[:, kc, t * 128:(t + 1) * 128], PT1[:, k4, 0:128], G2[:, kc:kc + 1], SH2[:, kc:kc + 1], ALU.mult, ALU.add),
                                 reads=[bPT1, bMOD], writes=[bHT])

            def rest(qi):
                bi = OWN0 + qi
                Xt, bXt = XD[qi % 2], bXD[qi % 2]
                HT = HTD[qi % 2]
                bHT = bHTD[qi % 2]
                if qi == 1:
                    P.op(DVE, TS_(UH[:], UH[:], FLG[:, 0:1], None, ALU.mult), reads=[bUH, bC0], writes=[bUH])
                for m in range(22):
                    u = upc[0] % 3; upc[0] += 1
                    bPU_, bUB_, bACC_ = bPU, bUB, bACC
                    for half, mm in ((0, m), (1, m + 22)):
                        for kc in range(8):
                            P.op(PE, MM_(PU[u][:, half, :], WUP[:, kc, mm * 128:(mm + 1) * 128], HT[:, kc, :], kc == 0, kc == 7),
                                 reads=[bWUP, bHT], writes=[bPU_[u]])
                    if qi > 0:
                        for half, mm in ((0, m), (1, m + 22)):
                            P.op(POOL, CP_(UB[u][:, half, 0:2], UH[:, mm, :]), reads=[bUH], writes=[bUBh[u]])
                    P.op(ACT, A_(UB[u][:, :, 2:258], PU[u][:], AF.Copy), reads=[bPU_[u]], writes=[bUB_[u]])
                    for half, mm in ((0, m), (1, m + 22)):
                        P.op(POOL, CP_(UH[:, mm, :], UB[u][:, half, 256:258]), reads=[bUB_[u]], writes=[bUH])
                    if qi == 0:
                        continue
                    for half, mm in ((0, m), (1, m + 22)):
                        P.op(ACT, A_(ACC[u][:, half, :], UB[u][:, half, 0:256], AF.Identity, scale=FCW[:, mm, 0:1], bias=FCB[:, mm:mm + 1]),
                             reads=[bUB_[u], bUBh[u], bC0], writes=[bACC_[u]])
                        for jj in (1, 2):
                            P.op(DVE, STT_(ACC[u][:, half, :], UB[u][:, half, jj:jj + 256], FCW[:, mm, jj:jj + 1], ACC[u][:, half, :], ALU.mult, ALU.add),
                                 reads=[bUB_[u], bUBh[u], bACC_[u], bC0], writes=[bACC_[u]])
                    P.op(ACT, A_(ACC[u][:, 0, :], ACC[u][:, 0, :], AF.Silu), reads=[bACC_[u]], writes=[bACC_[u]])
                    P.op(POOL, TT_(ACTT[:, m, :], ACC[u][:, 0, :], ACC[u][:, 1, :], ALU.mult), reads=[bACC_[u]], writes=[bACTTm[m]])
                if qi == 0:
                    return
                for m in range(22):
                    for t in range(2):
                        for nh in range(2):
                            pb = t * 2 + nh
                            P.op(PE, MM_(PY[pb][:], ACTT[:, m, t * 128:(t + 1) * 128], WDN[:, m, nh * 512:(nh + 1) * 512], m == 0, m == 21),
                                 reads=[bACTTm[m], bWDN], writes=[bPY[pb]])
                for t in range(2):
                    for nh in range(2):
                        pb = t * 2 + nh
                        P.op(DVE, TT_(Xt[:, t, nh * 512:(nh + 1) * 512], PY[pb][:], Xt[:, t, nh * 512:(nh + 1) * 512], ALU.add),
                             reads=[bPY[pb], bXt], writes=[bXt])
                P.dma(POOL, out[(qi - 1) * 256:qi * 256, :].rearrange("(t p) d -> p t d", p=128), Xt[:], reads=[bXt])

            front(0)
            for qi in range(NOWN):
                if qi + 1 < NOWN:
                    front(qi + 1)
                rest(qi)
            P.emit_segment(last=True)
            nc._est_log = P.est_log
    return nc


def make_in_maps(x, c, w_ada, b_ada, norm1_g, w_in, q_norm_g, k_norm_g, lru_conv_w, lru_conv_b,
                 lru_wa, lru_ba, lru_wx, lru_bx, lru_lambda, lru_out_g, attn_out_g, w_out,
                 norm2_g, w_up, ffn_conv_w, ffn_conv_b, w_down):
    f = lambda a: np.ascontiguousarray(np.asarray(a, dtype=np.float32))
    x = f(x); c = f(c)

    def fm(v, nch):
        return f(np.asarray(v, np.float32).reshape(nch, 128).T)

    def bd(w):
        w = np.asarray(w, np.float32)
        o = np.zeros((128, 4, 128), np.float32)
        for ch in range(4):
            for s_ in range(2):
                o[s_ * 64:(s_ + 1) * 64, ch, s_ * 64:(s_ + 1) * 64] = w[ch * 2 + s_]
        return o

    shared = {
        "w_ada": f(w_ada[0]), "b_ada": f(b_ada[0]).reshape(1, -1),
        "n1g": fm(norm1_g[0], 8), "n2g": fm(norm2_g[0], 8),
        "mixg": fm(np.concatenate([np.asarray(lru_out_g[0]), np.asarray(attn_out_g[0])]), 8),
        "w_in": f(w_in[0]),
        "gqb": f(np.broadcast_to(np.tile(np.asarray(q_norm_g[0], np.float32), 8), (128, 512))),
        "gkb": f(np.broadcast_to(np.tile(np.asarray(k_norm_g[0], np.float32), 8), (128, 512))),
        "cw": f(np.asarray(lru_conv_w[0], np.float32).reshape(4, 4, 128).transpose(2, 1, 0)),
        "cb": fm(lru_conv_b[0], 4),
        "wabd": bd(lru_wa[0]), "wxbd": bd(lru_wx[0]),
        "lba": fm(lru_ba[0], 4), "lbx": fm(lru_bx[0], 4), "lam": fm(lru_lambda[0], 4),
        "w_out": f(w_out[0]), "w_up": f(w_up[0]),
        "fcw": f(np.asarray(ffn_conv_w[0], np.float32).reshape(3, 44, 128).transpose(2, 1, 0)),
        "fcb": fm(ffn_conv_b[0], 44),
        "w_down": f(w_down[0]),
        "identb": np.eye(128, dtype=np.float32).astype(ml_dtypes.bfloat16),
        "identf": np.eye(128, dtype=np.float32),
        "onesb": np.ones((128, 128), np.float32).astype(ml_dtypes.bfloat16),
    }
    ohm = np.zeros((32, S), np.float32)
    for j in range(32):
        ohm[j, j * 256:(j + 1) * 256] = NEG
    shared["oh"] = ohm.astype(ml_dtypes.bfloat16)
    kk = np.arange(128)[:, None]
    qq = np.arange(128)[None, :]
    tri = np.where(kk <= qq, 0.0, NEG).astype(np.float32)
    md = np.zeros((128, 2, 256), np.float32)
    md[:, 0, 0:128] = tri
    md[:, 1, 0:128] = NEG
    md[:, 1, 128:256] = tri
    shared["maskd"] = md.astype(ml_dtypes.bfloat16)
    in_maps = []
    for core in range(8):
        b, half = core // 2, core % 2
        if half == 1:
            xsv = x[b]
        else:
            xsv = np.concatenate([np.zeros((4096, D), np.float32), x[b, :4096]], axis=0)
        pb = np.full((NOWN, 32), -1e30, np.float32)
        for qi in range(NOWN):
            bi = OWN0 + qi
            lo = 16 if half == 0 else 0
            if bi > lo:
                pb[qi, lo:bi] = 0.0
        m = dict(shared)
        m["xs"] = f(xsv)
        m["cT"] = fm(c[b], 8)
        m["flag"] = np.full((128, 1), float(half), np.float32)
        m["pastb"] = f(np.broadcast_to(pb[None], (128, NOWN, 32)))
        in_maps.append(m)
    return in_maps


_NC = {}


def kernel(**inputs):
    in_maps = make_in_maps(**inputs)
    if "nc" not in _NC:
        _NC["nc"] = build(False)
    res = run_bass_kernel_spmd(_NC["nc"], in_maps, core_ids=list(range(8)))
    outp = np.zeros((4, S, D), np.float32)
    for core in range(8):
        b, half = core // 2, core % 2
        outp[b, half * 4096:(half + 1) * 4096] = res.results[core]["out"]
    return outp
```

```python
import contextlib
import numpy as np
import ml_dtypes
import concourse.bass as bass
import concourse.mybir as mybir
from concourse.bass_utils import run_bass_kernel_spmd

F32 = mybir.dt.float32
BF16 = mybir.dt.bfloat16
AF = mybir.ActivationFunctionType
ALU = mybir.AluOpType
AX = mybir.AxisListType

PE, ACT, DVE, POOL, SP = "tensor", "scalar", "vector", "gpsimd", "sync"
COMPUTE = (PE, ACT, DVE, POOL)

D = 1024
S = 8192
NB = 32
OWN0 = 15
NOWN = NB - OWN0
NQ = NOWN * 256
DFF = 2816
NEG = -30000.0
EPS = 1e-6


class Buf:
    __slots__ = ("name", "w", "rs")

    def __init__(self, name=""):
        self.name = name
        self.w = None
        self.rs = []


class Op:
    __slots__ = ("eng", "fn", "deps", "dma", "sig", "sem", "need", "seg", "alld", "cost", "lat", "line")

    def __init__(self, eng, fn, dma, seg):
        self.eng = eng
        self.fn = fn
        self.dma = dma
        self.deps = set()
        self.sig = None
        self.sem = None
        self.need = False
        self.seg = seg


class Prog:
    def __init__(self, nc, stack, n_dma_sems=16):
        self.nc = nc
        self.n = n_dma_sems
        self.seg = 0
        self.ops = []
        self.esem = {e: stack.enter_context(nc.semaphore("s_" + e)) for e in COMPUTE}
        self.dsem = {q: [stack.enter_context(nc.semaphore("d_%s_%d" % (q, i))) for i in range(n_dma_sems)]
                     for q in (SP, POOL)}
        self.ecnt = {e: 0 for e in COMPUTE}
        self.dcnt = {q: [0] * n_dma_sems for q in self.dsem}
        self.dlast = {q: [None] * n_dma_sems for q in self.dsem}
        self.dnum = {q: 0 for q in self.dsem}
        self.finals = []
        self.rec = None
        self.sched = True
        self.est_ns = 0.0
        self.est_log = []

    def chain(self, fn):
        self.rec = []
        fn()
        lst, self.rec = self.rec, None
        return lst

    def merge(self, *chains):
        items = []
        for ci, ch in enumerate(chains):
            n = len(ch)
            for i, it in enumerate(ch):
                items.append(((i + 0.5) / n, ci, i, it))
        items.sort(key=lambda x: (x[0], x[1], x[2]))
        for _, _, _, it in items:
            self.op(*it)

    def op(self, eng, fn, reads=(), writes=(), dma=False):
        if self.rec is not None:
            self.rec.append((eng, fn, tuple(reads), tuple(writes), dma))
            return None
        o = Op(eng, fn, dma, self.seg)
        for b in reads:
            if b.w is not None:
                o.deps.add(b.w)
        for b in writes:
            if b.w is not None:
                o.deps.add(b.w)
            for r in b.rs:
                o.deps.add(r)
        for b in reads:
            b.rs.append(o)
        for b in writes:
            b.w = o
            b.rs = []
        o.deps.discard(o)
        import sys as _sys
        fr = _sys._getframe(1)
        if fr.f_code.co_name in ('dma', 'merge'):
            fr = fr.f_back
        o.line = fr.f_lineno
        n = getattr(fn, "n", 256)
        k = getattr(fn, "k", 0)
        if dma:
            o.cost = 60.0
            o.lat = 2200.0 + n / 60.0
        elif eng == PE:
            o.cost = 45.0 + 0.37 * n
            o.lat = o.cost + 200.0
        elif eng == ACT:
            o.cost = 130.0 + 0.83 * n + 120.0 * k
            o.lat = o.cost + 120.0
        elif eng == POOL:
            o.cost = 250.0 + 2.0 * n
            o.lat = o.cost + 150.0
        else:
            o.cost = 150.0 + (1.05 if k else 0.6) * n
            o.lat = o.cost + 120.0
        self.ops.append(o)
        return o

    def dma(self, eng, out, in_, reads=(), writes=()):
        nbytes = 1
        for d in out.shape:
            nbytes *= int(d)
        nbytes *= 2 if out.dtype == BF16 else 4
        f = lambda e: e.dma_start(out=out, in_=in_)
        f.n = nbytes
        return self.op(eng, f, reads, writes, dma=True)

    def schedule(self, ops):
        import bisect
        n = len(ops)
        pos = {id(o): i for i, o in enumerate(ops)}
        succ = [[] for _ in range(n)]
        indeg = [0] * n
        for i, o in enumerate(ops):
            for d in o.alld:
                j = pos.get(id(d))
                if j is None:
                    continue
                succ[j].append(i)
                indeg[i] += 1
        rt = [0.0] * n
        why = [None] * n
        stt = [0.0] * n
        lastop = {}
        avail = {e: [] for e in (PE, ACT, DVE, POOL, SP)}
        free = {e: 0.0 for e in avail}
        for i, o in enumerate(ops):
            if indeg[i] == 0:
                avail[o.eng].append(i)
        order = []
        WIN_ = 40
        done = 0
        while done < n:
            best = None
            for e, lst in avail.items():
                if not lst:
                    continue
                fe = free[e]
                cb = None
                for i in lst[:WIN_]:
                    st = rt[i] if rt[i] > fe else fe
                    if cb is None or st < cb[0]:
                        cb = (st, i)
                if best is None or cb < best[0:2]:
                    best = (cb[0], cb[1], e)
            st, i, e = best
            lst = avail[e]
            lst.pop(bisect.bisect_left(lst, i))
            o = ops[i]
            stt[i] = st
            if rt[i] < st and e in lastop:
                why[i] = ('eng', lastop[e])
            lastop[e] = i
            free[e] = st + o.cost
            fin = st + o.lat
            order.append(o)
            done += 1
            for k in succ[i]:
                if fin > rt[k]:
                    rt[k] = fin
                    if why[k] is None or why[k][0] == 'dep':
                        why[k] = ('dep', i)
                indeg[k] -= 1
                if indeg[k] == 0:
                    bisect.insort(avail[ops[k].eng], k)
        self.est_ns = max(free.values())
        busy = {}
        for o in ops:
            busy[o.eng] = busy.get(o.eng, 0.0) + o.cost
        self.est_busy = {k: round(v / 1e3) for k, v in busy.items()}
        return order

    def emit_segment(self, last=False):
        nc = self.nc
        seg = self.seg
        for o in self.ops:
            o.alld = [d for d in o.deps if d.seg == seg]
        ops = self.schedule(self.ops) if self.sched else self.ops
        self.est_log.append((seg, len(ops), round(self.est_ns / 1e3), self.est_busy))
        queues = {e: [] for e in (PE, ACT, DVE, POOL, SP)}
        for o in ops:
            queues[o.eng].append(o)
            keep = set()
            for d in o.deps:
                if d.seg != seg:
                    continue
                if d.dma:
                    keep.add(d)
                elif d.eng == o.eng and not o.dma:
                    if o.eng != PE:
                        keep.add(d)
                else:
                    keep.add(d)
            o.deps = keep
            for d in keep:
                d.need = True
        for e in COMPUTE:
            for o in reversed(queues[e]):
                if not o.dma:
                    o.need = True
                    break
        for o in ops:
            if o.dma:
                q = o.eng
                i = self.dnum[q] % self.n
                self.dnum[q] += 1
                prev = self.dlast[q][i]
                if prev is not None and prev.seg == seg:
                    o.deps.add(prev)
                self.dcnt[q][i] += 16
                o.sem = self.dsem[q][i]
                o.sig = self.dcnt[q][i]
                self.dlast[q][i] = o
            elif o.need:
                self.ecnt[o.eng] += 1
                o.sem = self.esem[o.eng]
                o.sig = self.ecnt[o.eng]
        with nc.Block() as block:
            def run(engname, eng):
                waited = {}
                for o in queues[engname]:
                    need = {}
                    for d in o.deps:
                        k = id(d.sem)
                        if waited.get(k, 0) >= d.sig:
                            continue
                        if k not in need or need[k][1] < d.sig:
                            need[k] = (d.sem, d.sig)
                    for k, (s, v) in need.items():
                        eng.wait_ge(s, v)
                        waited[k] = v
                    ins = o.fn(eng)
                    if o.sem is not None:
                        ins.then_inc(o.sem, 16 if o.dma else 1)
                for e in COMPUTE:
                    if e != engname and self.ecnt[e] > 0:
                        eng.wait_ge(self.esem[e], self.ecnt[e])
                for q in self.dsem:
                    for i in range(self.n):
                        if self.dcnt[q][i] > 0:
                            eng.wait_ge(self.dsem[q][i], self.dcnt[q][i])

            @block.sync
            def _(e):
                run(SP, e)

            @block.tensor
            def _(e):
                run(PE, e)

            @block.scalar
            def _(e):
                run(ACT, e)

            @block.vector
            def _(e):
                run(DVE, e)

            @block.gpsimd
            def _(e):
                run(POOL, e)
        self.ops = []
        self.seg += 1


def _fs(ap):
    n = 1
    for d in ap.shape[1:]:
        n *= int(d)
    return n


def _w(f, n):
    f.n = n
    return f


def A_(out, in_, func, **kw):
    f = _w(lambda e: e.activation(out=out, in_=in_, func=func, **kw), _fs(out))
    f.k = sum(1 for key in ("scale", "bias", "accum_out") if key in kw and not isinstance(kw[key], (int, float)))
    return f


def TS_(out, in0, s1, s2, op0, op1=None):
    if op1 is None:
        return _w(lambda e: e.tensor_scalar(out=out, in0=in0, scalar1=s1, scalar2=None, op0=op0), _fs(out))
    return _w(lambda e: e.tensor_scalar(out=out, in0=in0, scalar1=s1, scalar2=s2, op0=op0, op1=op1), _fs(out))


def TT_(out, in0, in1, op):
    f = _w(lambda e: e.tensor_tensor(out=out, in0=in0, in1=in1, op=op), _fs(out))
    f.k = 1
    return f


def STT_(out, in0, scalar, in1, op0, op1):
    f = _w(lambda e: e.scalar_tensor_tensor(out=out, in0=in0, scalar=scalar, in1=in1, op0=op0, op1=op1), _fs(out))
    f.k = 1
    return f


def CP_(out, in_):
    return _w(lambda e: e.tensor_copy(out, in_), _fs(out))


def MM_(out, lhsT, rhs, start, stop):
    return _w(lambda e: e.matmul(out, lhsT=lhsT, rhs=rhs, start=start, stop=stop), _fs(rhs))


def TR_(out, in_, ident):
    return _w(lambda e: e.transpose(out, in_, ident), 128)


def build(debug=False, stop_after=3):
    nc = bass.Bass("TRN2", target_bir_lowering=False)

    def din(name, shape, dt=F32):
        return nc.dram_tensor(name, list(shape), dt, kind="ExternalInput").ap()

    skind = "ExternalOutput" if debug else "Internal"

    def dscr(name, shape, dt):
        return nc.dram_tensor(name, list(shape), dt, kind=skind).ap()

    xs = din("xs", [S, D])
    cT = din("cT", [128, 8])
    w_ada = din("w_ada", [D, 6 * D])
    b_ada = din("b_ada", [1, 6 * D])
    n1g = din("n1g", [128, 8])
    n2g = din("n2g", [128, 8])
    mixg = din("mixg", [128, 8])
    w_in = din("w_in", [D, 2560])
    gqb = din("gqb", [128, 512])
    gkb = din("gkb", [128, 512])
    cw = din("cw", [128, 4, 4])
    cb = din("cb", [128, 4])
    wabd = din("wabd", [128, 4, 128])
    wxbd = din("wxbd", [128, 4, 128])
    lba = din("lba", [128, 4])
    lbx = din("lbx", [128, 4])
    lam = din("lam", [128, 4])
    w_out = din("w_out", [D, D])
    w_up = din("w_up", [D, 2 * DFF])
    fcw = din("fcw", [128, 44, 3])
    fcb = din("fcb", [128, 44])
    w_down = din("w_down", [DFF, D])
    flag = din("flag", [128, 1])
    pastb = din("pastb", [128, NOWN, 32])
    oh = din("oh", [32, S], BF16)
    identb = din("identb", [128, 128], BF16)
    identf = din("identf", [128, 128])
    onesb = din("onesb", [128, 128], BF16)
    maskd = din("maskd", [128, 2, 256], BF16)
    out = nc.dram_tensor("out", [4096, D], F32, kind="ExternalOutput").ap()

    KT_s = dscr("KT_s", [512, S], BF16)
    QT_s = dscr("QT_s", [512, NQ], BF16)
    NS_s = dscr("NS_s", [256, NQ], BF16)
    LR_s = dscr("LR_s", [512, NQ], BF16)
    V_s = dscr("V_s", [8, 64, 128, 64], BF16)
    AT_s = dscr("AT_s", [NQ, 512], F32)
    X1_s = dscr("X1_s", [NQ, D], F32) if debug else None

    with contextlib.ExitStack() as top:
        P = Prog(nc, top)

        def mk(stack):
            def sb(name, shape, dt=F32):
                return stack.enter_context(nc.sbuf_tensor(name, list(shape), dt))

            def ps(name, shape, dt=F32):
                return stack.enter_context(nc.psum_tensor(name, list(shape), dt))
            return sb, ps

        sbT, _ = mk(top)
        WOUT = sbT("WOUT", [128, 8, D], BF16); bWOUT = Buf()
        G2B = sbT("G2B", [128, D], BF16); bG2B = Buf()
        G1 = sbT("G1", [128, 8]); SH1 = sbT("SH1", [128, 8]); G2 = sbT("G2", [128, 8]); SH2 = sbT("SH2", [128, 8])
        bMOD = Buf()
        IDB = sbT("IDB", [128, 128], BF16); ONB = sbT("ONB", [128, 128], BF16); MKD = sbT("MKD", [128, 2, 256], BF16)
        FLG = sbT("FLG", [128, 1]); bCONST = Buf()
        CW = sbT("CW", [128, 4, 4]); CB = sbT("CB", [128, 4]); CL = sbT("CL", [128, 4]); CL2 = sbT("CL2", [128, 4])
        NBA = sbT("NBA", [128, 4]); NBX = sbT("NBX", [128, 4])
        WAB = sbT("WAB", [128, 4, 128], BF16); WXB = sbT("WXB", [128, 4, 128], BF16)
        FCW = sbT("FCW", [128, 44, 3]); FCB = sbT("FCB", [128, 44])

        sWI = contextlib.ExitStack()
        sbWI, _ = mk(sWI)
        WIN = sbWI("WIN", [128, 8, 2560], BF16); bWIN = Buf()
        with contextlib.ExitStack() as s0:
            sb, ps = mk(s0)
            WIS = [sb("WIS%d" % i, [128, 8, 320]) for i in range(2)]; bWIS = [Buf(), Buf()]
            CT = sb("CT", [128, 8]); CREP = sb("CREP", [128, 8, 128])
            WAS = [sb("WAS%d" % i, [128, 8, 512]) for i in range(3)]; bWAS = [Buf() for _ in range(3)]
            BB = [sb("BB%d" % i, [128, 512]) for i in range(3)]; bBB = [Buf() for _ in range(3)]
            MB = sb("MB", [128, 6 * D]); bMB = Buf()
            TMP = sb("TMP", [128, 48, 128]); bTMP = Buf()
            MT = sb("MT", [128, 48]); bMT = Buf()
            IDF = sb("IDF", [128, 128])
            N1G = sb("N1G", [128, 8]); N2G = sb("N2G", [128, 8]); MIXG = sb("MIXG", [128, 8])
            LBA = sb("LBA", [128, 4]); LBX = sb("LBX", [128, 4]); LAM = sb("LAM", [128, 4])
            WAF = sb("WAF", [128, 4, 128]); WXF = sb("WXF", [128, 4, 128])
            PS0 = [ps("PS0_%d" % i, [128, 512]) for i in range(2)]; bPS0 = [Buf(), Buf()]
            bC0 = Buf()
            for dst, src in ((CT, cT), (IDF, identf), (N1G, n1g), (N2G, n2g), (MIXG, mixg), (LBA, lba), (LBX, lbx),
                             (LAM, lam), (WAF, wabd), (WXF, wxbd), (IDB, identb), (ONB, onesb), (MKD, maskd),
                             (FLG, flag), (CW, cw), (CB, cb), (FCW, fcw), (FCB, fcb)):
                P.dma(SP, dst[:], src, writes=[bC0])
            P.op(DVE, CP_(CREP[:], CT[:, :].unsqueeze(2).to_broadcast([128, 8, 128])), reads=[bC0], writes=[bCONST])
            for g in range(12):
                i = g % 3
                P.dma(SP, WAS[i][:], w_ada[:, g * 512:(g + 1) * 512].rearrange("(kc p) n -> p kc n", p=128), writes=[bWAS[i]])
                P.dma(SP, BB[i][:], b_ada[0:1, g * 512:(g + 1) * 512].partition_broadcast(128)[:, 0, :], writes=[bBB[i]])
                for kc in range(8):
                    P.op(PE, MM_(PS0[g % 2][:], CREP[:, kc, :], WAS[i][:, kc, :], kc == 0, kc == 7),
                         reads=[bCONST, bWAS[i]], writes=[bPS0[g % 2]])
                P.op(DVE, TT_(MB[:, g * 512:(g + 1) * 512], PS0[g % 2][:], BB[i][:], ALU.add),
                     reads=[bPS0[g % 2], bBB[i]], writes=[bMB])
            for g in range(8):
                i = g % 2
                P.dma(SP, WIS[i][:], w_in[:, g * 320:(g + 1) * 320].rearrange("(kc p) n -> p kc n", p=128), writes=[bWIS[i]])
                P.op(DVE, CP_(WIN[:, :, g * 320:(g + 1) * 320], WIS[i][:]), reads=[bWIS[i]], writes=[bWIN])
            P.op(DVE, TT_(TMP[:], MB[:].rearrange("p (c j) -> p c j", j=128),
                          IDF[:, :].unsqueeze(1).to_broadcast([128, 48, 128]), ALU.mult),
                 reads=[bMB, bC0], writes=[bTMP])
            P.op(DVE, lambda e: e.tensor_reduce(out=MT[:], in_=TMP[:], axis=AX.X, op=ALU.add), reads=[bTMP], writes=[bMT])
            P.op(DVE, CP_(SH1[:], MT[:, 0:8]), reads=[bMT], writes=[bMOD])
            P.op(DVE, STT_(G1[:], MT[:, 8:16], 1.0, N1G[:], ALU.add, ALU.mult), reads=[bMT, bC0], writes=[bMOD])
            P.op(DVE, CP_(SH2[:], MT[:, 24:32]), reads=[bMT], writes=[bMOD])
            P.op(DVE, STT_(G2[:], MT[:, 32:40], 1.0, N2G[:], ALU.add, ALU.mult), reads=[bMT, bC0], writes=[bMOD])
            P.op(DVE, CP_(G2B[:], MB[:, 5 * D:6 * D]), reads=[bMB], writes=[bG2B])
            for nh in range(2):
                P.dma(SP, WAS[nh][:], w_out[:, nh * 512:(nh + 1) * 512].rearrange("(kc p) n -> p kc n", p=128), writes=[bWAS[nh]])
                for kc in range(8):
                    P.op(DVE, STT_(WOUT[:, kc, nh * 512:(nh + 1) * 512], WAS[nh][:, kc, :], MIXG[:, kc:kc + 1],
                                   MB[:, 2 * D + nh * 512:2 * D + (nh + 1) * 512], ALU.mult, ALU.mult),
                         reads=[bWAS[nh], bMB, bC0], writes=[bWOUT])
            P.op(ACT, A_(LAM[:], LAM[:], AF.Exp, scale=-1.0), reads=[bC0], writes=[bC0])
            P.op(ACT, A_(LAM[:], LAM[:], AF.Ln, bias=1.0), reads=[bC0], writes=[bC0])
            P.op(DVE, TS_(CL[:], LAM[:], -8.0, None, ALU.mult), reads=[bC0], writes=[bCONST])
            P.op(DVE, TS_(CL2[:], LAM[:], -16.0, None, ALU.mult), reads=[bC0], writes=[bCONST])
            P.op(DVE, TS_(NBA[:], LBA[:], -1.0, None, ALU.mult), reads=[bC0], writes=[bCONST])
            P.op(DVE, TS_(NBX[:], LBX[:], -1.0, None, ALU.mult), reads=[bC0], writes=[bCONST])
            P.op(DVE, CP_(WAB[:], WAF[:]), reads=[bC0], writes=[bCONST])
            P.op(DVE, CP_(WXB[:], WXF[:]), reads=[bC0], writes=[bCONST])
            P.emit_segment()

        with contextlib.ExitStack() as s1:
            if stop_after < 1:
                return nc
            sb, ps = mk(s1)
            PBT = sb("PBT", [128, NOWN, 32]); bPBT = Buf()
            GQB = sb("GQB", [128, 512]); GKB = sb("GKB", [128, 512])
            X = [sb("X%d" % i, [128, 2, D]) for i in range(2)]; bX = [Buf(), Buf()]
            J = sb("J", [128, D], BF16); bJ = Buf()
            ST = sb("ST", [128, 8]); bST = Buf()
            HN = sb("HN", [128, 2, D], BF16); bHN = Buf()
            HTd = [sb("HT%d" % i, [128, 8, 256], BF16) for i in range(2)]; bHTd = [Buf(), Buf()]
            SQ = sb("SQ", [128, 512]); bSQ = Buf()
            SK = sb("SK", [128, 24]); bSK = Buf()
            TK = sb("TK", [128, 512]); bTK = Buf()
            KN = sb("KN", [128, 512], BF16); bKN = Buf()
            KTt = sb("KTt", [128, 4, 256], BF16); bKTt = Buf()
            QTt = sb("QTt", [128, 4, 256], BF16); bQTt = Buf()
            KM = sb("KM", [128, 4, 2, 32], BF16); bKM = Buf()
            KMS = sb("KMS", [128, 4]); bKMS = Buf()
            VT = [sb("VT%d" % i, [128, 512], BF16) for i in range(2)]; bVT = [Buf(), Buf()]
            GB = sb("GB", [128, 8, 32]); bGB = Buf()
            M8 = sb("M8", [128, 8, 8]); bM8 = Buf()
            THR = sb("THR", [128, 8]); bTHR = Buf()
            NS = sb("NS", [128, 256], BF16); bNS = Buf()
            NSTt = sb("NSTt", [128, 2, 256], BF16); bNSTt = Buf()
            XR = sb("XR", [128, 4, 259]); bXR = Buf()
            XC = sb("XC", [128, 4, 256]); bXC = Buf()
            XCb = sb("XCb", [128, 4, 256], BF16); bXCb = Buf()
            R = sb("R", [128, 4, 256]); bR = Buf()
            I = sb("I", [128, 4, 256]); bI = Buf()
            AA = sb("AA", [128, 4, 256]); bAA = Buf()
            OM = sb("OM", [128, 4, 256]); bOM = Buf()
            H = sb("H", [128, 4, 256]); bH = Buf()
            HS = sb("HS", [128, 4]); bHS = Buf()
            T1 = sb("T1", [128, 4, 256]); bT1 = Buf()
            YSQ = sb("YSQ", [128, 4, 256], BF16); bYSQ = Buf()
            RS = sb("RS", [128, 256]); bRS = Buf()
            LRN = sb("LRN", [128, 4, 256], BF16); bLRN = Buf()
            PT1 = ps("PT1", [128, 4, 256], BF16); bPT1 = Buf()
            PF4 = ps("PF4", [128, 4, 256])
            PF = [PF4[:, 0:2, :], PF4[:, 2:4, :]]; bPF = [Buf(), Buf()]
            PK = [ps("PK%d" % i, [128, 512]) for i in range(2)]; bPK = [Buf(), Buf()]
            PM0 = ps("PM0", [128, 4, 256], BF16); bPM0 = Buf()
            PM1 = ps("PM1", [128, 512]); bPM1 = Buf()
            PRS = ps("PRS", [128, 512]); bPRS = Buf()

            P.dma(SP, PBT[:], pastb, writes=[bPBT])
            P.dma(SP, GQB[:], gqb, writes=[bPBT])
            P.dma(SP, GKB[:], gkb, writes=[bPBT])
            P.op(DVE, lambda e: e.memset(XR[:, :, 0:3], 0.0), writes=[bXR])
            P.op(DVE, lambda e: e.memset(HS[:], 0.0), writes=[bHS])
            P.op(DVE, lambda e: e.memset(KM[:], 0.0), writes=[bKM])

            def load_x(bi):
                P.dma(SP, X[bi % 2][:], xs[bi * 256:(bi + 1) * 256, :].rearrange("(t p) d -> p t d", p=128),
                      writes=[bX[bi % 2]])

            def norm_T(Xt, bXt, Gm, SHm, HT, bHT):
                for t in range(2):
                    P.op(ACT, A_(J[:], Xt[:, t, :], AF.Square, accum_out=ST[:, t:t + 1]), reads=[bXt], writes=[bJ, bST])
                P.op(ACT, A_(ST[:, 2:4], ST[:, 0:2], AF.Ln, scale=1.0 / D, bias=EPS), reads=[bST], writes=[bST])
                P.op(ACT, A_(ST[:, 4:6], ST[:, 2:4], AF.Exp, scale=-0.5), reads=[bST], writes=[bST])
                for t in range(2):
                    P.op(DVE, TS_(HN[:, t, :], Xt[:, t, :], ST[:, 4 + t:5 + t], None, ALU.mult), reads=[bXt, bST], writes=[bHN])
                for rnd in range(2):
                    for k4 in range(4):
                        kc = rnd * 4 + k4
                        for t in range(2):
                            P.op(PE, TR_(PT1[:, k4, t * 128:(t + 1) * 128], HN[:, t, kc * 128:(kc + 1) * 128], IDB[:]),
                                 reads=[bHN], writes=[bPT1])
                    for k4 in range(4):
                        kc = rnd * 4 + k4
                        P.op(DVE, TS_(HT[:, kc, :], PT1[:, k4, :], Gm[:, kc:kc + 1], SHm[:, kc:kc + 1], ALU.mult, ALU.add),
                             reads=[bPT1, bMOD], writes=[bHT])

            def headnorm(src_ps, bsrc, Gb, dst, bdst):
                P.op(ACT, A_(SQ[:], src_ps, AF.Square), reads=[bsrc], writes=[bSQ])
                P.op(DVE, lambda e: e.tensor_reduce(out=SK[:, 0:8], in_=SQ[:].rearrange("p (h d) -> p h d", h=8), axis=AX.X, op=ALU.add),
                     reads=[bSQ], writes=[bSK])
                P.op(ACT, A_(SK[:, 8:16], SK[:, 0:8], AF.Ln, scale=1.0 / 64, bias=EPS), reads=[bSK], writes=[bSK])
                P.op(ACT, A_(SK[:, 16:24], SK[:, 8:16], AF.Exp, scale=-0.5), reads=[bSK], writes=[bSK])
                P.op(DVE, TT_(TK[:].rearrange("p (h d) -> p h d", h=8), src_ps.rearrange("p (h d) -> p h d", h=8),
                              SK[:, 16:24].unsqueeze(2).to_broadcast([128, 8, 64]), ALU.mult), reads=[bsrc, bSK], writes=[bTK])
                P.op(DVE, TT_(dst, TK[:], Gb[:], ALU.mult), reads=[bTK, bPBT], writes=[bdst])

            pkc = [0]

            def chain1(bi, part=0):
                own = bi >= OWN0
                qi = bi - OWN0
                Xt, bXt = X[bi % 2], bX[bi % 2]
                HT, bHT = HTd[bi % 2], bHTd[bi % 2]
                if part in (0, 1):
                    norm_T(Xt, bXt, G1, SH1, HT, bHT)
                if part == 1:
                    return
                for t in range(2):
                    tile_idx = bi * 2 + t
                    bk = pkc[0] % 2; pkc[0] += 1
                    for kc in range(8):
                        P.op(PE, MM_(PK[bk][:], HT[:, kc, t * 128:(t + 1) * 128], WIN[:, kc, 512:1024], kc == 0, kc == 7),
                             reads=[bWIN, bHT], writes=[bPK[bk]])
                    headnorm(PK[bk][:], bPK[bk], GKB, KN[:], bKN)
                    for pr in range(4):
                        P.op(PE, TR_(PM0[:, pr, t * 128:(t + 1) * 128], KN[:, pr * 128:(pr + 1) * 128], IDB[:]), reads=[bKN], writes=[bPM0])
                    P.op(DVE, CP_(KTt[:, :, t * 128:(t + 1) * 128], PM0[:, :, t * 128:(t + 1) * 128]), reads=[bPM0], writes=[bKTt])
                    bv = pkc[0] % 2; pkc[0] += 1
                    for kc in range(8):
                        P.op(PE, MM_(PK[bv][:], HT[:, kc, t * 128:(t + 1) * 128], WIN[:, kc, 1024:1536], kc == 0, kc == 7),
                             reads=[bWIN, bHT], writes=[bPK[bv]])
                    P.op(ACT, A_(VT[t][:], PK[bv][:], AF.Copy), reads=[bPK[bv]], writes=[bVT[t]])
                    P.dma(POOL, V_s[:, tile_idx, :, :].rearrange("h p d -> p h d"), VT[t][:].rearrange("p (h d) -> p h d", h=8),
                          reads=[bVT[t]])
                P.dma(POOL, KT_s[:, bi * 256:(bi + 1) * 256].rearrange("(pr p) n -> p pr n", p=128), KTt[:], reads=[bKTt])
                P.op(DVE, lambda e: e.tensor_reduce(out=KMS[:], in_=KTt[:], axis=AX.X, op=ALU.add), reads=[bKTt], writes=[bKMS])
                if own:
                    for t in range(2):
                        bq = pkc[0] % 2; pkc[0] += 1
                        for kc in range(8):
                            P.op(PE, MM_(PK[bq][:], HT[:, kc, t * 128:(t + 1) * 128], WIN[:, kc, 0:512], kc == 0, kc == 7),
                                 reads=[bWIN, bHT], writes=[bPK[bq]])
                        headnorm(PK[bq][:], bPK[bq], GQB, KN[:], bKN)
                        for pr in range(4):
                            P.op(PE, TR_(PM0[:, pr, t * 128:(t + 1) * 128], KN[:, pr * 128:(pr + 1) * 128], IDB[:]), reads=[bKN], writes=[bPM0])
                        P.op(DVE, CP_(QTt[:, :, t * 128:(t + 1) * 128], PM0[:, :, t * 128:(t + 1) * 128]), reads=[bPM0], writes=[bQTt])
                        for pr in range(4):
                            P.op(PE, MM_(PM1[:, pr * 64:(pr + 1) * 64], QTt[:, pr, t * 128:(t + 1) * 128],
                                         KM[:, pr, :, :].rearrange("p s j -> p (s j)"), True, True), reads=[bQTt, bKM], writes=[bPM1])
                        P.op(DVE, TT_(GB[:], PM1[:, 0:256].rearrange("p (h j) -> p h j", h=8),
                                      PBT[:, qi, :].unsqueeze(1).to_broadcast([128, 8, 32]), ALU.add), reads=[bPM1, bPBT], writes=[bGB])
                        for h in range(8):
                            P.op(DVE, lambda e, h=h: e.max(out=M8[:, h, :], in_=GB[:, h, :]), reads=[bGB], writes=[bM8])
                        P.op(DVE, TS_(THR[:], M8[:, :, 2], -1e29, None, ALU.max), reads=[bM8], writes=[bTHR])
                        P.op(DVE, TT_(NS[:].rearrange("p (h j) -> p h j", h=8), GB[:],
                                      THR[:, :].unsqueeze(2).to_broadcast([128, 8, 32]), ALU.is_lt), reads=[bGB, bTHR], writes=[bNS])
                        for g in range(2):
                            P.op(PE, TR_(PT1[:, g, t * 128:(t + 1) * 128], NS[:, g * 128:(g + 1) * 128], IDB[:]), reads=[bNS], writes=[bPT1])
                        P.op(DVE, CP_(NSTt[:, :, t * 128:(t + 1) * 128], PT1[:, 0:2, t * 128:(t + 1) * 128]), reads=[bPT1], writes=[bNSTt])
                    P.dma(POOL, QT_s[:, qi * 256:(qi + 1) * 256].rearrange("(pr p) n -> p pr n", p=128), QTt[:], reads=[bQTt])
                    P.dma(POOL, NS_s[:, qi * 256:(qi + 1) * 256].rearrange("(g p) n -> p g n", p=128), NSTt[:], reads=[bNSTt])
                P.op(DVE, TS_(KM[0:64, :, 0, bi], KMS[0:64, :], 1.0 / 256, None, ALU.mult), reads=[bKMS], writes=[bKM])
                P.op(DVE, TS_(KM[64:128, :, 1, bi], KMS[64:128, :], 1.0 / 256, None, ALU.mult), reads=[bKMS], writes=[bKM])

            def chain2(bi):
                own = bi >= OWN0
                qi = bi - OWN0
                HT, bHT = HTd[bi % 2], bHTd[bi % 2]
                for c in range(4):
                    for kc in range(8):
                        P.op(PE, MM_(PF[c // 2][:, c % 2, :], WIN[:, kc, 1536 + c * 128:1536 + (c + 1) * 128], HT[:, kc, :], kc == 0, kc == 7),
                             reads=[bWIN, bHT], writes=[bPF[c // 2]])
                if bi == OWN0 + 1:
                    P.op(DVE, TS_(XR[:, :, 0:3], XR[:, :, 0:3], FLG[:, 0:1], None, ALU.mult), reads=[bXR, bCONST, bC0], writes=[bXR])
                    P.op(DVE, TS_(HS[:], HS[:], FLG[:, 0:1], None, ALU.mult), reads=[bHS, bC0], writes=[bHS])
                P.op(ACT, A_(XR[:, :, 3:259], PF4[:], AF.Copy), reads=[bPF[0], bPF[1]], writes=[bXR])
                for c in range(4):
                    P.op(ACT, A_(XC[:, c, :], XR[:, c, 0:256], AF.Identity, scale=CW[:, c, 0:1], bias=CB[:, c:c + 1]),
                         reads=[bXR, bC0], writes=[bXC])
                    for j in range(1, 4):
                        P.op(DVE, STT_(XC[:, c, :], XR[:, c, j:j + 256], CW[:, c, j:j + 1], XC[:, c, :], ALU.mult, ALU.add),
                             reads=[bXR, bXC, bC0], writes=[bXC])
                P.op(DVE, CP_(XR[:, :, 0:3], XR[:, :, 256:259]), reads=[bXR], writes=[bXR])
                P.op(DVE, CP_(XCb[:], XC[:]), reads=[bXC], writes=[bXCb])
                for (Wb, NBb, dst, bdst) in ((WAB, NBA, R, bR), (WXB, NBX, I, bI)):
                    for c in range(4):
                        P.op(PE, MM_(PF[c // 2][:, c % 2, :], Wb[:, c, :], XCb[:, c, :], True, True),
                             reads=[bCONST, bXCb], writes=[bPF[c // 2]])
                    for c in range(4):
                        P.op(ACT, A_(dst[:, c, :], PF[c // 2][:, c % 2, :], AF.Exp, scale=-1.0, bias=NBb[:, c:c + 1]),
                             reads=[bPF[c // 2], bCONST], writes=[bdst])
                    P.op(ACT, A_(dst[:], dst[:], AF.Ln, bias=1.0), reads=[bdst], writes=[bdst])
                    P.op(ACT, A_(dst[:], dst[:], AF.Exp, scale=-1.0), reads=[bdst], writes=[bdst])
                    if dst is I:
                        P.op(DVE, TT_(I[:], I[:], XC[:], ALU.mult), reads=[bI, bXC], writes=[bI])
                P.op(DVE, TT_(R[:], R[:], CL[:, :].unsqueeze(2).to_broadcast([128, 4, 256]), ALU.mult), reads=[bR, bCONST], writes=[bR])
                P.op(ACT, A_(AA[:], R[:], AF.Exp), reads=[bR], writes=[bAA])
                P.op(ACT, A_(OM[:], R[:], AF.Exp, scale=2.0), reads=[bR], writes=[bOM])
                P.op(DVE, TS_(OM[:], OM[:], -1.0, 1.0, ALU.mult, ALU.add), reads=[bOM], writes=[bOM])
                P.op(ACT, A_(OM[:], OM[:], AF.Ln), reads=[bOM], writes=[bOM])
                P.op(ACT, A_(OM[:], OM[:], AF.Exp, scale=0.5), reads=[bOM], writes=[bOM])
                P.op(DVE, TT_(OM[:], OM[:], I[:], ALU.mult), reads=[bOM, bI], writes=[bOM])
                for c in range(4):
                    P.op(DVE, lambda e, c=c: e.tensor_tensor_scan(out=H[:, c, :], data0=AA[:, c, :], data1=OM[:, c, :],
                                                                   initial=HS[:, c:c + 1], op0=ALU.mult, op1=ALU.add),
                         reads=[bAA, bOM, bHS], writes=[bH])
                P.op(DVE, CP_(HS[:], H[:, :, 255]), reads=[bH], writes=[bHS])
                if own:
                    for c in range(4):
                        for kc in range(8):
                            P.op(PE, MM_(PF[c // 2][:, c % 2, :], WIN[:, kc, 2048 + c * 128:2048 + (c + 1) * 128], HT[:, kc, :], kc == 0, kc == 7),
                                 reads=[bWIN, bHT], writes=[bPF[c // 2]])
                    for c2 in range(2):
                        cs = slice(2 * c2, 2 * c2 + 2)
                        P.op(ACT, A_(T1[:, cs, :], PF[c2][:], AF.Square), reads=[bPF[c2]], writes=[bT1])
                        P.op(DVE, TS_(T1[:, cs, :], T1[:, cs, :], 0.044715, 1.0, ALU.mult, ALU.add), reads=[bT1], writes=[bT1])
                        P.op(DVE, TT_(T1[:, cs, :], T1[:, cs, :], PF[c2][:], ALU.mult), reads=[bT1, bPF[c2]], writes=[bT1])
                        P.op(ACT, A_(T1[:, cs, :], T1[:, cs, :], AF.Exp, scale=-1.5957691216), reads=[bT1], writes=[bT1])
                        P.op(ACT, A_(T1[:, cs, :], T1[:, cs, :], AF.Ln, bias=1.0), reads=[bT1], writes=[bT1])
                        P.op(ACT, A_(T1[:, cs, :], T1[:, cs, :], AF.Exp, scale=-1.0), reads=[bT1], writes=[bT1])
                        P.op(DVE, TT_(T1[:, cs, :], T1[:, cs, :], PF[c2][:], ALU.mult), reads=[bT1, bPF[c2]], writes=[bT1])
                    P.op(DVE, TT_(T1[:], T1[:], H[:], ALU.mult), reads=[bT1, bH], writes=[bT1])
                    P.op(ACT, A_(YSQ[:], T1[:], AF.Square), reads=[bT1], writes=[bYSQ])
                    for c in range(4):
                        P.op(PE, MM_(PRS[:, 0:256], ONB[:], YSQ[:, c, :], c == 0, c == 3), reads=[bYSQ, bC0], writes=[bPRS])
                    P.op(ACT, A_(RS[:], PRS[:, 0:256], AF.Ln, scale=1.0 / 512, bias=EPS), reads=[bPRS], writes=[bRS])
                    P.op(ACT, A_(RS[:], RS[:], AF.Exp, scale=-0.5), reads=[bRS], writes=[bRS])
                    P.op(DVE, TT_(LRN[:], T1[:], RS[:, :].unsqueeze(1).to_broadcast([128, 4, 256]), ALU.mult), reads=[bT1, bRS], writes=[bLRN])
                    P.dma(POOL, LR_s[:, qi * 256:(qi + 1) * 256].rearrange("(c p) n -> p c n", p=128), LRN[:], reads=[bLRN])

            load_x(0)
            load_x(1)
            chain1(0, 1)
            for bi in range(NB):
                if bi + 1 < NB:
                    chain1(bi + 1, 1)
                chain2(bi)
                chain1(bi, 2)
                if bi + 2 < NB:
                    load_x(bi + 2)
            P.emit_segment()
        sWI.close()

        sW = top.enter_context(contextlib.ExitStack())
        sbW, _ = mk(sW)
        WUP = sbW("WUP", [128, 8, 2 * DFF], BF16); bWUP = Buf()
        with contextlib.ExitStack() as s2:
            if stop_after < 2:
                return nc
            sb, ps = mk(s2)
            WSW = [sb("WSW%d" % i, [128, 512]) for i in range(4)]; bWSW = [Buf() for _ in range(4)]
            KX = [sb("KX%d" % i, [96, S], BF16) for i in range(2)]; bKX = [Buf(), Buf()]
            QX = [sb("QX%d" % i, [96, NQ], BF16) for i in range(2)]; bQX = [Buf(), Buf()]
            VX = [sb("VX%d" % i, [128, 64, 65], BF16) for i in range(2)]; bVX = [Buf(), Buf()]
            PP = [sb("PP%d" % i, [128, 2, 256], BF16) for i in range(3)]; bPP = [Buf() for _ in range(3)]
            AO = [sb("AO%d" % i, [128, 2, 64]) for i in range(2)]; bAO = [Buf(), Buf()]
            RD = sb("RD", [128, 4]); bRD = Buf()
            SB_ = [ps("SB%d" % i, [128, 2, 256]) for i in range(3)]; bSB = [Buf() for _ in range(3)]
            OB = [[ps("OB%d_%d" % (i, q), [128, 512]) for q in range(2)] for i in range(2)]
            bOB = [[Buf(), Buf()], [Buf(), Buf()]]
            for i in range(2):
                P.dma(SP, KX[i][64:96, :], oh, writes=[bKX[i]])
                P.op(DVE, lambda e, i=i: e.memset(VX[i][:, :, 64:65], 1.0), writes=[bVX[i]])

            def load_head(h):
                i = h % 2
                P.dma(SP, KX[i][0:64, :], KT_s[h * 64:(h + 1) * 64, :], writes=[bKX[i]])
                P.dma(SP, QX[i][0:64, :], QT_s[h * 64:(h + 1) * 64, :], writes=[bQX[i]])
                P.dma(SP, QX[i][64:96, :], NS_s[h * 32:(h + 1) * 32, :], writes=[bQX[i]])
                P.dma(SP, VX[i][:, :, 0:64], V_s[h].rearrange("t p d -> p t d"), writes=[bVX[i]])

            its = []
            oi = 0
            for h in range(8):
                for qi in range(NOWN):
                    bi = OWN0 + qi
                    ob = oi % 2; oi += 1
                    for j in range(bi + 1):
                        its.append((h, qi, bi, ob, j))

            def emit_qk(n):
                h, qi, bi, ob, j = its[n]
                hb = h % 2
                sbk = n % 3
                qc = slice(qi * 256, (qi + 1) * 256)
                diag = (j == bi)
                for kh in range(2):
                    kc_ = slice(j * 256 + kh * 128, j * 256 + (kh + 1) * 128)
                    if not diag:
                        P.op(PE, MM_(SB_[sbk][:, kh, :], KX[hb][0:96, kc_], QX[hb][0:96, qc], True, True),
                             reads=[bKX[hb], bQX[hb]], writes=[bSB[sbk]])
                    else:
                        P.op(PE, MM_(SB_[sbk][:, kh, :], KX[hb][0:64, kc_], QX[hb][0:64, qc], True, False),
                             reads=[bKX[hb], bQX[hb]], writes=[bSB[sbk]])
                        P.op(PE, MM_(SB_[sbk][:, kh, :], IDB[:], MKD[:, kh, :], False, True),
                             reads=[bC0], writes=[bSB[sbk]])

            def emit_rest(n):
                h, qi, bi, ob, j = its[n]
                hb = h % 2
                sbk = n % 3
                nj = bi + 1
                P.op(ACT, A_(PP[sbk][:], SB_[sbk][:], AF.Exp, scale=0.125), reads=[bSB[sbk]], writes=[bPP[sbk]])
                if n + 2 < len(its):
                    emit_qk(n + 2)
                for kh in range(2):
                    for qh in range(2):
                        P.op(PE, MM_(OB[ob][qh][:, 0:65], PP[sbk][:, kh, qh * 128:(qh + 1) * 128], VX[hb][:, j * 2 + kh, :],
                                     (j == 0 and kh == 0), (j == nj - 1 and kh == 1)),
                             reads=[bPP[sbk], bVX[hb]], writes=[bOB[ob][qh]])
                if j == nj - 1:
                    for qh in range(2):
                        P.op(DVE, lambda e, ob=ob, qh=qh: e.reciprocal(RD[:, ob * 2 + qh:ob * 2 + qh + 1], OB[ob][qh][:, 64:65]),
                             reads=[bOB[ob][qh]], writes=[bRD])
                        P.op(DVE, TS_(AO[ob][:, qh, :], OB[ob][qh][:, 0:64], RD[:, ob * 2 + qh:ob * 2 + qh + 1], None, ALU.mult),
                             reads=[bOB[ob][qh], bRD], writes=[bAO[ob]])
                    P.dma(POOL, AT_s[qi * 256:(qi + 1) * 256, h * 64:(h + 1) * 64].rearrange("(qh p) d -> p qh d", p=128), AO[ob][:],
                          reads=[bAO[ob]])

            load_head(0)
            load_head(1)
            def wup_piece(g):
                i = g % 4
                P.dma(SP, WSW[i][:].rearrange("p (kc n) -> p kc n", kc=8),
                      w_up[:, g * 64:(g + 1) * 64].rearrange("(kc p) n -> p kc n", p=128), writes=[bWSW[i]])
                P.op(DVE, CP_(WUP[:, :, g * 64:(g + 1) * 64], WSW[i][:].rearrange("p (kc n) -> p kc n", kc=8)),
                     reads=[bWSW[i]], writes=[bWUP])
            wnext = [0]
            emit_qk(0)
            emit_qk(1)
            for n in range(len(its)):
                h, qi, bi, ob, j = its[n]
                if n > 0 and its[n - 1][0] != h and h + 1 < 8:
                    load_head(h + 1)
                if n % 24 == 12 and wnext[0] < 88:
                    wup_piece(wnext[0]); wnext[0] += 1
                emit_rest(n)
            while wnext[0] < 88:
                wup_piece(wnext[0]); wnext[0] += 1
            P.sched = False
            P.emit_segment()
            P.sched = True

        with contextlib.ExitStack() as s3:
            if stop_after < 3:
                return nc
            sb, ps = mk(s3)
            WDN = sb("WDN", [128, 22, D], BF16); bWDN = Buf()
            XD = [sb("XT3_%d" % i, [128, 2, D]) for i in range(2)]; bXD = [Buf(), Buf()]
            AT = sb("AT", [128, 512]); bAT = Buf()
            ST = sb("ST3", [128, 8]); bST = Buf()
            AN = sb("AN", [128, 512], BF16); bAN = Buf()
            HTD = [sb("HTD%d" % i, [128, 8, 256], BF16) for i in range(2)]; bHTD = [Buf(), Buf()]
            HN = sb("HN3", [128, D], BF16); bHN = Buf()
            UB = [sb("UB%d" % i, [128, 2, 258]) for i in range(3)]; bUB = [Buf() for _ in range(3)]; bUBh = [Buf() for _ in range(3)]
            ACC = [sb("ACC%d" % i, [128, 2, 256]) for i in range(3)]; bACC = [Buf() for _ in range(3)]
            WS3 = [ACC[i][:].rearrange("p a n -> p (a n)") for i in range(3)]; bWS3 = bACC
            UH = sb("UH", [128, 44, 2]); bUH = Buf()
            ACTT = sb("ACTT", [128, 22, 256], BF16); bACTT = Buf(); bACTTm = [Buf() for _ in range(22)]
            PY = [ps("PY%d" % i, [128, 512]) for i in range(4)]; bPY = [Buf() for _ in range(4)]
            PT1 = ps("PT3", [128, 4, 256], BF16); bPT1 = Buf()
            PU = [ps("PU%d" % i, [128, 2, 256]) for i in range(3)]; bPU = [Buf() for _ in range(3)]

            ACTF = ACTT[:].rearrange("p m n -> p (m n)").bitcast(F32)
            STG = [(WS3[0], bACC[0]), (WS3[1], bACC[1]), (WS3[2], bACC[2])] + [(ACTF[:, k * 512:(k + 1) * 512], Buf()) for k in range(5)]
            wi = 0
            for m in range(22):
                for nh in range(2):
                    stg, bstg = STG[wi % len(STG)]; wi += 1
                    P.dma(SP, stg, w_down[m * 128:(m + 1) * 128, nh * 512:(nh + 1) * 512], writes=[bstg])
                    P.op(DVE, TT_(WDN[:, m, nh * 512:(nh + 1) * 512], stg, G2B[:, nh * 512:(nh + 1) * 512], ALU.mult),
                         reads=[bstg, bG2B], writes=[bWDN])
            P.op(DVE, _w(lambda e: e.memset(UH[:], 0.0), 88), reads=[b for _, b in STG[3:]], writes=[bUH] + bACTTm)

            upc = [0]

            def front(qi):
                bi = OWN0 + qi
                Xt, bXt = XD[qi % 2], bXD[qi % 2]
                MIXT = HT = HTD[qi % 2]
                bMIXL = bMIXA = bHT = bHTD[qi % 2]
                P.dma(SP, Xt[:], xs[bi * 256:(bi + 1) * 256, :].rearrange("(t p) d -> p t d", p=128), writes=[bXt])
                P.dma(SP, MIXT[:, 0:4, :], LR_s[:, qi * 256:(qi + 1) * 256].rearrange("(c p) n -> p c n", p=128), writes=[bMIXL])
                for t in range(2):
                    P.dma(SP, AT[:], AT_s[qi * 256 + t * 128:qi * 256 + (t + 1) * 128, :], writes=[bAT])
                    P.op(ACT, A_(AN[:], AT[:], AF.Square, accum_out=ST[:, 0:1]), reads=[bAT], writes=[bAN, bST])
                    P.op(ACT, A_(ST[:, 2:3], ST[:, 0:1], AF.Ln, scale=1.0 / 512, bias=EPS), reads=[bST], writes=[bST])
                    P.op(ACT, A_(ST[:, 4:5], ST[:, 2:3], AF.Exp, scale=-0.5), reads=[bST], writes=[bST])
                    P.op(DVE, TS_(AN[:], AT[:], ST[:, 4:5], None, ALU.mult), reads=[bAT, bST], writes=[bAN])
                    for c in range(4):
                        P.op(PE, TR_(PT1[:, c, t * 128:(t + 1) * 128], AN[:, c * 128:(c + 1) * 128], IDB[:]), reads=[bAN], writes=[bPT1])
                P.op(DVE, CP_(MIXT[:, 4:8, :], PT1[:]), reads=[bPT1], writes=[bMIXA])
                for t in range(2):
                    for nh in range(2):
                        pb = t * 2 + nh
                        for kc in range(8):
                            P.op(PE, MM_(PY[pb][:], MIXT[:, kc, t * 128:(t + 1) * 128], WOUT[:, kc, nh * 512:(nh + 1) * 512], kc == 0, kc == 7),
                                 reads=[bMIXL, bMIXA, bWOUT], writes=[bPY[pb]])
                        P.op(DVE, TT_(Xt[:, t, nh * 512:(nh + 1) * 512], PY[pb][:], Xt[:, t, nh * 512:(nh + 1) * 512], ALU.add),
                             reads=[bPY[pb], bXt], writes=[bXt])
                if debug:
                    P.dma(POOL, X1_s[qi * 256:(qi + 1) * 256, :].rearrange("(t p) d -> p t d", p=128), Xt[:], reads=[bXt])
                for t in range(2):
                    P.op(ACT, A_(HN[:], Xt[:, t, :], AF.Square, accum_out=ST[:, t:t + 1]), reads=[bXt], writes=[bHN, bST])
                P.op(ACT, A_(ST[:, 2:4], ST[:, 0:2], AF.Ln, scale=1.0 / D, bias=EPS), reads=[bST], writes=[bST])
                P.op(ACT, A_(ST[:, 4:6], ST[:, 2:4], AF.Exp, scale=-0.5), reads=[bST], writes=[bST])
                HT2 = HT[:].rearrange("p k (t n) -> p k t n", t=2)
                for t in range(2):
                    P.op(DVE, TS_(HN[:], Xt[:, t, :], ST[:, 4 + t:5 + t], None, ALU.mult), reads=[bXt, bST], writes=[bHN])
                    for rnd in range(2):
                        for k4 in range(4):
                            kc = rnd * 4 + k4
                            P.op(PE, TR_(PT1[:, k4, 0:128], HN[:, kc * 128:(kc + 1) * 128], IDB[:]), reads=[bHN], writes=[bPT1])
                        for k4 in range(4):
                            kc = rnd * 4 + k4
                            P.op(DVE, TS_(HT[:, kc, t * 128:(t + 1) * 128], PT1[:, k4, 0:128], G2[:, kc:kc + 1], SH2[:, kc:kc + 1], ALU.mult, ALU.add),
                                 reads=[bPT1, bMOD], writes=[bHT])

            def rest(qi):
                bi = OWN0 + qi
                Xt, bXt = XD[qi % 2], bXD[qi % 2]
                HT = HTD[qi % 2]
                bHT = bHTD[qi % 2]
                if qi == 1:
                    P.op(DVE, TS_(UH[:], UH[:], FLG[:, 0:1], None, ALU.mult), reads=[bUH, bC0], writes=[bUH])
                for m in range(22):
                    u = upc[0] % 3; upc[0] += 1
                    bPU_, bUB_, bACC_ = bPU, bUB, bACC
                    for half, mm in ((0, m), (1, m + 22)):
                        for kc in range(8):
                            P.op(PE, MM_(PU[u][:, half, :], WUP[:, kc, mm * 128:(mm + 1) * 128], HT[:, kc, :], kc == 0, kc == 7),
                                 reads=[bWUP, bHT], writes=[bPU_[u]])
                    if qi > 0:
                        for half, mm in ((0, m), (1, m + 22)):
                            P.op(POOL, CP_(UB[u][:, half, 0:2], UH[:, mm, :]), reads=[bUH], writes=[bUBh[u]])
                    P.op(ACT, A_(UB[u][:, :, 2:258], PU[u][:], AF.Copy), reads=[bPU_[u]], writes=[bUB_[u]])
                    for half, mm in ((0, m), (1, m + 22)):
                        P.op(POOL, CP_(UH[:, mm, :], UB[u][:, half, 256:258]), reads=[bUB_[u]], writes=[bUH])
                    if qi == 0:
                        continue
                    for half, mm in ((0, m), (1, m + 22)):
                        P.op(ACT, A_(ACC[u][:, half, :], UB[u][:, half, 0:256], AF.Identity, scale=FCW[:, mm, 0:1], bias=FCB[:, mm:mm + 1]),
                             reads=[bUB_[u], bUBh[u], bC0], writes=[bACC_[u]])
                        for jj in (1, 2):
                            P.op(DVE, STT_(ACC[u][:, half, :], UB[u][:, half, jj:jj + 256], FCW[:, mm, jj:jj + 1], ACC[u][:, half, :], ALU.mult, ALU.add),
                                 reads=[bUB_[u], bUBh[u], bACC_[u], bC0], writes=[bACC_[u]])
                    P.op(ACT, A_(ACC[u][:, 0, :], ACC[u][:, 0, :], AF.Silu), reads=[bACC_[u]], writes=[bACC_[u]])
                    P.op(POOL, TT_(ACTT[:, m, :], ACC[u][:, 0, :], ACC[u][:, 1, :], ALU.mult), reads=[bACC_[u]], writes=[bACTTm[m]])
                if qi == 0:
                    return
                for m in range(22):
                    for t in range(2):
                        for nh in range(2):
                            pb = t * 2 + nh
                            P.op(PE, MM_(PY[pb][:], ACTT[:, m, t * 128:(t + 1) * 128], WDN[:, m, nh * 512:(nh + 1) * 512], m == 0, m == 21),
                                 reads=[bACTTm[m], bWDN], writes=[bPY[pb]])
                for t in range(2):
                    for nh in range(2):
                        pb = t * 2 + nh
                        P.op(DVE, TT_(Xt[:, t, nh * 512:(nh + 1) * 512], PY[pb][:], Xt[:, t, nh * 512:(nh + 1) * 512], ALU.add),
                             reads=[bPY[pb], bXt], writes=[bXt])
                P.dma(POOL, out[(qi - 1) * 256:qi * 256, :].rearrange("(t p) d -> p t d", p=128), Xt[:], reads=[bXt])

            front(0)
            for qi in range(NOWN):
                if qi + 1 < NOWN:
                    front(qi + 1)
                rest(qi)
            P.emit_segment(last=True)
            nc._est_log = P.est_log
    return nc


def make_in_maps(x, c, w_ada, b_ada, norm1_g, w_in, q_norm_g, k_norm_g, lru_conv_w, lru_conv_b,
                 lru_wa, lru_ba, lru_wx, lru_bx, lru_lambda, lru_out_g, attn_out_g, w_out,
                 norm2_g, w_up, ffn_conv_w, ffn_conv_b, w_down):
    f = lambda a: np.ascontiguousarray(np.asarray(a, dtype=np.float32))
    x = f(x); c = f(c)

    def fm(v, nch):
        return f(np.asarray(v, np.float32).reshape(nch, 128).T)

    def bd(w):
        w = np.asarray(w, np.float32)
        o = np.zeros((128, 4, 128), np.float32)
        for ch in range(4):
            for s_ in range(2):
                o[s_ * 64:(s_ + 1) * 64, ch, s_ * 64:(s_ + 1) * 64] = w[ch * 2 + s_]
        return o

    shared = {
        "w_ada": f(w_ada[0]), "b_ada": f(b_ada[0]).reshape(1, -1),
        "n1g": fm(norm1_g[0], 8), "n2g": fm(norm2_g[0], 8),
        "mixg": fm(np.concatenate([np.asarray(lru_out_g[0]), np.asarray(attn_out_g[0])]), 8),
        "w_in": f(w_in[0]),
        "gqb": f(np.broadcast_to(np.tile(np.asarray(q_norm_g[0], np.float32), 8), (128, 512))),
        "gkb": f(np.broadcast_to(np.tile(np.asarray(k_norm_g[0], np.float32), 8), (128, 512))),
        "cw": f(np.asarray(lru_conv_w[0], np.float32).reshape(4, 4, 128).transpose(2, 1, 0)),
        "cb": fm(lru_conv_b[0], 4),
        "wabd": bd(lru_wa[0]), "wxbd": bd(lru_wx[0]),
        "lba": fm(lru_ba[0], 4), "lbx": fm(lru_bx[0], 4), "lam": fm(lru_lambda[0], 4),
        "w_out": f(w_out[0]), "w_up": f(w_up[0]),
        "fcw": f(np.asarray(ffn_conv_w[0], np.float32).reshape(3, 44, 128).transpose(2, 1, 0)),
        "fcb": fm(ffn_conv_b[0], 44),
        "w_down": f(w_down[0]),
        "identb": np.eye(128, dtype=np.float32).astype(ml_dtypes.bfloat16),
        "identf": np.eye(128, dtype=np.float32),
        "onesb": np.ones((128, 128), np.float32).astype(ml_dtypes.bfloat16),
    }
    ohm = np.zeros((32, S), np.float32)
    for j in range(32):
        ohm[j, j * 256:(j + 1) * 256] = NEG
    shared["oh"] = ohm.astype(ml_dtypes.bfloat16)
    kk = np.arange(128)[:, None]
    qq = np.arange(128)[None, :]
    tri = np.where(kk <= qq, 0.0, NEG).astype(np.float32)
    md = np.zeros((128, 2, 256), np.float32)
    md[:, 0, 0:128] = tri
    md[:, 1, 0:128] = NEG
    md[:, 1, 128:256] = tri
    shared["maskd"] = md.astype(ml_dtypes.bfloat16)
    in_maps = []
    for core in range(8):
        b, half = core // 2, core % 2
        if half == 1:
            xsv = x[b]
        else:
            xsv = np.concatenate([np.zeros((4096, D), np.float32), x[b, :4096]], axis=0)
        pb = np.full((NOWN, 32), -1e30, np.float32)
        for qi in range(NOWN):
            bi = OWN0 + qi
            lo = 16 if half == 0 else 0
            if bi > lo:
                pb[qi, lo:bi] = 0.0
        m = dict(shared)
        m["xs"] = f(xsv)
        m["cT"] = fm(c[b], 8)
        m["flag"] = np.full((128, 1), float(half), np.float32)
        m["pastb"] = f(np.broadcast_to(pb[None], (128, NOWN, 32)))
        in_maps.append(m)
    return in_maps


_NC = {}


def kernel(**inputs):
    in_maps = make_in_maps(**inputs)
    if "nc" not in _NC:
        _NC["nc"] = build(False)
    res = run_bass_kernel_spmd(_NC["nc"], in_maps, core_ids=list(range(8)))
    outp = np.zeros((4, S, D), np.float32)
    for core in range(8):
        b, half = core // 2, core % 2
        outp[b, half * 4096:(half + 1) * 4096] = res.results[core]["out"]
    return outp
```

```python
import contextlib
import numpy as np
import ml_dtypes
import concourse.bass as bass
import concourse.mybir as mybir
from concourse.bass_utils import run_bass_kernel_spmd

F32 = mybir.dt.float32
BF16 = mybir.dt.bfloat16
AF = mybir.ActivationFunctionType
ALU = mybir.AluOpType
AX = mybir.AxisListType

PE, ACT, DVE, POOL, SP = "tensor", "scalar", "vector", "gpsimd", "sync"
COMPUTE = (PE, ACT, DVE, POOL)

D = 1024
S = 8192
NB = 32
OWN0 = 15
NOWN = NB - OWN0
NQ = NOWN * 256
DFF = 2816
NEG = -30000.0
EPS = 1e-6


class Buf:
    __slots__ = ("name", "w", "rs")

    def __init__(self, name=""):
        self.name = name
        self.w = None
        self.rs = []


class Op:
    __slots__ = ("eng", "fn", "deps", "dma", "sig", "sem", "need", "seg", "alld", "cost", "lat", "line")

    def __init__(self, eng, fn, dma, seg):
        self.eng = eng
        self.fn = fn
        self.dma = dma
        self.deps = set()
        self.sig = None
        self.sem = None
        self.need = False
        self.seg = seg


class Prog:
    def __init__(self, nc, stack, n_dma_sems=16):
        self.nc = nc
        self.n = n_dma_sems
        self.seg = 0
        self.ops = []
        self.esem = {e: stack.enter_context(nc.semaphore("s_" + e)) for e in COMPUTE}
        self.dsem = {q: [stack.enter_context(nc.semaphore("d_%s_%d" % (q, i))) for i in range(n_dma_sems)]
                     for q in (SP, POOL)}
        self.ecnt = {e: 0 for e in COMPUTE}
        self.dcnt = {q: [0] * n_dma_sems for q in self.dsem}
        self.dlast = {q: [None] * n_dma_sems for q in self.dsem}
        self.dnum = {q: 0 for q in self.dsem}
        self.finals = []
        self.rec = None
        self.sched = True
        self.est_ns = 0.0
        self.est_log = []

    def chain(self, fn):
        self.rec = []
        fn()
        lst, self.rec = self.rec, None
        return lst

    def merge(self, *chains):
        items = []
        for ci, ch in enumerate(chains):
            n = len(ch)
            for i, it in enumerate(ch):
                items.append(((i + 0.5) / n, ci, i, it))
        items.sort(key=lambda x: (x[0], x[1], x[2]))
        for _, _, _, it in items:
            self.op(*it)

    def op(self, eng, fn, reads=(), writes=(), dma=False):
        if self.rec is not None:
            self.rec.append((eng, fn, tuple(reads), tuple(writes), dma))
            return None
        o = Op(eng, fn, dma, self.seg)
        for b in reads:
            if b.w is not None:
                o.deps.add(b.w)
        for b in writes:
            if b.w is not None:
                o.deps.add(b.w)
            for r in b.rs:
                o.deps.add(r)
        for b in reads:
            b.rs.append(o)
        for b in writes:
            b.w = o
            b.rs = []
        o.deps.discard(o)
        import sys as _sys
        fr = _sys._getframe(1)
        if fr.f_code.co_name in ('dma', 'merge'):
            fr = fr.f_back
        o.line = fr.f_lineno
        n = getattr(fn, "n", 256)
        k = getattr(fn, "k", 0)
        if dma:
            o.cost = 60.0
            o.lat = 2200.0 + n / 60.0
        elif eng == PE:
            o.cost = 45.0 + 0.37 * n
            o.lat = o.cost + 200.0
        elif eng == ACT:
            o.cost = 130.0 + 0.83 * n + 120.0 * k
            o.lat = o.cost + 120.0
        elif eng == POOL:
            o.cost = 250.0 + 2.0 * n
            o.lat = o.cost + 150.0
        else:
            o.cost = 150.0 + (1.05 if k else 0.6) * n
            o.lat = o.cost + 120.0
        self.ops.append(o)
        return o

    def dma(self, eng, out, in_, reads=(), writes=()):
        nbytes = 1
        for d in out.shape:
            nbytes *= int(d)
        nbytes *= 2 if out.dtype == BF16 else 4
        f = lambda e: e.dma_start(out=out, in_=in_)
        f.n = nbytes
        return self.op(eng, f, reads, writes, dma=True)

    def schedule(self, ops):
        import bisect
        n = len(ops)
        pos = {id(o): i for i, o in enumerate(ops)}
        succ = [[] for _ in range(n)]
        indeg = [0] * n
        for i, o in enumerate(ops):
            for d in o.alld:
                j = pos.get(id(d))
                if j is None:
                    continue
                succ[j].append(i)
                indeg[i] += 1
        rt = [0.0] * n
        why = [None] * n
        stt = [0.0] * n
        lastop = {}
        avail = {e: [] for e in (PE, ACT, DVE, POOL, SP)}
        free = {e: 0.0 for e in avail}
        for i, o in enumerate(ops):
            if indeg[i] == 0:
                avail[o.eng].append(i)
        order = []
        WIN_ = 40
        done = 0
        while done < n:
            best = None
            for e, lst in avail.items():
                if not lst:
                    continue
                fe = free[e]
                cb = None
                for i in lst[:WIN_]:
                    st = rt[i] if rt[i] > fe else fe
                    if cb is None or st < cb[0]:
                        cb = (st, i)
                if best is None or cb < best[0:2]:
                    best = (cb[0], cb[1], e)
            st, i, e = best
            lst = avail[e]
            lst.pop(bisect.bisect_left(lst, i))
            o = ops[i]
            stt[i] = st
            if rt[i] < st and e in lastop:
                why[i] = ('eng', lastop[e])
            lastop[e] = i
            free[e] = st + o.cost
            fin = st + o.lat
            order.append(o)
            done += 1
            for k in succ[i]:
                if fin > rt[k]:
                    rt[k] = fin
                    if why[k] is None or why[k][0] == 'dep':
                        why[k] = ('dep', i)
                indeg[k] -= 1
                if indeg[k] == 0:
                    bisect.insort(avail[ops[k].eng], k)
        self.est_ns = max(free.values())
        busy = {}
        for o in ops:
            busy[o.eng] = busy.get(o.eng, 0.0) + o.cost
        self.est_busy = {k: round(v / 1e3) for k, v in busy.items()}
        return order

    def emit_segment(self, last=False):
        nc = self.nc
        seg = self.seg
        for o in self.ops:
            o.alld = [d for d in o.deps if d.seg == seg]
        ops = self.schedule(self.ops) if self.sched else self.ops
        self.est_log.append((seg, len(ops), round(self.est_ns / 1e3), self.est_busy))
        queues = {e: [] for e in (PE, ACT, DVE, POOL, SP)}
        for o in ops:
            queues[o.eng].append(o)
            keep = set()
            for d in o.deps:
                if d.seg != seg:
                    continue
                if d.dma:
                    keep.add(d)
                elif d.eng == o.eng and not o.dma:
                    if o.eng != PE:
                        keep.add(d)
                else:
                    keep.add(d)
            o.deps = keep
            for d in keep:
                d.need = True
        for e in COMPUTE:
            for o in reversed(queues[e]):
                if not o.dma:
                    o.need = True
                    break
        for o in ops:
            if o.dma:
                q = o.eng
                i = self.dnum[q] % self.n
                self.dnum[q] += 1
                prev = self.dlast[q][i]
                if prev is not None and prev.seg == seg:
                    o.deps.add(prev)
                self.dcnt[q][i] += 16
                o.sem = self.dsem[q][i]
                o.sig = self.dcnt[q][i]
                self.dlast[q][i] = o
            elif o.need:
                self.ecnt[o.eng] += 1
                o.sem = self.esem[o.eng]
                o.sig = self.ecnt[o.eng]
        with nc.Block() as block:
            def run(engname, eng):
                waited = {}
                for o in queues[engname]:
                    need = {}
                    for d in o.deps:
                        k = id(d.sem)
                        if waited.get(k, 0) >= d.sig:
                            continue
                        if k not in need or need[k][1] < d.sig:
                            need[k] = (d.sem, d.sig)
                    for k, (s, v) in need.items():
                        eng.wait_ge(s, v)
                        waited[k] = v
                    ins = o.fn(eng)
                    if o.sem is not None:
                        ins.then_inc(o.sem, 16 if o.dma else 1)
                for e in COMPUTE:
                    if e != engname and self.ecnt[e] > 0:
                        eng.wait_ge(self.esem[e], self.ecnt[e])
                for q in self.dsem:
                    for i in range(self.n):
                        if self.dcnt[q][i] > 0:
                            eng.wait_ge(self.dsem[q][i], self.dcnt[q][i])

            @block.sync
            def _(e):
                run(SP, e)

            @block.tensor
            def _(e):
                run(PE, e)

            @block.scalar
            def _(e):
                run(ACT, e)

            @block.vector
            def _(e):
                run(DVE, e)

            @block.gpsimd
            def _(e):
                run(POOL, e)
        self.ops = []
        self.seg += 1


def _fs(ap):
    n = 1
    for d in ap.shape[1:]:
        n *= int(d)
    return n


def _w(f, n):
    f.n = n
    return f


def A_(out, in_, func, **kw):
    f = _w(lambda e: e.activation(out=out, in_=in_, func=func, **kw), _fs(out))
    f.k = sum(1 for key in ("scale", "bias", "accum_out") if key in kw and not isinstance(kw[key], (int, float)))
    return f


def TS_(out, in0, s1, s2, op0, op1=None):
    if op1 is None:
        return _w(lambda e: e.tensor_scalar(out=out, in0=in0, scalar1=s1, scalar2=None, op0=op0), _fs(out))
    return _w(lambda e: e.tensor_scalar(out=out, in0=in0, scalar1=s1, scalar2=s2, op0=op0, op1=op1), _fs(out))


def TT_(out, in0, in1, op):
    f = _w(lambda e: e.tensor_tensor(out=out, in0=in0, in1=in1, op=op), _fs(out))
    f.k = 1
    return f


def STT_(out, in0, scalar, in1, op0, op1):
    f = _w(lambda e: e.scalar_tensor_tensor(out=out, in0=in0, scalar=scalar, in1=in1, op0=op0, op1=op1), _fs(out))
    f.k = 1
    return f


def CP_(out, in_):
    return _w(lambda e: e.tensor_copy(out, in_), _fs(out))


def MM_(out, lhsT, rhs, start, stop):
    return _w(lambda e: e.matmul(out, lhsT=lhsT, rhs=rhs, start=start, stop=stop), _fs(rhs))


def TR_(out, in_, ident):
    return _w(lambda e: e.transpose(out, in_, ident), 128)


def build(debug=False, stop_after=3):
    nc = bass.Bass("TRN2", target_bir_lowering=False)

    def din(name, shape, dt=F32):
        return nc.dram_tensor(name, list(shape), dt, kind="ExternalInput").ap()

    skind = "ExternalOutput" if debug else "Internal"

    def dscr(name, shape, dt):
        return nc.dram_tensor(name, list(shape), dt, kind=skind).ap()

    xs = din("xs", [S, D])
    cT = din("cT", [128, 8])
    w_ada = din("w_ada", [D, 6 * D])
    b_ada = din("b_ada", [1, 6 * D])
    n1g = din("n1g", [128, 8])
    n2g = din("n2g", [128, 8])
    mixg = din("mixg", [128, 8])
    w_in = din("w_in", [D, 2560])
    gqp = din("gqp", [128, 1])
    gkp = din("gkp", [128, 1])
    cw = din("cw", [128, 4, 4])
    cb = din("cb", [128, 4])
    wabd = din("wabd", [128, 4, 128])
    wxbd = din("wxbd", [128, 4, 128])
    lba = din("lba", [128, 4])
    lbx = din("lbx", [128, 4])
    lam = din("lam", [128, 4])
    w_out = din("w_out", [D, D])
    w_up = din("w_up", [D, 2 * DFF])
    fcw = din("fcw", [128, 44, 3])
    fcb = din("fcb", [128, 44])
    w_down = din("w_down", [DFF, D])
    flag = din("flag", [128, 1])
    pastb = din("pastb", [128, NOWN, 32])
    oh = din("oh", [32, S], BF16)
    identb = din("identb", [128, 128], BF16)
    identf = din("identf", [128, 128])
    onesb = din("onesb", [128, 128], BF16)
    maskd = din("maskd", [128, 2, 256], BF16)
    out = nc.dram_tensor("out", [4096, D], F32, kind="ExternalOutput").ap()

    KT_s = dscr("KT_s", [512, S], BF16)
    QT_s = dscr("QT_s", [512, NQ], BF16)
    NS_s = dscr("NS_s", [256, NQ], BF16)
    LR_s = dscr("LR_s", [512, NQ], BF16)
    V_s = dscr("V_s", [8, 64, 128, 64], BF16)
    AT_s = dscr("AT_s", [NQ, 512], F32)
    X1_s = dscr("X1_s", [NQ, D], F32) if debug else None

    with contextlib.ExitStack() as top:
        P = Prog(nc, top)

        def mk(stack):
            def sb(name, shape, dt=F32):
                return stack.enter_context(nc.sbuf_tensor(name, list(shape), dt))

            def ps(name, shape, dt=F32):
                return stack.enter_context(nc.psum_tensor(name, list(shape), dt))
            return sb, ps

        sbT, _ = mk(top)
        WOUT = sbT("WOUT", [128, 8, D], BF16); bWOUT = Buf()
        G2B = sbT("G2B", [128, D], BF16); bG2B = Buf()
        G1 = sbT("G1", [128, 8]); SH1 = sbT("SH1", [128, 8]); G2 = sbT("G2", [128, 8]); SH2 = sbT("SH2", [128, 8])
        bMOD = Buf()
        IDB = sbT("IDB", [128, 128], BF16); ONB = sbT("ONB", [128, 128], BF16); MKD = sbT("MKD", [128, 2, 256], BF16)
        FLG = sbT("FLG", [128, 1]); bCONST = Buf()
        CW = sbT("CW", [128, 4, 4]); CB = sbT("CB", [128, 4]); CL = sbT("CL", [128, 4]); CL2 = sbT("CL2", [128, 4])
        NBA = sbT("NBA", [128, 4]); NBX = sbT("NBX", [128, 4])
        WAB = sbT("WAB", [128, 4, 128], BF16); WXB = sbT("WXB", [128, 4, 128], BF16)
        FCW = sbT("FCW", [128, 44, 3]); FCB = sbT("FCB", [128, 44])

        sWI = contextlib.ExitStack()
        sbWI, _ = mk(sWI)
        WIN = sbWI("WIN", [128, 8, 2560], BF16); bWIN = Buf()
        with contextlib.ExitStack() as s0:
            sb, ps = mk(s0)
            WIS = [sb("WIS%d" % i, [128, 8, 320]) for i in range(2)]; bWIS = [Buf(), Buf()]
            CT = sb("CT", [128, 8]); CREP = sb("CREP", [128, 8, 128])
            WAS = [sb("WAS%d" % i, [128, 8, 512]) for i in range(3)]; bWAS = [Buf() for _ in range(3)]
            BB = [sb("BB%d" % i, [128, 512]) for i in range(3)]; bBB = [Buf() for _ in range(3)]
            MB = sb("MB", [128, 6 * D]); bMB = Buf()
            TMP = sb("TMP", [128, 48, 128]); bTMP = Buf()
            MT = sb("MT", [128, 48]); bMT = Buf()
            IDF = sb("IDF", [128, 128])
            N1G = sb("N1G", [128, 8]); N2G = sb("N2G", [128, 8]); MIXG = sb("MIXG", [128, 8])
            LBA = sb("LBA", [128, 4]); LBX = sb("LBX", [128, 4]); LAM = sb("LAM", [128, 4])
            WAF = sb("WAF", [128, 4, 128]); WXF = sb("WXF", [128, 4, 128])
            PS0 = [ps("PS0_%d" % i, [128, 512]) for i in range(2)]; bPS0 = [Buf(), Buf()]
            bC0 = Buf()
            for dst, src in ((CT, cT), (IDF, identf), (N1G, n1g), (N2G, n2g), (MIXG, mixg), (LBA, lba), (LBX, lbx),
                             (LAM, lam), (WAF, wabd), (WXF, wxbd), (IDB, identb), (ONB, onesb), (MKD, maskd),
                             (FLG, flag), (CW, cw), (CB, cb), (FCW, fcw), (FCB, fcb)):
                P.dma(SP, dst[:], src, writes=[bC0])
            P.op(DVE, CP_(CREP[:], CT[:, :].unsqueeze(2).to_broadcast([128, 8, 128])), reads=[bC0], writes=[bCONST])
            for g in range(12):
                i = g % 3
                P.dma(SP, WAS[i][:], w_ada[:, g * 512:(g + 1) * 512].rearrange("(kc p) n -> p kc n", p=128), writes=[bWAS[i]])
                P.dma(SP, BB[i][:], b_ada[0:1, g * 512:(g + 1) * 512].partition_broadcast(128)[:, 0, :], writes=[bBB[i]])
                for kc in range(8):
                    P.op(PE, MM_(PS0[g % 2][:], CREP[:, kc, :], WAS[i][:, kc, :], kc == 0, kc == 7),
                         reads=[bCONST, bWAS[i]], writes=[bPS0[g % 2]])
                P.op(DVE, TT_(MB[:, g * 512:(g + 1) * 512], PS0[g % 2][:], BB[i][:], ALU.add),
                     reads=[bPS0[g % 2], bBB[i]], writes=[bMB])
            for g in range(8):
                i = g % 2
                P.dma(SP, WIS[i][:], w_in[:, g * 320:(g + 1) * 320].rearrange("(kc p) n -> p kc n", p=128), writes=[bWIS[i]])
                P.op(DVE, CP_(WIN[:, :, g * 320:(g + 1) * 320], WIS[i][:]), reads=[bWIS[i]], writes=[bWIN])
            P.op(DVE, TT_(TMP[:], MB[:].rearrange("p (c j) -> p c j", j=128),
                          IDF[:, :].unsqueeze(1).to_broadcast([128, 48, 128]), ALU.mult),
                 reads=[bMB, bC0], writes=[bTMP])
            P.op(DVE, lambda e: e.tensor_reduce(out=MT[:], in_=TMP[:], axis=AX.X, op=ALU.add), reads=[bTMP], writes=[bMT])
            P.op(DVE, CP_(SH1[:], MT[:, 0:8]), reads=[bMT], writes=[bMOD])
            P.op(DVE, STT_(G1[:], MT[:, 8:16], 1.0, N1G[:], ALU.add, ALU.mult), reads=[bMT, bC0], writes=[bMOD])
            P.op(DVE, CP_(SH2[:], MT[:, 24:32]), reads=[bMT], writes=[bMOD])
            P.op(DVE, STT_(G2[:], MT[:, 32:40], 1.0, N2G[:], ALU.add, ALU.mult), reads=[bMT, bC0], writes=[bMOD])
            P.op(DVE, CP_(G2B[:], MB[:, 5 * D:6 * D]), reads=[bMB], writes=[bG2B])
            for nh in range(2):
                P.dma(SP, WAS[nh][:], w_out[:, nh * 512:(nh + 1) * 512].rearrange("(kc p) n -> p kc n", p=128), writes=[bWAS[nh]])
                for kc in range(8):
                    P.op(DVE, STT_(WOUT[:, kc, nh * 512:(nh + 1) * 512], WAS[nh][:, kc, :], MIXG[:, kc:kc + 1],
                                   MB[:, 2 * D + nh * 512:2 * D + (nh + 1) * 512], ALU.mult, ALU.mult),
                         reads=[bWAS[nh], bMB, bC0], writes=[bWOUT])
            P.op(ACT, A_(LAM[:], LAM[:], AF.Exp, scale=-1.0), reads=[bC0], writes=[bC0])
            P.op(ACT, A_(LAM[:], LAM[:], AF.Ln, bias=1.0), reads=[bC0], writes=[bC0])
            P.op(DVE, TS_(CL[:], LAM[:], -8.0, None, ALU.mult), reads=[bC0], writes=[bCONST])
            P.op(DVE, TS_(CL2[:], LAM[:], -16.0, None, ALU.mult), reads=[bC0], writes=[bCONST])
            P.op(DVE, TS_(NBA[:], LBA[:], -1.0, None, ALU.mult), reads=[bC0], writes=[bCONST])
            P.op(DVE, TS_(NBX[:], LBX[:], -1.0, None, ALU.mult), reads=[bC0], writes=[bCONST])
            P.op(DVE, CP_(WAB[:], WAF[:]), reads=[bC0], writes=[bCONST])
            P.op(DVE, CP_(WXB[:], WXF[:]), reads=[bC0], writes=[bCONST])
            P.emit_segment()

        with contextlib.ExitStack() as s1:
            if stop_after < 1:
                return nc
            sb, ps = mk(s1)
            PBT = sb("PBT", [128, NOWN, 32]); bPBT = Buf()
            GQP = sb("GQP", [128, 1]); GKP = sb("GKP", [128, 1])
            X = [sb("X%d" % i, [128, 2, D]) for i in range(2)]; bX = [Buf(), Buf()]
            J = sb("J", [128, D], BF16); bJ = Buf()
            ST = sb("ST", [128, 8]); bST = Buf()
            HN = sb("HN", [128, 2, D], BF16); bHN = Buf()
            HTd = [sb("HT%d" % i, [128, 8, 256], BF16) for i in range(2)]; bHTd = [Buf(), Buf()]
            SQ = sb("SQ", [128, 512]); bSQ = Buf()
            SK = sb("SK", [128, 24]); bSK = Buf()
            TK = sb("TK", [128, 512]); bTK = Buf()
            KN = sb("KN", [128, 512], BF16); bKN = Buf()
            KTt = sb("KTt", [128, 4, 256], BF16); bKTt = Buf()
            QTt = sb("QTt", [128, 4, 256], BF16); bQTt = Buf()
            KM = sb("KM", [128, 4, 2, 32], BF16); bKM = Buf()
            KMS = sb("KMS", [128, 4]); bKMS = Buf()
            VT = [sb("VT%d" % i, [128, 512], BF16) for i in range(2)]; bVT = [Buf(), Buf()]
            GB = sb("GB", [128, 8, 32]); bGB = Buf()
            M8 = sb("M8", [128, 8, 8]); bM8 = Buf()
            THR = sb("THR", [128, 8]); bTHR = Buf()
            NS = sb("NS", [128, 256], BF16); bNS = Buf()
            NSTt = sb("NSTt", [128, 2, 256], BF16); bNSTt = Buf()
            XR = sb("XR", [128, 4, 259]); bXR = Buf()
            XC = sb("XC", [128, 4, 256]); bXC = Buf()
            XCb = sb("XCb", [128, 4, 256], BF16); bXCb = Buf()
            R = sb("R", [128, 4, 256]); bR = Buf()
            I = sb("I", [128, 4, 256]); bI = Buf()
            AA = sb("AA", [128, 4, 256]); bAA = Buf()
            OM = sb("OM", [128, 4, 256]); bOM = Buf()
            H = sb("H", [128, 4, 256]); bH = Buf()
            HS = sb("HS", [128, 4]); bHS = Buf()
            T1 = sb("T1", [128, 4, 256]); bT1 = Buf()
            YSQ = sb("YSQ", [128, 4, 256], BF16); bYSQ = Buf()
            RS = sb("RS", [128, 256]); bRS = Buf()
            LRN = sb("LRN", [128, 4, 256], BF16); bLRN = Buf()
            PT1 = ps("PT1", [128, 4, 256], BF16); bPT1 = Buf()
            PF4 = ps("PF4", [128, 4, 256])
            PF = [PF4[:, 0:2, :], PF4[:, 2:4, :]]; bPF = [Buf(), Buf()]
            PK = [ps("PK%d" % i, [128, 512]) for i in range(2)]; bPK = [Buf(), Buf()]
            PM0 = ps("PM0", [128, 4, 256], BF16); bPM0 = Buf()
            PM1 = ps("PM1", [128, 512]); bPM1 = Buf()
            PRS = ps("PRS", [128, 512]); bPRS = Buf()

            P.dma(SP, PBT[:], pastb, writes=[bPBT])
            P.dma(SP, GQP[:], gqp, writes=[bPBT])
            P.dma(SP, GKP[:], gkp, writes=[bPBT])
            P.op(DVE, lambda e: e.memset(XR[:, :, 0:3], 0.0), writes=[bXR])
            P.op(DVE, lambda e: e.memset(HS[:], 0.0), writes=[bHS])
            P.op(DVE, lambda e: e.memset(KM[:], 0.0), writes=[bKM])

            def load_x(bi):
                P.dma(SP, X[bi % 2][:], xs[bi * 256:(bi + 1) * 256, :].rearrange("(t p) d -> p t d", p=128),
                      writes=[bX[bi % 2]])

            def norm_T(Xt, bXt, Gm, SHm, HT, bHT):
                for t in range(2):
                    P.op(ACT, A_(J[:], Xt[:, t, :], AF.Square, accum_out=ST[:, t:t + 1]), reads=[bXt], writes=[bJ, bST])
                P.op(ACT, A_(ST[:, 2:4], ST[:, 0:2], AF.Ln, scale=1.0 / D, bias=EPS), reads=[bST], writes=[bST])
                P.op(ACT, A_(ST[:, 4:6], ST[:, 2:4], AF.Exp, scale=-0.5), reads=[bST], writes=[bST])
                for t in range(2):
                    P.op(DVE, TS_(HN[:, t, :], Xt[:, t, :], ST[:, 4 + t:5 + t], None, ALU.mult), reads=[bXt, bST], writes=[bHN])
                for rnd in range(2):
                    for k4 in range(4):
                        kc = rnd * 4 + k4
                        for t in range(2):
                            P.op(PE, TR_(PT1[:, k4, t * 128:(t + 1) * 128], HN[:, t, kc * 128:(kc + 1) * 128], IDB[:]),
                                 reads=[bHN], writes=[bPT1])
                    for k4 in range(4):
                        kc = rnd * 4 + k4
                        P.op(DVE, TS_(HT[:, kc, :], PT1[:, k4, :], Gm[:, kc:kc + 1], SHm[:, kc:kc + 1], ALU.mult, ALU.add),
                             reads=[bPT1, bMOD], writes=[bHT])

            def headnorm(src_ps, bsrc, Gb, dst, bdst):
                P.op(ACT, A_(SQ[:], src_ps, AF.Square), reads=[bsrc], writes=[bSQ])
                P.op(DVE, lambda e: e.tensor_reduce(out=SK[:, 0:8], in_=SQ[:].rearrange("p (h d) -> p h d", h=8), axis=AX.X, op=ALU.add),
                     reads=[bSQ], writes=[bSK])
                P.op(ACT, A_(SK[:, 8:16], SK[:, 0:8], AF.Ln, scale=1.0 / 64, bias=EPS), reads=[bSK], writes=[bSK])
                P.op(ACT, A_(SK[:, 16:24], SK[:, 8:16], AF.Exp, scale=-0.5), reads=[bSK], writes=[bSK])
                P.op(DVE, TT_(dst.rearrange("p (h d) -> p h d", h=8), src_ps.rearrange("p (h d) -> p h d", h=8),
                              SK[:, 16:24].unsqueeze(2).to_broadcast([128, 8, 64]), ALU.mult), reads=[bsrc, bSK], writes=[bdst])

            pkc = [0]

            def chain1(bi, part=0):
                own = bi >= OWN0
                qi = bi - OWN0
                Xt, bXt = X[bi % 2], bX[bi % 2]
                HT, bHT = HTd[bi % 2], bHTd[bi % 2]
                if part in (0, 1):
                    norm_T(Xt, bXt, G1, SH1, HT, bHT)
                if part == 1:
                    return
                for t in range(2):
                    tile_idx = bi * 2 + t
                    bk = pkc[0] % 2; pkc[0] += 1
                    for kc in range(8):
                        P.op(PE, MM_(PK[bk][:], HT[:, kc, t * 128:(t + 1) * 128], WIN[:, kc, 512:1024], kc == 0, kc == 7),
                             reads=[bWIN, bHT], writes=[bPK[bk]])
                    headnorm(PK[bk][:], bPK[bk], None, KN[:], bKN)
                    for pr in range(4):
                        P.op(PE, TR_(PM0[:, pr, t * 128:(t + 1) * 128], KN[:, pr * 128:(pr + 1) * 128], IDB[:]), reads=[bKN], writes=[bPM0])
                    P.op(DVE, TS_(KTt[:, :, t * 128:(t + 1) * 128], PM0[:, :, t * 128:(t + 1) * 128], GKP[:, 0:1], None, ALU.mult), reads=[bPM0, bPBT], writes=[bKTt])
                    bv = pkc[0] % 2; pkc[0] += 1
                    for kc in range(8):
                        P.op(PE, MM_(PK[bv][:], HT[:, kc, t * 128:(t + 1) * 128], WIN[:, kc, 1024:1536], kc == 0, kc == 7),
                             reads=[bWIN, bHT], writes=[bPK[bv]])
                    P.op(ACT, A_(VT[t][:], PK[bv][:], AF.Copy), reads=[bPK[bv]], writes=[bVT[t]])
                    P.dma(POOL, V_s[:, tile_idx, :, :].rearrange("h p d -> p h d"), VT[t][:].rearrange("p (h d) -> p h d", h=8),
                          reads=[bVT[t]])
                P.dma(POOL, KT_s[:, bi * 256:(bi + 1) * 256].rearrange("(pr p) n -> p pr n", p=128), KTt[:], reads=[bKTt])
                P.op(DVE, lambda e: e.tensor_reduce(out=KMS[:], in_=KTt[:], axis=AX.X, op=ALU.add), reads=[bKTt], writes=[bKMS])
                if own:
                    for t in range(2):
                        bq = pkc[0] % 2; pkc[0] += 1
                        for kc in range(8):
                            P.op(PE, MM_(PK[bq][:], HT[:, kc, t * 128:(t + 1) * 128], WIN[:, kc, 0:512], kc == 0, kc == 7),
                                 reads=[bWIN, bHT], writes=[bPK[bq]])
                        headnorm(PK[bq][:], bPK[bq], None, KN[:], bKN)
                        for pr in range(4):
                            P.op(PE, TR_(PM0[:, pr, t * 128:(t + 1) * 128], KN[:, pr * 128:(pr + 1) * 128], IDB[:]), reads=[bKN], writes=[bPM0])
                        P.op(DVE, TS_(QTt[:, :, t * 128:(t + 1) * 128], PM0[:, :, t * 128:(t + 1) * 128], GQP[:, 0:1], None, ALU.mult), reads=[bPM0, bPBT], writes=[bQTt])
                        for pr in range(4):
                            P.op(PE, MM_(PM1[:, pr * 64:(pr + 1) * 64], QTt[:, pr, t * 128:(t + 1) * 128],
                                         KM[:, pr, :, :].rearrange("p s j -> p (s j)"), True, True), reads=[bQTt, bKM], writes=[bPM1])
                        P.op(DVE, TT_(GB[:], PM1[:, 0:256].rearrange("p (h j) -> p h j", h=8),
                                      PBT[:, qi, :].unsqueeze(1).to_broadcast([128, 8, 32]), ALU.add), reads=[bPM1, bPBT], writes=[bGB])
                        for h in range(8):
                            P.op(DVE, lambda e, h=h: e.max(out=M8[:, h, :], in_=GB[:, h, :]), reads=[bGB], writes=[bM8])
                        P.op(DVE, TS_(THR[:], M8[:, :, 2], -1e29, None, ALU.max), reads=[bM8], writes=[bTHR])
                        P.op(DVE, TT_(NS[:].rearrange("p (h j) -> p h j", h=8), GB[:],
                                      THR[:, :].unsqueeze(2).to_broadcast([128, 8, 32]), ALU.is_lt), reads=[bGB, bTHR], writes=[bNS])
                        for g in range(2):
                            P.op(PE, TR_(PT1[:, g, t * 128:(t + 1) * 128], NS[:, g * 128:(g + 1) * 128], IDB[:]), reads=[bNS], writes=[bPT1])
                        P.op(DVE, CP_(NSTt[:, :, t * 128:(t + 1) * 128], PT1[:, 0:2, t * 128:(t + 1) * 128]), reads=[bPT1], writes=[bNSTt])
                    P.dma(POOL, QT_s[:, qi * 256:(qi + 1) * 256].rearrange("(pr p) n -> p pr n", p=128), QTt[:], reads=[bQTt])
                    P.dma(POOL, NS_s[:, qi * 256:(qi + 1) * 256].rearrange("(g p) n -> p g n", p=128), NSTt[:], reads=[bNSTt])
                P.op(DVE, TS_(KM[0:64, :, 0, bi], KMS[0:64, :], 1.0 / 256, None, ALU.mult), reads=[bKMS], writes=[bKM])
                P.op(DVE, TS_(KM[64:128, :, 1, bi], KMS[64:128, :], 1.0 / 256, None, ALU.mult), reads=[bKMS], writes=[bKM])

            def chain2(bi):
                own = bi >= OWN0
                qi = bi - OWN0
                HT, bHT = HTd[bi % 2], bHTd[bi % 2]
                for c in range(4):
                    for kc in range(8):
                        P.op(PE, MM_(PF[c // 2][:, c % 2, :], WIN[:, kc, 1536 + c * 128:1536 + (c + 1) * 128], HT[:, kc, :], kc == 0, kc == 7),
                             reads=[bWIN, bHT], writes=[bPF[c // 2]])
                if bi == OWN0 + 1:
                    P.op(DVE, TS_(XR[:, :, 0:3], XR[:, :, 0:3], FLG[:, 0:1], None, ALU.mult), reads=[bXR, bCONST, bC0], writes=[bXR])
                    P.op(DVE, TS_(HS[:], HS[:], FLG[:, 0:1], None, ALU.mult), reads=[bHS, bC0], writes=[bHS])
                P.op(ACT, A_(XR[:, :, 3:259], PF4[:], AF.Copy), reads=[bPF[0], bPF[1]], writes=[bXR])
                for c in range(4):
                    P.op(ACT, A_(XC[:, c, :], XR[:, c, 0:256], AF.Identity, scale=CW[:, c, 0:1], bias=CB[:, c:c + 1]),
                         reads=[bXR, bC0], writes=[bXC])
                    for j in range(1, 3):
                        P.op(DVE, STT_(XC[:, c, :], XR[:, c, j:j + 256], CW[:, c, j:j + 1], XC[:, c, :], ALU.mult, ALU.add),
                             reads=[bXR, bXC, bC0], writes=[bXC])
                    P.op(DVE, STT_(XCb[:, c, :], XR[:, c, 3:259], CW[:, c, 3:4], XC[:, c, :], ALU.mult, ALU.add),
                         reads=[bXR, bXC, bC0], writes=[bXCb])
                for c in range(4):
                    P.op(DVE, STT_(XC[:, c, :], XR[:, c, 3:259], CW[:, c, 3:4], XC[:, c, :], ALU.mult, ALU.add),
                         reads=[bXR, bXC, bC0], writes=[bXC])
                P.op(DVE, CP_(XR[:, :, 0:3], XR[:, :, 256:259]), reads=[bXR], writes=[bXR])
                for (Wb, NBb, dst, bdst) in ((WAB, NBA, R, bR), (WXB, NBX, I, bI)):
                    for c in range(4):
                        P.op(PE, MM_(PF[c // 2][:, c % 2, :], Wb[:, c, :], XCb[:, c, :], True, True),
                             reads=[bCONST, bXCb], writes=[bPF[c // 2]])
                    for c in range(4):
                        P.op(ACT, A_(dst[:, c, :], PF[c // 2][:, c % 2, :], AF.Exp, scale=-1.0, bias=NBb[:, c:c + 1]),
                             reads=[bPF[c // 2], bCONST], writes=[bdst])
                    P.op(ACT, A_(dst[:], dst[:], AF.Ln, bias=1.0), reads=[bdst], writes=[bdst])
                    P.op(ACT, A_(dst[:], dst[:], AF.Exp, scale=-1.0), reads=[bdst], writes=[bdst])
                    if dst is I:
                        P.op(DVE, TT_(I[:], I[:], XC[:], ALU.mult), reads=[bI, bXC], writes=[bI])
                P.op(DVE, TT_(R[:], R[:], CL[:, :].unsqueeze(2).to_broadcast([128, 4, 256]), ALU.mult), reads=[bR, bCONST], writes=[bR])
                P.op(ACT, A_(AA[:], R[:], AF.Exp), reads=[bR], writes=[bAA])
                P.op(ACT, A_(OM[:], R[:], AF.Exp, scale=2.0), reads=[bR], writes=[bOM])
                P.op(DVE, TS_(OM[:], OM[:], -1.0, 1.0, ALU.mult, ALU.add), reads=[bOM], writes=[bOM])
                P.op(ACT, A_(OM[:], OM[:], AF.Ln), reads=[bOM], writes=[bOM])
                P.op(ACT, A_(OM[:], OM[:], AF.Exp, scale=0.5), reads=[bOM], writes=[bOM])
                P.op(DVE, TT_(OM[:], OM[:], I[:], ALU.mult), reads=[bOM, bI], writes=[bOM])
                for c in range(4):
                    P.op(DVE, lambda e, c=c: e.tensor_tensor_scan(out=H[:, c, :], data0=AA[:, c, :], data1=OM[:, c, :],
                                                                   initial=HS[:, c:c + 1], op0=ALU.mult, op1=ALU.add),
                         reads=[bAA, bOM, bHS], writes=[bH])
                P.op(DVE, CP_(HS[:], H[:, :, 255]), reads=[bH], writes=[bHS])
                if own:
                    for c in range(4):
                        for kc in range(8):
                            P.op(PE, MM_(PF[c // 2][:, c % 2, :], WIN[:, kc, 2048 + c * 128:2048 + (c + 1) * 128], HT[:, kc, :], kc == 0, kc == 7),
                                 reads=[bWIN, bHT], writes=[bPF[c // 2]])
                    for c2 in range(2):
                        cs = slice(2 * c2, 2 * c2 + 2)
                        P.op(ACT, A_(T1[:, cs, :], PF[c2][:], AF.Square), reads=[bPF[c2]], writes=[bT1])
                        P.op(DVE, TS_(T1[:, cs, :], T1[:, cs, :], 0.044715, 1.0, ALU.mult, ALU.add), reads=[bT1], writes=[bT1])
                        P.op(DVE, TT_(T1[:, cs, :], T1[:, cs, :], PF[c2][:], ALU.mult), reads=[bT1, bPF[c2]], writes=[bT1])
                        P.op(ACT, A_(T1[:, cs, :], T1[:, cs, :], AF.Exp, scale=-1.5957691216), reads=[bT1], writes=[bT1])
                        P.op(ACT, A_(T1[:, cs, :], T1[:, cs, :], AF.Ln, bias=1.0), reads=[bT1], writes=[bT1])
                        P.op(ACT, A_(T1[:, cs, :], T1[:, cs, :], AF.Exp, scale=-1.0), reads=[bT1], writes=[bT1])
                        P.op(DVE, TT_(T1[:, cs, :], T1[:, cs, :], PF[c2][:], ALU.mult), reads=[bT1, bPF[c2]], writes=[bT1])
                    P.op(DVE, TT_(T1[:], T1[:], H[:], ALU.mult), reads=[bT1, bH], writes=[bT1])
                    P.op(ACT, A_(YSQ[:], T1[:], AF.Square), reads=[bT1], writes=[bYSQ])
                    for c in range(4):
                        P.op(PE, MM_(PRS[:, 0:256], ONB[:], YSQ[:, c, :], c == 0, c == 3), reads=[bYSQ, bC0], writes=[bPRS])
                    P.op(ACT, A_(RS[:], PRS[:, 0:256], AF.Ln, scale=1.0 / 512, bias=EPS), reads=[bPRS], writes=[bRS])
                    P.op(ACT, A_(RS[:], RS[:], AF.Exp, scale=-0.5), reads=[bRS], writes=[bRS])
                    P.op(DVE, TT_(LRN[:], T1[:], RS[:, :].unsqueeze(1).to_broadcast([128, 4, 256]), ALU.mult), reads=[bT1, bRS], writes=[bLRN])
                    P.dma(POOL, LR_s[:, qi * 256:(qi + 1) * 256].rearrange("(c p) n -> p c n", p=128), LRN[:], reads=[bLRN])

            load_x(0)
            load_x(1)
            chain1(0, 1)
            for bi in range(NB):
                if bi + 1 < NB:
                    chain1(bi + 1, 1)
                chain2(bi)
                chain1(bi, 2)
                if bi + 2 < NB:
                    load_x(bi + 2)
            P.emit_segment()
        sWI.close()

        sW = top.enter_context(contextlib.ExitStack())
        sbW, _ = mk(sW)
        WUP = sbW("WUP", [128, 8, 2 * DFF], BF16); bWUP = Buf()
        with contextlib.ExitStack() as s2:
            if stop_after < 2:
                return nc
            sb, ps = mk(s2)
            WSW = [sb("WSW%d" % i, [128, 512]) for i in range(4)]; bWSW = [Buf() for _ in range(4)]
            KX = [sb("KX%d" % i, [96, S], BF16) for i in range(2)]; bKX = [Buf(), Buf()]
            QX = [sb("QX%d" % i, [96, NQ], BF16) for i in range(2)]; bQX = [Buf(), Buf()]
            VX = [sb("VX%d" % i, [128, 64, 65], BF16) for i in range(2)]; bVX = [Buf(), Buf()]
            PP = [sb("PP%d" % i, [128, 2, 256], BF16) for i in range(3)]; bPP = [Buf() for _ in range(3)]
            AO = [sb("AO%d" % i, [128, 2, 64]) for i in range(2)]; bAO = [Buf(), Buf()]
            RD = sb("RD", [128, 4]); bRD = Buf()
            SB_ = [ps("SB%d" % i, [128, 2, 256]) for i in range(3)]; bSB = [Buf() for _ in range(3)]
            OB = [[ps("OB%d_%d" % (i, q), [128, 512]) for q in range(2)] for i in range(2)]
            bOB = [[Buf(), Buf()], [Buf(), Buf()]]
            for i in range(2):
                P.dma(SP, KX[i][64:96, :], oh, writes=[bKX[i]])
                P.op(DVE, lambda e, i=i: e.memset(VX[i][:, :, 64:65], 1.0), writes=[bVX[i]])

            def load_head(h):
                i = h % 2
                P.dma(SP, KX[i][0:64, :], KT_s[h * 64:(h + 1) * 64, :], writes=[bKX[i]])
                P.dma(SP, QX[i][0:64, :], QT_s[h * 64:(h + 1) * 64, :], writes=[bQX[i]])
                P.dma(SP, QX[i][64:96, :], NS_s[h * 32:(h + 1) * 32, :], writes=[bQX[i]])
                P.dma(SP, VX[i][:, :, 0:64], V_s[h].rearrange("t p d -> p t d"), writes=[bVX[i]])

            its = []
            oi = 0
            for h in range(8):
                for qi in range(NOWN):
                    bi = OWN0 + qi
                    ob = oi % 2; oi += 1
                    for j in range(bi + 1):
                        its.append((h, qi, bi, ob, j))

            def emit_qk(n):
                h, qi, bi, ob, j = its[n]
                hb = h % 2
                sbk = n % 3
                qc = slice(qi * 256, (qi + 1) * 256)
                diag = (j == bi)
                for kh in range(2):
                    kc_ = slice(j * 256 + kh * 128, j * 256 + (kh + 1) * 128)
                    if not diag:
                        P.op(PE, MM_(SB_[sbk][:, kh, :], KX[hb][0:96, kc_], QX[hb][0:96, qc], True, True),
                             reads=[bKX[hb], bQX[hb]], writes=[bSB[sbk]])
                    else:
                        P.op(PE, MM_(SB_[sbk][:, kh, :], KX[hb][0:64, kc_], QX[hb][0:64, qc], True, False),
                             reads=[bKX[hb], bQX[hb]], writes=[bSB[sbk]])
                        P.op(PE, MM_(SB_[sbk][:, kh, :], IDB[:], MKD[:, kh, :], False, True),
                             reads=[bC0], writes=[bSB[sbk]])

            def emit_rest(n):
                h, qi, bi, ob, j = its[n]
                hb = h % 2
                sbk = n % 3
                nj = bi + 1
                P.op(ACT, A_(PP[sbk][:], SB_[sbk][:], AF.Exp, scale=0.125), reads=[bSB[sbk]], writes=[bPP[sbk]])
                if n + 2 < len(its):
                    emit_qk(n + 2)
                for kh in range(2):
                    for qh in range(2):
                        P.op(PE, MM_(OB[ob][qh][:, 0:65], PP[sbk][:, kh, qh * 128:(qh + 1) * 128], VX[hb][:, j * 2 + kh, :],
                                     (j == 0 and kh == 0), (j == nj - 1 and kh == 1)),
                             reads=[bPP[sbk], bVX[hb]], writes=[bOB[ob][qh]])
                if j == nj - 1:
                    for qh in range(2):
                        P.op(DVE, lambda e, ob=ob, qh=qh: e.reciprocal(RD[:, ob * 2 + qh:ob * 2 + qh + 1], OB[ob][qh][:, 64:65]),
                             reads=[bOB[ob][qh]], writes=[bRD])
                        P.op(DVE, TS_(AO[ob][:, qh, :], OB[ob][qh][:, 0:64], RD[:, ob * 2 + qh:ob * 2 + qh + 1], None, ALU.mult),
                             reads=[bOB[ob][qh], bRD], writes=[bAO[ob]])
                    P.dma(POOL, AT_s[qi * 256:(qi + 1) * 256, h * 64:(h + 1) * 64].rearrange("(qh p) d -> p qh d", p=128), AO[ob][:],
                          reads=[bAO[ob]])

            load_head(0)
            load_head(1)
            def wup_piece(g):
                i = g % 4
                P.dma(SP, WSW[i][:].rearrange("p (kc n) -> p kc n", kc=8),
                      w_up[:, g * 64:(g + 1) * 64].rearrange("(kc p) n -> p kc n", p=128), writes=[bWSW[i]])
                P.op(DVE, CP_(WUP[:, :, g * 64:(g + 1) * 64], WSW[i][:].rearrange("p (kc n) -> p kc n", kc=8)),
                     reads=[bWSW[i]], writes=[bWUP])
            wnext = [0]
            emit_qk(0)
            emit_qk(1)
            for n in range(len(its)):
                h, qi, bi, ob, j = its[n]
                if n > 0 and its[n - 1][0] != h and h + 1 < 8:
                    load_head(h + 1)
                if n % 24 == 12 and wnext[0] < 88:
                    wup_piece(wnext[0]); wnext[0] += 1
                emit_rest(n)
            while wnext[0] < 88:
                wup_piece(wnext[0]); wnext[0] += 1
            P.sched = False
            P.emit_segment()
            P.sched = True

        with contextlib.ExitStack() as s3:
            if stop_after < 3:
                return nc
            sb, ps = mk(s3)
            WDN = sb("WDN", [128, 22, D], BF16); bWDN = Buf()
            XD = [sb("XT3_%d" % i, [128, 2, D]) for i in range(2)]; bXD = [Buf(), Buf()]
            AT = sb("AT", [128, 512]); bAT = Buf()
            ST = sb("ST3", [128, 8]); bST = Buf()
            AN = sb("AN", [128, 512], BF16); bAN = Buf()
            HTD = [sb("HTD%d" % i, [128, 8, 256], BF16) for i in range(2)]; bHTD = [Buf(), Buf()]
            HN = sb("HN3", [128, D], BF16); bHN = Buf()
            UB = [sb("UB%d" % i, [128, 2, 258]) for i in range(3)]; bUB = [Buf() for _ in range(3)]; bUBh = [Buf() for _ in range(3)]
            ACC = [sb("ACC%d" % i, [128, 2, 256]) for i in range(3)]; bACC = [Buf() for _ in range(3)]
            WS3 = [ACC[i][:].rearrange("p a n -> p (a n)") for i in range(3)]; bWS3 = bACC
            UH = sb("UH", [128, 44, 2]); bUH = Buf()
            ACTT = sb("ACTT", [128, 22, 256], BF16); bACTT = Buf(); bACTTm = [Buf() for _ in range(22)]
            PY = [ps("PY%d" % i, [128, 512]) for i in range(4)]; bPY = [Buf() for _ in range(4)]
            PT1 = ps("PT3", [128, 4, 256], BF16); bPT1 = Buf()
            PU = [ps("PU%d" % i, [128, 2, 256]) for i in range(3)]; bPU = [Buf() for _ in range(3)]

            ACTF = ACTT[:].rearrange("p m n -> p (m n)").bitcast(F32)
            STG = [(WS3[0], bACC[0]), (WS3[1], bACC[1]), (WS3[2], bACC[2])] + [(ACTF[:, k * 512:(k + 1) * 512], Buf()) for k in range(5)]
            wi = 0
            for m in range(22):
                for nh in range(2):
                    stg, bstg = STG[wi % len(STG)]; wi += 1
                    P.dma(SP, stg, w_down[m * 128:(m + 1) * 128, nh * 512:(nh + 1) * 512], writes=[bstg])
                    P.op(DVE, TT_(WDN[:, m, nh * 512:(nh + 1) * 512], stg, G2B[:, nh * 512:(nh + 1) * 512], ALU.mult),
                         reads=[bstg, bG2B], writes=[bWDN])
            P.op(DVE, _w(lambda e: e.memset(UH[:], 0.0), 88), reads=[b for _, b in STG[3:]], writes=[bUH] + bACTTm)

            upc = [0]

            def front(qi):
                bi = OWN0 + qi
                Xt, bXt = XD[qi % 2], bXD[qi % 2]
                MIXT = HT = HTD[qi % 2]
                bMIXL = bMIXA = bHT = bHTD[qi % 2]
                P.dma(SP, Xt[:], xs[bi * 256:(bi + 1) * 256, :].rearrange("(t p) d -> p t d", p=128), writes=[bXt])
                P.dma(SP, MIXT[:, 0:4, :], LR_s[:, qi * 256:(qi + 1) * 256].rearrange("(c p) n -> p c n", p=128), writes=[bMIXL])
                for t in range(2):
                    P.dma(SP, AT[:], AT_s[qi * 256 + t * 128:qi * 256 + (t + 1) * 128, :], writes=[bAT])
                    P.op(ACT, A_(AN[:], AT[:], AF.Square, accum_out=ST[:, 0:1]), reads=[bAT], writes=[bAN, bST])
                    P.op(ACT, A_(ST[:, 2:3], ST[:, 0:1], AF.Ln, scale=1.0 / 512, bias=EPS), reads=[bST], writes=[bST])
                    P.op(ACT, A_(ST[:, 4:5], ST[:, 2:3], AF.Exp, scale=-0.5), reads=[bST], writes=[bST])
                    P.op(DVE, TS_(AN[:], AT[:], ST[:, 4:5], None, ALU.mult), reads=[bAT, bST], writes=[bAN])
                    for c in range(4):
                        P.op(PE, TR_(PT1[:, c, t * 128:(t + 1) * 128], AN[:, c * 128:(c + 1) * 128], IDB[:]), reads=[bAN], writes=[bPT1])
                P.op(DVE, CP_(MIXT[:, 4:8, :], PT1[:]), reads=[bPT1], writes=[bMIXA])
                for t in range(2):
                    for nh in range(2):
                        pb = t * 2 + nh
                        for kc in range(8):
                            P.op(PE, MM_(PY[pb][:], MIXT[:, kc, t * 128:(t + 1) * 128], WOUT[:, kc, nh * 512:(nh + 1) * 512], kc == 0, kc == 7),
                                 reads=[bMIXL, bMIXA, bWOUT], writes=[bPY[pb]])
                        P.op(DVE, TT_(Xt[:, t, nh * 512:(nh + 1) * 512], PY[pb][:], Xt[:, t, nh * 512:(nh + 1) * 512], ALU.add),
                             reads=[bPY[pb], bXt], writes=[bXt])
                if debug:
                    P.dma(POOL, X1_s[qi * 256:(qi + 1) * 256, :].rearrange("(t p) d -> p t d", p=128), Xt[:], reads=[bXt])
                for t in range(2):
                    P.op(ACT, A_(HN[:], Xt[:, t, :], AF.Square, accum_out=ST[:, t:t + 1]), reads=[bXt], writes=[bHN, bST])
                P.op(ACT, A_(ST[:, 2:4], ST[:, 0:2], AF.Ln, scale=1.0 / D, bias=EPS), reads=[bST], writes=[bST])
                P.op(ACT, A_(ST[:, 4:6], ST[:, 2:4], AF.Exp, scale=-0.5), reads=[bST], writes=[bST])
                HT2 = HT[:].rearrange("p k (t n) -> p k t n", t=2)
                for t in range(2):
                    P.op(DVE, TS_(HN[:], Xt[:, t, :], ST[:, 4 + t:5 + t], None, ALU.mult), reads=[bXt, bST], writes=[bHN])
                    for rnd in range(2):
                        for k4 in range(4):
                            kc = rnd * 4 + k4
                            P.op(PE, TR_(PT1[:, k4, 0:128], HN[:, kc * 128:(kc + 1) * 128], IDB[:]), reads=[bHN], writes=[bPT1])
                        for k4 in range(4):
                            kc = rnd * 4 + k4
                            P.op(DVE, TS_(HT[:, kc, t * 128:(t + 1) * 128], PT1[:, k4, 0:128], G2[:, kc:kc + 1], SH2[:, kc:kc + 1], ALU.mult, ALU.add),
                                 reads=[bPT1, bMOD], writes=[bHT])

            def rest(qi):
                bi = OWN0 + qi
                Xt, bXt = XD[qi % 2], bXD[qi % 2]
                HT = HTD[qi % 2]
                bHT = bHTD[qi % 2]
                if qi == 1:
                    P.op(DVE, TS_(UH[:], UH[:], FLG[:, 0:1], None, ALU.mult), reads=[bUH, bC0], writes=[bUH])
                for m in range(22):
                    u = upc[0] % 3; upc[0] += 1
                    bPU_, bUB_, bACC_ = bPU, bUB, bACC
                    for half, mm in ((0, m), (1, m + 22)):
                        for kc in range(8):
                            P.op(PE, MM_(PU[u][:, half, :], WUP[:, kc, mm * 128:(mm + 1) * 128], HT[:, kc, :], kc == 0, kc == 7),
                                 reads=[bWUP, bHT], writes=[bPU_[u]])
                    if qi > 0:
                        for half, mm in ((0, m), (1, m + 22)):
                            P.op(POOL, CP_(UB[u][:, half, 0:2], UH[:, mm, :]), reads=[bUH], writes=[bUBh[u]])
                    P.op(ACT, A_(UB[u][:, :, 2:258], PU[u][:], AF.Copy), reads=[bPU_[u]], writes=[bUB_[u]])
                    for half, mm in ((0, m), (1, m + 22)):
                        P.op(POOL, CP_(UH[:, mm, :], UB[u][:, half, 256:258]), reads=[bUB_[u]], writes=[bUH])
                    if qi == 0:
                        continue
                    for half, mm in ((0, m), (1, m + 22)):
                        P.op(ACT, A_(ACC[u][:, half, :], UB[u][:, half, 0:256], AF.Identity, scale=FCW[:, mm, 0:1], bias=FCB[:, mm:mm + 1]),
                             reads=[bUB_[u], bUBh[u], bC0], writes=[bACC_[u]])
                        for jj in (1, 2):
                            P.op(DVE, STT_(ACC[u][:, half, :], UB[u][:, half, jj:jj + 256], FCW[:, mm, jj:jj + 1], ACC[u][:, half, :], ALU.mult, ALU.add),
                                 reads=[bUB_[u], bUBh[u], bACC_[u], bC0], writes=[bACC_[u]])
                    P.op(ACT, A_(ACC[u][:, 0, :], ACC[u][:, 0, :], AF.Silu), reads=[bACC_[u]], writes=[bACC_[u]])
                    P.op(POOL, TT_(ACTT[:, m, :], ACC[u][:, 0, :], ACC[u][:, 1, :], ALU.mult), reads=[bACC_[u]], writes=[bACTTm[m]])
                if qi == 0:
                    return
                for m in range(22):
                    for t in range(2):
                        for nh in range(2):
                            pb = t * 2 + nh
                            P.op(PE, MM_(PY[pb][:], ACTT[:, m, t * 128:(t + 1) * 128], WDN[:, m, nh * 512:(nh + 1) * 512], m == 0, m == 21),
                                 reads=[bACTTm[m], bWDN], writes=[bPY[pb]])
                for t in range(2):
                    for nh in range(2):
                        pb = t * 2 + nh
                        P.op(DVE, TT_(Xt[:, t, nh * 512:(nh + 1) * 512], PY[pb][:], Xt[:, t, nh * 512:(nh + 1) * 512], ALU.add),
                             reads=[bPY[pb], bXt], writes=[bXt])
                P.dma(POOL, out[(qi - 1) * 256:qi * 256, :].rearrange("(t p) d -> p t d", p=128), Xt[:], reads=[bXt])

            front(0)
            for qi in range(NOWN):
                if qi + 1 < NOWN:
                    front(qi + 1)
                rest(qi)
            P.emit_segment(last=True)
            nc._est_log = P.est_log
    return nc


def make_in_maps(x, c, w_ada, b_ada, norm1_g, w_in, q_norm_g, k_norm_g, lru_conv_w, lru_conv_b,
                 lru_wa, lru_ba, lru_wx, lru_bx, lru_lambda, lru_out_g, attn_out_g, w_out,
                 norm2_g, w_up, ffn_conv_w, ffn_conv_b, w_down):
    f = lambda a: np.ascontiguousarray(np.asarray(a, dtype=np.float32))
    x = f(x); c = f(c)

    def fm(v, nch):
        return f(np.asarray(v, np.float32).reshape(nch, 128).T)

    def bd(w):
        w = np.asarray(w, np.float32)
        o = np.zeros((128, 4, 128), np.float32)
        for ch in range(4):
            for s_ in range(2):
                o[s_ * 64:(s_ + 1) * 64, ch, s_ * 64:(s_ + 1) * 64] = w[ch * 2 + s_]
        return o

    shared = {
        "w_ada": f(w_ada[0]), "b_ada": f(b_ada[0]).reshape(1, -1),
        "n1g": fm(norm1_g[0], 8), "n2g": fm(norm2_g[0], 8),
        "mixg": fm(np.concatenate([np.asarray(lru_out_g[0]), np.asarray(attn_out_g[0])]), 8),
        "w_in": f(w_in[0]),
        "gqp": f(np.tile(np.asarray(q_norm_g[0], np.float32), 2).reshape(128, 1)),
        "gkp": f(np.tile(np.asarray(k_norm_g[0], np.float32), 2).reshape(128, 1)),
        "cw": f(np.asarray(lru_conv_w[0], np.float32).reshape(4, 4, 128).transpose(2, 1, 0)),
        "cb": fm(lru_conv_b[0], 4),
        "wabd": bd(lru_wa[0]), "wxbd": bd(lru_wx[0]),
        "lba": fm(lru_ba[0], 4), "lbx": fm(lru_bx[0], 4), "lam": fm(lru_lambda[0], 4),
        "w_out": f(w_out[0]), "w_up": f(w_up[0]),
        "fcw": f(np.asarray(ffn_conv_w[0], np.float32).reshape(3, 44, 128).transpose(2, 1, 0)),
        "fcb": fm(ffn_conv_b[0], 44),
        "w_down": f(w_down[0]),
        "identb": np.eye(128, dtype=np.float32).astype(ml_dtypes.bfloat16),
        "identf": np.eye(128, dtype=np.float32),
        "onesb": np.ones((128, 128), np.float32).astype(ml_dtypes.bfloat16),
    }
    ohm = np.zeros((32, S), np.float32)
    for j in range(32):
        ohm[j, j * 256:(j + 1) * 256] = NEG
    shared["oh"] = ohm.astype(ml_dtypes.bfloat16)
    kk = np.arange(128)[:, None]
    qq = np.arange(128)[None, :]
    tri = np.where(kk <= qq, 0.0, NEG).astype(np.float32)
    md = np.zeros((128, 2, 256), np.float32)
    md[:, 0, 0:128] = tri
    md[:, 1, 0:128] = NEG
    md[:, 1, 128:256] = tri
    shared["maskd"] = md.astype(ml_dtypes.bfloat16)
    in_maps = []
    for core in range(8):
        b, half = core // 2, core % 2
        if half == 1:
            xsv = x[b]
        else:
            xsv = np.concatenate([np.zeros((4096, D), np.float32), x[b, :4096]], axis=0)
        pb = np.full((NOWN, 32), -1e30, np.float32)
        for qi in range(NOWN):
            bi = OWN0 + qi
            lo = 16 if half == 0 else 0
            if bi > lo:
                pb[qi, lo:bi] = 0.0
        m = dict(shared)
        m["xs"] = f(xsv)
        m["cT"] = fm(c[b], 8)
        m["flag"] = np.full((128, 1), float(half), np.float32)
        m["pastb"] = f(np.broadcast_to(pb[None], (128, NOWN, 32)))
        in_maps.append(m)
    return in_maps


_NC = {}


def kernel(**inputs):
    in_maps = make_in_maps(**inputs)
    if "nc" not in _NC:
        _NC["nc"] = build(False)
    res = run_bass_kernel_spmd(_NC["nc"], in_maps, core_ids=list(range(8)))
    outp = np.zeros((4, S, D), np.float32)
    for core in range(8):
        b, half = core // 2, core % 2
        outp[b, half * 4096:(half + 1) * 4096] = res.results[core]["out"]
    return outp
```

```python
import contextlib
import numpy as np
import ml_dtypes
import concourse.bass as bass
import concourse.mybir as mybir
from concourse.bass_utils import run_bass_kernel_spmd

F32 = mybir.dt.float32
BF16 = mybir.dt.bfloat16
AF = mybir.ActivationFunctionType
ALU = mybir.AluOpType
AX = mybir.AxisListType

PE, ACT, DVE, POOL, SP = "tensor", "scalar", "vector", "gpsimd", "sync"
COMPUTE = (PE, ACT, DVE, POOL)

D = 1024
S = 8192
NB = 32
OWN0 = 15
NOWN = NB - OWN0
NQ = NOWN * 256
DFF = 2816
NEG = -30000.0
EPS = 1e-6


class Buf:
    __slots__ = ("name", "w", "rs")

    def __init__(self, name=""):
        self.name = name
        self.w = None
        self.rs = []


class Op:
    __slots__ = ("eng", "fn", "deps", "dma", "sig", "sem", "need", "seg", "alld", "cost", "lat", "line")

    def __init__(self, eng, fn, dma, seg):
        self.eng = eng
        self.fn = fn
        self.dma = dma
        self.deps = set()
        self.sig = None
        self.sem = None
        self.need = False
        self.seg = seg


class Prog:
    def __init__(self, nc, stack, n_dma_sems=16):
        self.nc = nc
        self.n = n_dma_sems
        self.seg = 0
        self.ops = []
        self.esem = {e: stack.enter_context(nc.semaphore("s_" + e)) for e in COMPUTE}
        self.dsem = {q: [stack.enter_context(nc.semaphore("d_%s_%d" % (q, i))) for i in range(n_dma_sems)]
                     for q in (SP, POOL)}
        self.ecnt = {e: 0 for e in COMPUTE}
        self.dcnt = {q: [0] * n_dma_sems for q in self.dsem}
        self.dlast = {q: [None] * n_dma_sems for q in self.dsem}
        self.dnum = {q: 0 for q in self.dsem}
        self.finals = []
        self.rec = None
        self.sched = True
        self.est_ns = 0.0
        self.est_log = []

    def chain(self, fn):
        self.rec = []
        fn()
        lst, self.rec = self.rec, None
        return lst

    def merge(self, *chains):
        items = []
        for ci, ch in enumerate(chains):
            n = len(ch)
            for i, it in enumerate(ch):
                items.append(((i + 0.5) / n, ci, i, it))
        items.sort(key=lambda x: (x[0], x[1], x[2]))
        for _, _, _, it in items:
            self.op(*it)

    def op(self, eng, fn, reads=(), writes=(), dma=False):
        if self.rec is not None:
            self.rec.append((eng, fn, tuple(reads), tuple(writes), dma))
            return None
        o = Op(eng, fn, dma, self.seg)
        for b in reads:
            if b.w is not None:
                o.deps.add(b.w)
        for b in writes:
            if b.w is not None:
                o.deps.add(b.w)
            for r in b.rs:
                o.deps.add(r)
        for b in reads:
            b.rs.append(o)
        for b in writes:
            b.w = o
            b.rs = []
        o.deps.discard(o)
        import sys as _sys
        fr = _sys._getframe(1)
        if fr.f_code.co_name in ('dma', 'merge'):
            fr = fr.f_back
        o.line = fr.f_lineno
        n = getattr(fn, "n", 256)
        k = getattr(fn, "k", 0)
        if dma:
            o.cost = 60.0
            o.lat = 2200.0 + n / 60.0
        elif eng == PE:
            o.cost = 45.0 + 0.37 * n
            o.lat = o.cost + 200.0
        elif eng == ACT:
            o.cost = 130.0 + 0.83 * n + 120.0 * k
            o.lat = o.cost + 120.0
        elif eng == POOL:
            o.cost = 250.0 + 2.0 * n
            o.lat = o.cost + 150.0
        else:
            o.cost = 150.0 + (1.05 if k else 0.6) * n
            o.lat = o.cost + 120.0
        self.ops.append(o)
        return o

    def dma(self, eng, out, in_, reads=(), writes=()):
        nbytes = 1
        for d in out.shape:
            nbytes *= int(d)
        nbytes *= 2 if out.dtype == BF16 else 4
        f = lambda e: e.dma_start(out=out, in_=in_)
        f.n = nbytes
        return self.op(eng, f, reads, writes, dma=True)

    def schedule(self, ops):
        import bisect
        n = len(ops)
        pos = {id(o): i for i, o in enumerate(ops)}
        succ = [[] for _ in range(n)]
        indeg = [0] * n
        for i, o in enumerate(ops):
            for d in o.alld:
                j = pos.get(id(d))
                if j is None:
                    continue
                succ[j].append(i)
                indeg[i] += 1
        rt = [0.0] * n
        why = [None] * n
        stt = [0.0] * n
        lastop = {}
        avail = {e: [] for e in (PE, ACT, DVE, POOL, SP)}
        free = {e: 0.0 for e in avail}
        for i, o in enumerate(ops):
            if indeg[i] == 0:
                avail[o.eng].append(i)
        order = []
        WIN_ = 40
        done = 0
        while done < n:
            best = None
            for e, lst in avail.items():
                if not lst:
                    continue
                fe = free[e]
                cb = None
                for i in lst[:WIN_]:
                    st = rt[i] if rt[i] > fe else fe
                    if cb is None or st < cb[0]:
                        cb = (st, i)
                if best is None or cb < best[0:2]:
                    best = (cb[0], cb[1], e)
            st, i, e = best
            lst = avail[e]
            lst.pop(bisect.bisect_left(lst, i))
            o = ops[i]
            stt[i] = st
            if rt[i] < st and e in lastop:
                why[i] = ('eng', lastop[e])
            lastop[e] = i
            free[e] = st + o.cost
            fin = st + o.lat
            order.append(o)
            done += 1
            for k in succ[i]:
                if fin > rt[k]:
                    rt[k] = fin
                    if why[k] is None or why[k][0] == 'dep':
                        why[k] = ('dep', i)
                indeg[k] -= 1
                if indeg[k] == 0:
                    bisect.insort(avail[ops[k].eng], k)
        self.est_ns = max(free.values())
        busy = {}
        for o in ops:
            busy[o.eng] = busy.get(o.eng, 0.0) + o.cost
        self.est_busy = {k: round(v / 1e3) for k, v in busy.items()}
        return order

    def emit_segment(self, last=False):
        nc = self.nc
        seg = self.seg
        for o in self.ops:
            o.alld = [d for d in o.deps if d.seg == seg]
        ops = self.schedule(self.ops) if self.sched else self.ops
        self.est_log.append((seg, len(ops), round(self.est_ns / 1e3), self.est_busy))
        queues = {e: [] for e in (PE, ACT, DVE, POOL, SP)}
        for o in ops:
            queues[o.eng].append(o)
            keep = set()
            for d in o.deps:
                if d.seg != seg:
                    continue
                if d.dma:
                    keep.add(d)
                elif d.eng == o.eng and not o.dma:
                    if o.eng != PE:
                        keep.add(d)
                else:
                    keep.add(d)
            o.deps = keep
            for d in keep:
                d.need = True
        for e in COMPUTE:
            for o in reversed(queues[e]):
                if not o.dma:
                    o.need = True
                    break
        for o in ops:
            if o.dma:
                q = o.eng
                i = self.dnum[q] % self.n
                self.dnum[q] += 1
                prev = self.dlast[q][i]
                if prev is not None and prev.seg == seg:
                    o.deps.add(prev)
                self.dcnt[q][i] += 16
                o.sem = self.dsem[q][i]
                o.sig = self.dcnt[q][i]
                self.dlast[q][i] = o
            elif o.need:
                self.ecnt[o.eng] += 1
                o.sem = self.esem[o.eng]
                o.sig = self.ecnt[o.eng]
        with nc.Block() as block:
            def run(engname, eng):
                waited = {}
                for o in queues[engname]:
                    need = {}
                    for d in o.deps:
                        k = id(d.sem)
                        if waited.get(k, 0) >= d.sig:
                            continue
                        if k not in need or need[k][1] < d.sig:
                            need[k] = (d.sem, d.sig)
                    for k, (s, v) in need.items():
                        eng.wait_ge(s, v)
                        waited[k] = v
                    ins = o.fn(eng)
                    if o.sem is not None:
                        ins.then_inc(o.sem, 16 if o.dma else 1)
                for e in COMPUTE:
                    if e != engname and self.ecnt[e] > 0:
                        eng.wait_ge(self.esem[e], self.ecnt[e])
                for q in self.dsem:
                    for i in range(self.n):
                        if self.dcnt[q][i] > 0:
                            eng.wait_ge(self.dsem[q][i], self.dcnt[q][i])

            @block.sync
            def _(e):
                run(SP, e)

            @block.tensor
            def _(e):
                run(PE, e)

            @block.scalar
            def _(e):
                run(ACT, e)

            @block.vector
            def _(e):
                run(DVE, e)

            @block.gpsimd
            def _(e):
                run(POOL, e)
        self.ops = []
        self.seg += 1


def _fs(ap):
    n = 1
    for d in ap.shape[1:]:
        n *= int(d)
    return n


def _w(f, n):
    f.n = n
    return f


def A_(out, in_, func, **kw):
    f = _w(lambda e: e.activation(out=out, in_=in_, func=func, **kw), _fs(out))
    f.k = sum(1 for key in ("scale", "bias", "accum_out") if key in kw and not isinstance(kw[key], (int, float)))
    return f


def TS_(out, in0, s1, s2, op0, op1=None):
    if op1 is None:
        return _w(lambda e: e.tensor_scalar(out=out, in0=in0, scalar1=s1, scalar2=None, op0=op0), _fs(out))
    return _w(lambda e: e.tensor_scalar(out=out, in0=in0, scalar1=s1, scalar2=s2, op0=op0, op1=op1), _fs(out))


def TT_(out, in0, in1, op):
    f = _w(lambda e: e.tensor_tensor(out=out, in0=in0, in1=in1, op=op), _fs(out))
    f.k = 1
    return f


def STT_(out, in0, scalar, in1, op0, op1):
    f = _w(lambda e: e.scalar_tensor_tensor(out=out, in0=in0, scalar=scalar, in1=in1, op0=op0, op1=op1), _fs(out))
    f.k = 1
    return f


def CP_(out, in_):
    return _w(lambda e: e.tensor_copy(out, in_), _fs(out))


def MM_(out, lhsT, rhs, start, stop):
    return _w(lambda e: e.matmul(out, lhsT=lhsT, rhs=rhs, start=start, stop=stop), _fs(rhs))


def TR_(out, in_, ident):
    return _w(lambda e: e.transpose(out, in_, ident), 128)


def build(debug=False, stop_after=3):
    nc = bass.Bass("TRN2", target_bir_lowering=False)

    def din(name, shape, dt=F32):
        return nc.dram_tensor(name, list(shape), dt, kind="ExternalInput").ap()

    skind = "ExternalOutput" if debug else "Internal"

    def dscr(name, shape, dt):
        return nc.dram_tensor(name, list(shape), dt, kind=skind).ap()

    xs = din("xs", [S, D])
    cT = din("cT", [128, 8])
    w_ada = din("w_ada", [D, 6 * D])
    b_ada = din("b_ada", [1, 6 * D])
    n1g = din("n1g", [128, 8])
    n2g = din("n2g", [128, 8])
    mixg = din("mixg", [128, 8])
    w_in = din("w_in", [D, 2560])
    gqp = din("gqp", [128, 1])
    gkp = din("gkp", [128, 1])
    cw = din("cw", [128, 4, 4])
    cb = din("cb", [128, 4])
    wabd = din("wabd", [128, 4, 128])
    wxbd = din("wxbd", [128, 4, 128])
    lba = din("lba", [128, 4])
    lbx = din("lbx", [128, 4])
    lam = din("lam", [128, 4])
    w_out = din("w_out", [D, D])
    w_up = din("w_up", [D, 2 * DFF])
    fcw = din("fcw", [128, 44, 3])
    fcb = din("fcb", [128, 44])
    w_down = din("w_down", [DFF, D])
    flag = din("flag", [128, 1])
    pastb = din("pastb", [128, NOWN, 32])
    oh = din("oh", [32, S], BF16)
    identb = din("identb", [128, 128], BF16)
    identf = din("identf", [128, 128])
    onesb = din("onesb", [128, 128], BF16)
    maskd = din("maskd", [128, 2, 256], BF16)
    out = nc.dram_tensor("out", [4096, D], F32, kind="ExternalOutput").ap()

    KT_s = dscr("KT_s", [512, S], BF16)
    QT_s = dscr("QT_s", [512, NQ], BF16)
    NS_s = dscr("NS_s", [256, NQ], BF16)
    LR_s = dscr("LR_s", [512, NQ], BF16)
    V_s = dscr("V_s", [8, 64, 128, 64], BF16)
    AT_s = dscr("AT_s", [NQ, 512], F32)
    X1_s = dscr("X1_s", [NQ, D], F32) if debug else None

    with contextlib.ExitStack() as top:
        P = Prog(nc, top)

        def mk(stack):
            def sb(name, shape, dt=F32):
                return stack.enter_context(nc.sbuf_tensor(name, list(shape), dt))

            def ps(name, shape, dt=F32):
                return stack.enter_context(nc.psum_tensor(name, list(shape), dt))
            return sb, ps

        sbT, _ = mk(top)
        WOUT = sbT("WOUT", [128, 8, D], BF16); bWOUT = Buf()
        G2B = sbT("G2B", [128, D], BF16); bG2B = Buf()
        G1 = sbT("G1", [128, 8]); SH1 = sbT("SH1", [128, 8]); G2 = sbT("G2", [128, 8]); SH2 = sbT("SH2", [128, 8])
        bMOD = Buf()
        IDB = sbT("IDB", [128, 128], BF16); ONB = sbT("ONB", [128, 128], BF16); MKD = sbT("MKD", [128, 2, 256], BF16)
        FLG = sbT("FLG", [128, 1]); bCONST = Buf()
        CW = sbT("CW", [128, 4, 4]); CB = sbT("CB", [128, 4]); CL = sbT("CL", [128, 4]); CL2 = sbT("CL2", [128, 4])
        NBA = sbT("NBA", [128, 4]); NBX = sbT("NBX", [128, 4])
        WAB = sbT("WAB", [128, 4, 128], BF16); WXB = sbT("WXB", [128, 4, 128], BF16)
        FCW = sbT("FCW", [128, 44, 3]); FCB = sbT("FCB", [128, 44])

        sWI = contextlib.ExitStack()
        sbWI, _ = mk(sWI)
        WIN = sbWI("WIN", [128, 8, 2560], BF16); bWIN = Buf()
        with contextlib.ExitStack() as s0:
            sb, ps = mk(s0)
            WIS = [sb("WIS%d" % i, [128, 8, 320]) for i in range(2)]; bWIS = [Buf(), Buf()]
            CT = sb("CT", [128, 8]); CREP = sb("CREP", [128, 8, 128])
            WAS = [sb("WAS%d" % i, [128, 8, 512]) for i in range(3)]; bWAS = [Buf() for _ in range(3)]
            BB = [sb("BB%d" % i, [128, 512]) for i in range(3)]; bBB = [Buf() for _ in range(3)]
            MB = sb("MB", [128, 6 * D]); bMB = Buf()
            TMP = sb("TMP", [128, 48, 128]); bTMP = Buf()
            MT = sb("MT", [128, 48]); bMT = Buf()
            IDF = sb("IDF", [128, 128])
            N1G = sb("N1G", [128, 8]); N2G = sb("N2G", [128, 8]); MIXG = sb("MIXG", [128, 8])
            LBA = sb("LBA", [128, 4]); LBX = sb("LBX", [128, 4]); LAM = sb("LAM", [128, 4])
            WAF = sb("WAF", [128, 4, 128]); WXF = sb("WXF", [128, 4, 128])
            PS0 = [ps("PS0_%d" % i, [128, 512]) for i in range(2)]; bPS0 = [Buf(), Buf()]
            bC0 = Buf()
            for dst, src in ((CT, cT), (IDF, identf), (N1G, n1g), (N2G, n2g), (MIXG, mixg), (LBA, lba), (LBX, lbx),
                             (LAM, lam), (WAF, wabd), (WXF, wxbd), (IDB, identb), (ONB, onesb), (MKD, maskd),
                             (FLG, flag), (CW, cw), (CB, cb), (FCW, fcw), (FCB, fcb)):
                P.dma(SP, dst[:], src, writes=[bC0])
            P.op(DVE, CP_(CREP[:], CT[:, :].unsqueeze(2).to_broadcast([128, 8, 128])), reads=[bC0], writes=[bCONST])
            for g in range(12):
                i = g % 3
                P.dma(SP, WAS[i][:], w_ada[:, g * 512:(g + 1) * 512].rearrange("(kc p) n -> p kc n", p=128), writes=[bWAS[i]])
                P.dma(SP, BB[i][:], b_ada[0:1, g * 512:(g + 1) * 512].partition_broadcast(128)[:, 0, :], writes=[bBB[i]])
                for kc in range(8):
                    P.op(PE, MM_(PS0[g % 2][:], CREP[:, kc, :], WAS[i][:, kc, :], kc == 0, kc == 7),
                         reads=[bCONST, bWAS[i]], writes=[bPS0[g % 2]])
                P.op(DVE, TT_(MB[:, g * 512:(g + 1) * 512], PS0[g % 2][:], BB[i][:], ALU.add),
                     reads=[bPS0[g % 2], bBB[i]], writes=[bMB])
            for g in range(8):
                i = g % 2
                P.dma(SP, WIS[i][:], w_in[:, g * 320:(g + 1) * 320].rearrange("(kc p) n -> p kc n", p=128), writes=[bWIS[i]])
                P.op(DVE, CP_(WIN[:, :, g * 320:(g + 1) * 320], WIS[i][:]), reads=[bWIS[i]], writes=[bWIN])
            P.op(DVE, TT_(TMP[:], MB[:].rearrange("p (c j) -> p c j", j=128),
                          IDF[:, :].unsqueeze(1).to_broadcast([128, 48, 128]), ALU.mult),
                 reads=[bMB, bC0], writes=[bTMP])
            P.op(DVE, lambda e: e.tensor_reduce(out=MT[:], in_=TMP[:], axis=AX.X, op=ALU.add), reads=[bTMP], writes=[bMT])
            P.op(DVE, CP_(SH1[:], MT[:, 0:8]), reads=[bMT], writes=[bMOD])
            P.op(DVE, STT_(G1[:], MT[:, 8:16], 1.0, N1G[:], ALU.add, ALU.mult), reads=[bMT, bC0], writes=[bMOD])
            P.op(DVE, CP_(SH2[:], MT[:, 24:32]), reads=[bMT], writes=[bMOD])
            P.op(DVE, STT_(G2[:], MT[:, 32:40], 1.0, N2G[:], ALU.add, ALU.mult), reads=[bMT, bC0], writes=[bMOD])
            P.op(DVE, CP_(G2B[:], MB[:, 5 * D:6 * D]), reads=[bMB], writes=[bG2B])
            for nh in range(2):
                P.dma(SP, WAS[nh][:], w_out[:, nh * 512:(nh + 1) * 512].rearrange("(kc p) n -> p kc n", p=128), writes=[bWAS[nh]])
                for kc in range(8):
                    P.op(DVE, STT_(WOUT[:, kc, nh * 512:(nh + 1) * 512], WAS[nh][:, kc, :], MIXG[:, kc:kc + 1],
                                   MB[:, 2 * D + nh * 512:2 * D + (nh + 1) * 512], ALU.mult, ALU.mult),
                         reads=[bWAS[nh], bMB, bC0], writes=[bWOUT])
            P.op(ACT, A_(LAM[:], LAM[:], AF.Exp, scale=-1.0), reads=[bC0], writes=[bC0])
            P.op(ACT, A_(LAM[:], LAM[:], AF.Ln, bias=1.0), reads=[bC0], writes=[bC0])
            P.op(DVE, TS_(CL[:], LAM[:], -8.0, None, ALU.mult), reads=[bC0], writes=[bCONST])
            P.op(DVE, TS_(CL2[:], LAM[:], -16.0, None, ALU.mult), reads=[bC0], writes=[bCONST])
            P.op(DVE, TS_(NBA[:], LBA[:], -1.0, None, ALU.mult), reads=[bC0], writes=[bCONST])
            P.op(DVE, TS_(NBX[:], LBX[:], -1.0, None, ALU.mult), reads=[bC0], writes=[bCONST])
            P.op(DVE, CP_(WAB[:], WAF[:]), reads=[bC0], writes=[bCONST])
            P.op(DVE, CP_(WXB[:], WXF[:]), reads=[bC0], writes=[bCONST])
            P.emit_segment()

        with contextlib.ExitStack() as s1:
            if stop_after < 1:
                return nc
            sb, ps = mk(s1)
            PBT = sb("PBT", [128, NOWN, 32]); bPBT = Buf()
            GQP = sb("GQP", [128, 1]); GKP = sb("GKP", [128, 1])
            X = [sb("X%d" % i, [128, 2, D]) for i in range(2)]; bX = [Buf(), Buf()]
            J = sb("J", [128, D], BF16); bJ = Buf()
            ST = sb("ST", [128, 8]); bST = Buf()
            HN = sb("HN", [128, 2, D], BF16); bHN = Buf()
            HTd = [sb("HT%d" % i, [128, 8, 256], BF16) for i in range(2)]; bHTd = [Buf(), Buf()]
            SQ = sb("SQ", [128, 512]); bSQ = Buf()
            SK = sb("SK", [128, 24]); bSK = Buf()
            TK = sb("TK", [128, 512]); bTK = Buf()
            KN = sb("KN", [128, 512], BF16); bKN = Buf()
            KTt = sb("KTt", [128, 4, 256], BF16); bKTt = Buf()
            QTt = sb("QTt", [128, 4, 256], BF16); bQTt = Buf()
            KM = sb("KM", [128, 4, 2, 32], BF16); bKM = Buf()
            KMS = sb("KMS", [128, 4]); bKMS = Buf()
            VT = [sb("VT%d" % i, [128, 512], BF16) for i in range(2)]; bVT = [Buf(), Buf()]
            GB = sb("GB", [128, 8, 32]); bGB = Buf()
            M8 = sb("M8", [128, 8, 8]); bM8 = Buf()
            THR = sb("THR", [128, 8]); bTHR = Buf()
            NS = sb("NS", [128, 256], BF16); bNS = Buf()
            NSTt = sb("NSTt", [128, 2, 256], BF16); bNSTt = Buf()
            XR = sb("XR", [128, 4, 259]); bXR = Buf()
            XC = sb("XC", [128, 4, 256]); bXC = Buf()
            XCb = sb("XCb", [128, 4, 256], BF16); bXCb = Buf()
            R = sb("R", [128, 4, 256]); bR = Buf()
            I = sb("I", [128, 4, 256]); bI = Buf()
            AA = sb("AA", [128, 4, 256]); bAA = Buf()
            OM = sb("OM", [128, 4, 256]); bOM = Buf()
            H = sb("H", [128, 4, 256]); bH = Buf()
            HS = sb("HS", [128, 4]); bHS = Buf()
            T1 = sb("T1", [128, 4, 256]); bT1 = Buf()
            YSQ = sb("YSQ", [128, 4, 256], BF16); bYSQ = Buf()
            RS = sb("RS", [128, 256]); bRS = Buf()
            LRN = sb("LRN", [128, 4, 256], BF16); bLRN = Buf()
            PT1 = ps("PT1", [128, 4, 256], BF16); bPT1 = Buf()
            PF4 = ps("PF4", [128, 4, 256])
            PF = [PF4[:, 0:2, :], PF4[:, 2:4, :]]; bPF = [Buf(), Buf()]
            PK = [ps("PK%d" % i, [128, 512]) for i in range(2)]; bPK = [Buf(), Buf()]
            PM0 = ps("PM0", [128, 4, 256], BF16); bPM0 = Buf()
            PM1 = ps("PM1", [128, 512]); bPM1 = Buf()
            PRS = ps("PRS", [128, 512]); bPRS = Buf()

            P.dma(SP, PBT[:], pastb, writes=[bPBT])
            P.dma(SP, GQP[:], gqp, writes=[bPBT])
            P.dma(SP, GKP[:], gkp, writes=[bPBT])
            P.op(DVE, lambda e: e.memset(XR[:, :, 0:3], 0.0), writes=[bXR])
            P.op(DVE, lambda e: e.memset(HS[:], 0.0), writes=[bHS])
            P.op(DVE, lambda e: e.memset(KM[:], 0.0), writes=[bKM])

            def load_x(bi):
                P.dma(SP, X[bi % 2][:], xs[bi * 256:(bi + 1) * 256, :].rearrange("(t p) d -> p t d", p=128),
                      writes=[bX[bi % 2]])

            def norm_T(Xt, bXt, Gm, SHm, HT, bHT):
                for t in range(2):
                    P.op(ACT, A_(J[:], Xt[:, t, :], AF.Square, accum_out=ST[:, t:t + 1]), reads=[bXt], writes=[bJ, bST])
                P.op(ACT, A_(ST[:, 2:4], ST[:, 0:2], AF.Ln, scale=1.0 / D, bias=EPS), reads=[bST], writes=[bST])
                P.op(ACT, A_(ST[:, 4:6], ST[:, 2:4], AF.Exp, scale=-0.5), reads=[bST], writes=[bST])
                for t in range(2):
                    P.op(DVE, TS_(HN[:, t, :], Xt[:, t, :], ST[:, 4 + t:5 + t], None, ALU.mult), reads=[bXt, bST], writes=[bHN])
                for rnd in range(2):
                    for k4 in range(4):
                        kc = rnd * 4 + k4
                        for t in range(2):
                            P.op(PE, TR_(PT1[:, k4, t * 128:(t + 1) * 128], HN[:, t, kc * 128:(kc + 1) * 128], IDB[:]),
                                 reads=[bHN], writes=[bPT1])
                    for k4 in range(4):
                        kc = rnd * 4 + k4
                        P.op(DVE, TS_(HT[:, kc, :], PT1[:, k4, :], Gm[:, kc:kc + 1], SHm[:, kc:kc + 1], ALU.mult, ALU.add),
                             reads=[bPT1, bMOD], writes=[bHT])

            def headnorm(src_ps, bsrc, Gb, dst, bdst):
                P.op(ACT, A_(SQ[:], src_ps, AF.Square), reads=[bsrc], writes=[bSQ])
                P.op(DVE, lambda e: e.tensor_reduce(out=SK[:, 0:8], in_=SQ[:].rearrange("p (h d) -> p h d", h=8), axis=AX.X, op=ALU.add),
                     reads=[bSQ], writes=[bSK])
                P.op(ACT, A_(SK[:, 8:16], SK[:, 0:8], AF.Ln, scale=1.0 / 64, bias=EPS), reads=[bSK], writes=[bSK])
                P.op(ACT, A_(SK[:, 16:24], SK[:, 8:16], AF.Exp, scale=-0.5), reads=[bSK], writes=[bSK])
                P.op(DVE, TT_(dst.rearrange("p (h d) -> p h d", h=8), src_ps.rearrange("p (h d) -> p h d", h=8),
                              SK[:, 16:24].unsqueeze(2).to_broadcast([128, 8, 64]), ALU.mult), reads=[bsrc, bSK], writes=[bdst])

            pkc = [0]

            def chain1(bi, part=0):
                own = bi >= OWN0
                qi = bi - OWN0
                Xt, bXt = X[bi % 2], bX[bi % 2]
                HT, bHT = HTd[bi % 2], bHTd[bi % 2]
                if part in (0, 1):
                    norm_T(Xt, bXt, G1, SH1, HT, bHT)
                if part == 1:
                    return
                for t in range(2):
                    tile_idx = bi * 2 + t
                    bk = pkc[0] % 2; pkc[0] += 1
                    for kc in range(8):
                        P.op(PE, MM_(PK[bk][:], HT[:, kc, t * 128:(t + 1) * 128], WIN[:, kc, 512:1024], kc == 0, kc == 7),
                             reads=[bWIN, bHT], writes=[bPK[bk]])
                    headnorm(PK[bk][:], bPK[bk], None, KN[:], bKN)
                    for pr in range(4):
                        P.op(PE, TR_(PM0[:, pr, t * 128:(t + 1) * 128], KN[:, pr * 128:(pr + 1) * 128], IDB[:]), reads=[bKN], writes=[bPM0])
                    P.op(DVE, TS_(KTt[:, :, t * 128:(t + 1) * 128], PM0[:, :, t * 128:(t + 1) * 128], GKP[:, 0:1], None, ALU.mult), reads=[bPM0, bPBT], writes=[bKTt])
                    bv = pkc[0] % 2; pkc[0] += 1
                    for kc in range(8):
                        P.op(PE, MM_(PK[bv][:], HT[:, kc, t * 128:(t + 1) * 128], WIN[:, kc, 1024:1536], kc == 0, kc == 7),
                             reads=[bWIN, bHT], writes=[bPK[bv]])
                    P.op(ACT, A_(VT[t][:], PK[bv][:], AF.Copy), reads=[bPK[bv]], writes=[bVT[t]])
                    P.dma(POOL, V_s[:, tile_idx, :, :].rearrange("h p d -> p h d"), VT[t][:].rearrange("p (h d) -> p h d", h=8),
                          reads=[bVT[t]])
                P.dma(POOL, KT_s[:, bi * 256:(bi + 1) * 256].rearrange("(pr p) n -> p pr n", p=128), KTt[:], reads=[bKTt])
                P.op(DVE, lambda e: e.tensor_reduce(out=KMS[:], in_=KTt[:], axis=AX.X, op=ALU.add), reads=[bKTt], writes=[bKMS])
                if own:
                    for t in range(2):
                        bq = pkc[0] % 2; pkc[0] += 1
                        for kc in range(8):
                            P.op(PE, MM_(PK[bq][:], HT[:, kc, t * 128:(t + 1) * 128], WIN[:, kc, 0:512], kc == 0, kc == 7),
                                 reads=[bWIN, bHT], writes=[bPK[bq]])
                        headnorm(PK[bq][:], bPK[bq], None, KN[:], bKN)
                        for pr in range(4):
                            P.op(PE, TR_(PM0[:, pr, t * 128:(t + 1) * 128], KN[:, pr * 128:(pr + 1) * 128], IDB[:]), reads=[bKN], writes=[bPM0])
                        P.op(DVE, TS_(QTt[:, :, t * 128:(t + 1) * 128], PM0[:, :, t * 128:(t + 1) * 128], GQP[:, 0:1], None, ALU.mult), reads=[bPM0, bPBT], writes=[bQTt])
                        for pr in range(4):
                            P.op(PE, MM_(PM1[:, pr * 64:(pr + 1) * 64], QTt[:, pr, t * 128:(t + 1) * 128],
                                         KM[:, pr, :, :].rearrange("p s j -> p (s j)"), True, True), reads=[bQTt, bKM], writes=[bPM1])
                        P.op(DVE, TT_(GB[:], PM1[:, 0:256].rearrange("p (h j) -> p h j", h=8),
                                      PBT[:, qi, :].unsqueeze(1).to_broadcast([128, 8, 32]), ALU.add), reads=[bPM1, bPBT], writes=[bGB])
                        for h in range(8):
                            P.op(DVE, lambda e, h=h: e.max(out=M8[:, h, :], in_=GB[:, h, :]), reads=[bGB], writes=[bM8])
                        P.op(DVE, TS_(THR[:], M8[:, :, 2], -1e29, None, ALU.max), reads=[bM8], writes=[bTHR])
                        P.op(DVE, TT_(NS[:].rearrange("p (h j) -> p h j", h=8), GB[:],
                                      THR[:, :].unsqueeze(2).to_broadcast([128, 8, 32]), ALU.is_lt), reads=[bGB, bTHR], writes=[bNS])
                        for g in range(2):
                            P.op(PE, TR_(PT1[:, g, t * 128:(t + 1) * 128], NS[:, g * 128:(g + 1) * 128], IDB[:]), reads=[bNS], writes=[bPT1])
                        P.op(DVE, CP_(NSTt[:, :, t * 128:(t + 1) * 128], PT1[:, 0:2, t * 128:(t + 1) * 128]), reads=[bPT1], writes=[bNSTt])
                    P.dma(POOL, QT_s[:, qi * 256:(qi + 1) * 256].rearrange("(pr p) n -> p pr n", p=128), QTt[:], reads=[bQTt])
                    P.dma(POOL, NS_s[:, qi * 256:(qi + 1) * 256].rearrange("(g p) n -> p g n", p=128), NSTt[:], reads=[bNSTt])
                P.op(DVE, TS_(KM[0:64, :, 0, bi], KMS[0:64, :], 1.0 / 256, None, ALU.mult), reads=[bKMS], writes=[bKM])
                P.op(DVE, TS_(KM[64:128, :, 1, bi], KMS[64:128, :], 1.0 / 256, None, ALU.mult), reads=[bKMS], writes=[bKM])

            def chain2(bi):
                own = bi >= OWN0
                qi = bi - OWN0
                HT, bHT = HTd[bi % 2], bHTd[bi % 2]
                for c in range(4):
                    for kc in range(8):
                        P.op(PE, MM_(PF[c // 2][:, c % 2, :], WIN[:, kc, 1536 + c * 128:1536 + (c + 1) * 128], HT[:, kc, :], kc == 0, kc == 7),
                             reads=[bWIN, bHT], writes=[bPF[c // 2]])
                if bi == OWN0 + 1:
                    P.op(DVE, TS_(XR[:, :, 0:3], XR[:, :, 0:3], FLG[:, 0:1], None, ALU.mult), reads=[bXR, bCONST, bC0], writes=[bXR])
                    P.op(DVE, TS_(HS[:], HS[:], FLG[:, 0:1], None, ALU.mult), reads=[bHS, bC0], writes=[bHS])
                P.op(ACT, A_(XR[:, :, 3:259], PF4[:], AF.Copy), reads=[bPF[0], bPF[1]], writes=[bXR])
                for c in range(4):
                    P.op(ACT, A_(XC[:, c, :], XR[:, c, 0:256], AF.Identity, scale=CW[:, c, 0:1], bias=CB[:, c:c + 1]),
                         reads=[bXR, bC0], writes=[bXC])
                    for j in range(1, 3):
                        P.op(DVE, STT_(XC[:, c, :], XR[:, c, j:j + 256], CW[:, c, j:j + 1], XC[:, c, :], ALU.mult, ALU.add),
                             reads=[bXR, bXC, bC0], writes=[bXC])
                    P.op(DVE, STT_(XCb[:, c, :], XR[:, c, 3:259], CW[:, c, 3:4], XC[:, c, :], ALU.mult, ALU.add),
                         reads=[bXR, bXC, bC0], writes=[bXCb])
                for c in range(4):
                    P.op(DVE, STT_(XC[:, c, :], XR[:, c, 3:259], CW[:, c, 3:4], XC[:, c, :], ALU.mult, ALU.add),
                         reads=[bXR, bXC, bC0], writes=[bXC])
                P.op(DVE, CP_(XR[:, :, 0:3], XR[:, :, 256:259]), reads=[bXR], writes=[bXR])
                for (Wb, NBb, dst, bdst) in ((WAB, NBA, R, bR), (WXB, NBX, I, bI)):
                    for c in range(4):
                        P.op(PE, MM_(PF[c // 2][:, c % 2, :], Wb[:, c, :], XCb[:, c, :], True, True),
                             reads=[bCONST, bXCb], writes=[bPF[c // 2]])
                    for c in range(4):
                        P.op(ACT, A_(dst[:, c, :], PF[c // 2][:, c % 2, :], AF.Exp, scale=-1.0, bias=NBb[:, c:c + 1]),
                             reads=[bPF[c // 2], bCONST], writes=[bdst])
                    P.op(ACT, A_(dst[:], dst[:], AF.Ln, bias=1.0), reads=[bdst], writes=[bdst])
                    P.op(ACT, A_(dst[:], dst[:], AF.Exp, scale=-1.0), reads=[bdst], writes=[bdst])
                    if dst is I:
                        P.op(DVE, TT_(I[:], I[:], XC[:], ALU.mult), reads=[bI, bXC], writes=[bI])
                P.op(DVE, TT_(R[:], R[:], CL[:, :].unsqueeze(2).to_broadcast([128, 4, 256]), ALU.mult), reads=[bR, bCONST], writes=[bR])
                P.op(ACT, A_(AA[:], R[:], AF.Exp), reads=[bR], writes=[bAA])
                P.op(ACT, A_(OM[:], R[:], AF.Exp, scale=2.0), reads=[bR], writes=[bOM])
                P.op(ACT, A_(OM[:], OM[:], AF.Ln, scale=-1.0, bias=1.0), reads=[bOM], writes=[bOM])
                P.op(ACT, A_(OM[:], OM[:], AF.Exp, scale=0.5), reads=[bOM], writes=[bOM])
                P.op(DVE, TT_(OM[:], OM[:], I[:], ALU.mult), reads=[bOM, bI], writes=[bOM])
                for c in range(4):
                    P.op(DVE, lambda e, c=c: e.tensor_tensor_scan(out=H[:, c, :], data0=AA[:, c, :], data1=OM[:, c, :],
                                                                   initial=HS[:, c:c + 1], op0=ALU.mult, op1=ALU.add),
                         reads=[bAA, bOM, bHS], writes=[bH])
                P.op(DVE, CP_(HS[:], H[:, :, 255]), reads=[bH], writes=[bHS])
                if own:
                    for c in range(4):
                        for kc in range(8):
                            P.op(PE, MM_(PF[c // 2][:, c % 2, :], WIN[:, kc, 2048 + c * 128:2048 + (c + 1) * 128], HT[:, kc, :], kc == 0, kc == 7),
                                 reads=[bWIN, bHT], writes=[bPF[c // 2]])
                    for c2 in range(2):
                        cs = slice(2 * c2, 2 * c2 + 2)
                        P.op(ACT, A_(T1[:, cs, :], PF[c2][:], AF.Square, scale=0.21145921592), reads=[bPF[c2]], writes=[bT1])
                        P.op(DVE, STT_(T1[:, cs, :], T1[:, cs, :], 1.0, PF[c2][:], ALU.add, ALU.mult), reads=[bT1, bPF[c2]], writes=[bT1])
                        P.op(ACT, A_(T1[:, cs, :], T1[:, cs, :], AF.Exp, scale=-1.5957691216), reads=[bT1], writes=[bT1])
                        P.op(ACT, A_(T1[:, cs, :], T1[:, cs, :], AF.Ln, bias=1.0), reads=[bT1], writes=[bT1])
                        P.op(ACT, A_(T1[:, cs, :], T1[:, cs, :], AF.Exp, scale=-1.0), reads=[bT1], writes=[bT1])
                        P.op(DVE, TT_(T1[:, cs, :], T1[:, cs, :], PF[c2][:], ALU.mult), reads=[bT1, bPF[c2]], writes=[bT1])
                    P.op(DVE, TT_(T1[:], T1[:], H[:], ALU.mult), reads=[bT1, bH], writes=[bT1])
                    P.op(ACT, A_(YSQ[:], T1[:], AF.Square), reads=[bT1], writes=[bYSQ])
                    for c in range(4):
                        P.op(PE, MM_(PRS[:, 0:256], ONB[:], YSQ[:, c, :], c == 0, c == 3), reads=[bYSQ, bC0], writes=[bPRS])
                    P.op(ACT, A_(RS[:], PRS[:, 0:256], AF.Ln, scale=1.0 / 512, bias=EPS), reads=[bPRS], writes=[bRS])
                    P.op(ACT, A_(RS[:], RS[:], AF.Exp, scale=-0.5), reads=[bRS], writes=[bRS])
                    P.op(DVE, TT_(LRN[:], T1[:], RS[:, :].unsqueeze(1).to_broadcast([128, 4, 256]), ALU.mult), reads=[bT1, bRS], writes=[bLRN])
                    P.dma(POOL, LR_s[:, qi * 256:(qi + 1) * 256].rearrange("(c p) n -> p c n", p=128), LRN[:], reads=[bLRN])

            load_x(0)
            load_x(1)
            chain1(0, 1)
            for bi in range(NB):
                if bi + 1 < NB:
                    chain1(bi + 1, 1)
                chain2(bi)
                chain1(bi, 2)
                if bi + 2 < NB:
                    load_x(bi + 2)
            P.emit_segment()
        sWI.close()

        sW = top.enter_context(contextlib.ExitStack())
        sbW, _ = mk(sW)
        WUP = sbW("WUP", [128, 8, 2 * DFF], BF16); bWUP = Buf()
        with contextlib.ExitStack() as s2:
            if stop_after < 2:
                return nc
            sb, ps = mk(s2)
            WSW = [sb("WSW%d" % i, [128, 512]) for i in range(4)]; bWSW = [Buf() for _ in range(4)]
            KX = [sb("KX%d" % i, [96, S], BF16) for i in range(2)]; bKX = [Buf(), Buf()]
            QX = [sb("QX%d" % i, [96, NQ], BF16) for i in range(2)]; bQX = [Buf(), Buf()]
            VX = [sb("VX%d" % i, [128, 64, 65], BF16) for i in range(2)]; bVX = [Buf(), Buf()]
            PP = [sb("PP%d" % i, [128, 2, 256], BF16) for i in range(3)]; bPP = [Buf() for _ in range(3)]
            AO = [sb("AO%d" % i, [128, 2, 64]) for i in range(2)]; bAO = [Buf(), Buf()]
            RD = sb("RD", [128, 4]); bRD = Buf()
            SB_ = [ps("SB%d" % i, [128, 2, 256]) for i in range(3)]; bSB = [Buf() for _ in range(3)]
            OB = [[ps("OB%d_%d" % (i, q), [128, 512]) for q in range(2)] for i in range(2)]
            bOB = [[Buf(), Buf()], [Buf(), Buf()]]
            for i in range(2):
                P.dma(SP, KX[i][64:96, :], oh, writes=[bKX[i]])
                P.op(DVE, lambda e, i=i: e.memset(VX[i][:, :, 64:65], 1.0), writes=[bVX[i]])

            def load_head(h):
                i = h % 2
                P.dma(SP, KX[i][0:64, :], KT_s[h * 64:(h + 1) * 64, :], writes=[bKX[i]])
                P.dma(SP, QX[i][0:64, :], QT_s[h * 64:(h + 1) * 64, :], writes=[bQX[i]])
                P.dma(SP, QX[i][64:96, :], NS_s[h * 32:(h + 1) * 32, :], writes=[bQX[i]])
                P.dma(SP, VX[i][:, :, 0:64], V_s[h].rearrange("t p d -> p t d"), writes=[bVX[i]])

            its = []
            oi = 0
            for h in range(8):
                for qi in range(NOWN):
                    bi = OWN0 + qi
                    ob = oi % 2; oi += 1
                    for j in range(bi + 1):
                        its.append((h, qi, bi, ob, j))

            def emit_qk(n):
                h, qi, bi, ob, j = its[n]
                hb = h % 2
                sbk = n % 3
                qc = slice(qi * 256, (qi + 1) * 256)
                diag = (j == bi)
                for kh in range(2):
                    kc_ = slice(j * 256 + kh * 128, j * 256 + (kh + 1) * 128)
                    if not diag:
                        P.op(PE, MM_(SB_[sbk][:, kh, :], KX[hb][0:96, kc_], QX[hb][0:96, qc], True, True),
                             reads=[bKX[hb], bQX[hb]], writes=[bSB[sbk]])
                    else:
                        P.op(PE, MM_(SB_[sbk][:, kh, :], KX[hb][0:64, kc_], QX[hb][0:64, qc], True, False),
                             reads=[bKX[hb], bQX[hb]], writes=[bSB[sbk]])
                        P.op(PE, MM_(SB_[sbk][:, kh, :], IDB[:], MKD[:, kh, :], False, True),
                             reads=[bC0], writes=[bSB[sbk]])

            def emit_rest(n):
                h, qi, bi, ob, j = its[n]
                hb = h % 2
                sbk = n % 3
                nj = bi + 1
                P.op(ACT, A_(PP[sbk][:], SB_[sbk][:], AF.Exp, scale=0.125), reads=[bSB[sbk]], writes=[bPP[sbk]])
                if n + 2 < len(its):
                    emit_qk(n + 2)
                for kh in range(2):
                    for qh in range(2):
                        P.op(PE, MM_(OB[ob][qh][:, 0:65], PP[sbk][:, kh, qh * 128:(qh + 1) * 128], VX[hb][:, j * 2 + kh, :],
                                     (j == 0 and kh == 0), (j == nj - 1 and kh == 1)),
                             reads=[bPP[sbk], bVX[hb]], writes=[bOB[ob][qh]])
                if j == nj - 1:
                    for qh in range(2):
                        P.op(DVE, lambda e, ob=ob, qh=qh: e.reciprocal(RD[:, ob * 2 + qh:ob * 2 + qh + 1], OB[ob][qh][:, 64:65]),
                             reads=[bOB[ob][qh]], writes=[bRD])
                        P.op(DVE, TS_(AO[ob][:, qh, :], OB[ob][qh][:, 0:64], RD[:, ob * 2 + qh:ob * 2 + qh + 1], None, ALU.mult),
                             reads=[bOB[ob][qh], bRD], writes=[bAO[ob]])
                    P.dma(POOL, AT_s[qi * 256:(qi + 1) * 256, h * 64:(h + 1) * 64].rearrange("(qh p) d -> p qh d", p=128), AO[ob][:],
                          reads=[bAO[ob]])

            load_head(0)
            load_head(1)
            def wup_piece(g):
                i = g % 4
                P.dma(SP, WSW[i][:].rearrange("p (kc n) -> p kc n", kc=8),
                      w_up[:, g * 64:(g + 1) * 64].rearrange("(kc p) n -> p kc n", p=128), writes=[bWSW[i]])
                P.op(DVE, CP_(WUP[:, :, g * 64:(g + 1) * 64], WSW[i][:].rearrange("p (kc n) -> p kc n", kc=8)),
                     reads=[bWSW[i]], writes=[bWUP])
            wnext = [0]
            emit_qk(0)
            emit_qk(1)
            for n in range(len(its)):
                h, qi, bi, ob, j = its[n]
                if n > 0 and its[n - 1][0] != h and h + 1 < 8:
                    load_head(h + 1)
                if n % 24 == 12 and wnext[0] < 88:
                    wup_piece(wnext[0]); wnext[0] += 1
                emit_rest(n)
            while wnext[0] < 88:
                wup_piece(wnext[0]); wnext[0] += 1
            P.sched = False
            P.emit_segment()
            P.sched = True

        with contextlib.ExitStack() as s3:
            if stop_after < 3:
                return nc
            sb, ps = mk(s3)
            WDN = sb("WDN", [128, 22, D], BF16); bWDN = Buf()
            XD = [sb("XT3_%d" % i, [128, 2, D]) for i in range(2)]; bXD = [Buf(), Buf()]
            AT = sb("AT", [128, 512]); bAT = Buf()
            ST = sb("ST3", [128, 8]); bST = Buf()
            AN = sb("AN", [128, 512], BF16); bAN = Buf()
            HTD = [sb("HTD%d" % i, [128, 8, 256], BF16) for i in range(2)]; bHTD = [Buf(), Buf()]
            HN = sb("HN3", [128, D], BF16); bHN = Buf()
            UB = [sb("UB%d" % i, [128, 2, 258]) for i in range(3)]; bUB = [Buf() for _ in range(3)]; bUBh = [Buf() for _ in range(3)]
            ACC = [sb("ACC%d" % i, [128, 2, 256]) for i in range(3)]; bACC = [Buf() for _ in range(3)]
            WS3 = [ACC[i][:].rearrange("p a n -> p (a n)") for i in range(3)]; bWS3 = bACC
            UH = sb("UH", [128, 44, 2]); bUH = Buf()
            ACTT = sb("ACTT", [128, 22, 256], BF16); bACTT = Buf(); bACTTm = [Buf() for _ in range(22)]
            PY = [ps("PY%d" % i, [128, 512]) for i in range(4)]; bPY = [Buf() for _ in range(4)]
            PT1 = ps("PT3", [128, 4, 256], BF16); bPT1 = Buf()
            PU = [ps("PU%d" % i, [128, 2, 256]) for i in range(3)]; bPU = [Buf() for _ in range(3)]

            ACTF = ACTT[:].rearrange("p m n -> p (m n)").bitcast(F32)
            STG = [(WS3[0], bACC[0]), (WS3[1], bACC[1]), (WS3[2], bACC[2])] + [(ACTF[:, k * 512:(k + 1) * 512], Buf()) for k in range(5)]
            wi = 0
            for m in range(22):
                for nh in range(2):
                    stg, bstg = STG[wi % len(STG)]; wi += 1
                    P.dma(SP, stg, w_down[m * 128:(m + 1) * 128, nh * 512:(nh + 1) * 512], writes=[bstg])
                    P.op(DVE, TT_(WDN[:, m, nh * 512:(nh + 1) * 512], stg, G2B[:, nh * 512:(nh + 1) * 512], ALU.mult),
                         reads=[bstg, bG2B], writes=[bWDN])
            P.op(DVE, _w(lambda e: e.memset(UH[:], 0.0), 88), reads=[b for _, b in STG[3:]], writes=[bUH] + bACTTm)

            upc = [0]

            def front(qi):
                bi = OWN0 + qi
                Xt, bXt = XD[qi % 2], bXD[qi % 2]
                MIXT = HT = HTD[qi % 2]
                bMIXL = bMIXA = bHT = bHTD[qi % 2]
                P.dma(SP, Xt[:], xs[bi * 256:(bi + 1) * 256, :].rearrange("(t p) d -> p t d", p=128), writes=[bXt])
                P.dma(SP, MIXT[:, 0:4, :], LR_s[:, qi * 256:(qi + 1) * 256].rearrange("(c p) n -> p c n", p=128), writes=[bMIXL])
                for t in range(2):
                    P.dma(SP, AT[:], AT_s[qi * 256 + t * 128:qi * 256 + (t + 1) * 128, :], writes=[bAT])
                    P.op(ACT, A_(AN[:], AT[:], AF.Square, accum_out=ST[:, 0:1]), reads=[bAT], writes=[bAN, bST])
                    P.op(ACT, A_(ST[:, 2:3], ST[:, 0:1], AF.Ln, scale=1.0 / 512, bias=EPS), reads=[bST], writes=[bST])
                    P.op(ACT, A_(ST[:, 4:5], ST[:, 2:3], AF.Exp, scale=-0.5), reads=[bST], writes=[bST])
                    P.op(DVE, TS_(AN[:], AT[:], ST[:, 4:5], None, ALU.mult), reads=[bAT, bST], writes=[bAN])
                    for c in range(4):
                        P.op(PE, TR_(PT1[:, c, t * 128:(t + 1) * 128], AN[:, c * 128:(c + 1) * 128], IDB[:]), reads=[bAN], writes=[bPT1])
                P.op(DVE, CP_(MIXT[:, 4:8, :], PT1[:]), reads=[bPT1], writes=[bMIXA])
                for t in range(2):
                    for nh in range(2):
                        pb = t * 2 + nh
                        for kc in range(8):
                            P.op(PE, MM_(PY[pb][:], MIXT[:, kc, t * 128:(t + 1) * 128], WOUT[:, kc, nh * 512:(nh + 1) * 512], kc == 0, kc == 7),
                                 reads=[bMIXL, bMIXA, bWOUT], writes=[bPY[pb]])
                        P.op(DVE, TT_(Xt[:, t, nh * 512:(nh + 1) * 512], PY[pb][:], Xt[:, t, nh * 512:(nh + 1) * 512], ALU.add),
                             reads=[bPY[pb], bXt], writes=[bXt])
                if debug:
                    P.dma(POOL, X1_s[qi * 256:(qi + 1) * 256, :].rearrange("(t p) d -> p t d", p=128), Xt[:], reads=[bXt])
                for t in range(2):
                    P.op(ACT, A_(HN[:], Xt[:, t, :], AF.Square, accum_out=ST[:, t:t + 1]), reads=[bXt], writes=[bHN, bST])
                P.op(ACT, A_(ST[:, 2:4], ST[:, 0:2], AF.Ln, scale=1.0 / D, bias=EPS), reads=[bST], writes=[bST])
                P.op(ACT, A_(ST[:, 4:6], ST[:, 2:4], AF.Exp, scale=-0.5), reads=[bST], writes=[bST])
                HT2 = HT[:].rearrange("p k (t n) -> p k t n", t=2)
                for t in range(2):
                    P.op(DVE, TS_(HN[:], Xt[:, t, :], ST[:, 4 + t:5 + t], None, ALU.mult), reads=[bXt, bST], writes=[bHN])
                    for rnd in range(2):
                        for k4 in range(4):
                            kc = rnd * 4 + k4
                            P.op(PE, TR_(PT1[:, k4, 0:128], HN[:, kc * 128:(kc + 1) * 128], IDB[:]), reads=[bHN], writes=[bPT1])
                        for k4 in range(4):
                            kc = rnd * 4 + k4
                            P.op(DVE, TS_(HT[:, kc, t * 128:(t + 1) * 128], PT1[:, k4, 0:128], G2[:, kc:kc + 1], SH2[:, kc:kc + 1], ALU.mult, ALU.add),
                                 reads=[bPT1, bMOD], writes=[bHT])

            def rest(qi):
                bi = OWN0 + qi
                Xt, bXt = XD[qi % 2], bXD[qi % 2]
                HT = HTD[qi % 2]
                bHT = bHTD[qi % 2]
                if qi == 1:
                    P.op(DVE, TS_(UH[:], UH[:], FLG[:, 0:1], None, ALU.mult), reads=[bUH, bC0], writes=[bUH])
                for m in range(22):
                    u = upc[0] % 3; upc[0] += 1
                    bPU_, bUB_, bACC_ = bPU, bUB, bACC
                    for half, mm in ((0, m), (1, m + 22)):
                        for kc in range(8):
                            P.op(PE, MM_(PU[u][:, half, :], WUP[:, kc, mm * 128:(mm + 1) * 128], HT[:, kc, :], kc == 0, kc == 7),
                                 reads=[bWUP, bHT], writes=[bPU_[u]])
                    if qi > 0:
                        for half, mm in ((0, m), (1, m + 22)):
                            P.op(POOL, CP_(UB[u][:, half, 0:2], UH[:, mm, :]), reads=[bUH], writes=[bUBh[u]])
                    P.op(ACT, A_(UB[u][:, :, 2:258], PU[u][:], AF.Copy), reads=[bPU_[u]], writes=[bUB_[u]])
                    for half, mm in ((0, m), (1, m + 22)):
                        P.op(POOL, CP_(UH[:, mm, :], UB[u][:, half, 256:258]), reads=[bUB_[u]], writes=[bUH])
                    if qi == 0:
                        continue
                    for half, mm in ((0, m), (1, m + 22)):
                        P.op(ACT, A_(ACC[u][:, half, :], UB[u][:, half, 0:256], AF.Identity, scale=FCW[:, mm, 0:1], bias=FCB[:, mm:mm + 1]),
                             reads=[bUB_[u], bUBh[u], bC0], writes=[bACC_[u]])
                        for jj in (1, 2):
                            P.op(DVE, STT_(ACC[u][:, half, :], UB[u][:, half, jj:jj + 256], FCW[:, mm, jj:jj + 1], ACC[u][:, half, :], ALU.mult, ALU.add),
                                 reads=[bUB_[u], bUBh[u], bACC_[u], bC0], writes=[bACC_[u]])
                    P.op(ACT, A_(ACC[u][:, 0, :], ACC[u][:, 0, :], AF.Silu), reads=[bACC_[u]], writes=[bACC_[u]])
                    P.op(POOL, TT_(ACTT[:, m, :], ACC[u][:, 0, :], ACC[u][:, 1, :], ALU.mult), reads=[bACC_[u]], writes=[bACTTm[m]])
                if qi == 0:
                    return
                for m in range(22):
                    for t in range(2):
                        for nh in range(2):
                            pb = t * 2 + nh
                            P.op(PE, MM_(PY[pb][:], ACTT[:, m, t * 128:(t + 1) * 128], WDN[:, m, nh * 512:(nh + 1) * 512], m == 0, m == 21),
                                 reads=[bACTTm[m], bWDN], writes=[bPY[pb]])
                for t in range(2):
                    for nh in range(2):
                        pb = t * 2 + nh
                        P.op(DVE, TT_(Xt[:, t, nh * 512:(nh + 1) * 512], PY[pb][:], Xt[:, t, nh * 512:(nh + 1) * 512], ALU.add),
                             reads=[bPY[pb], bXt], writes=[bXt])
                P.dma(POOL, out[(qi - 1) * 256:qi * 256, :].rearrange("(t p) d -> p t d", p=128), Xt[:], reads=[bXt])

            front(0)
            for qi in range(NOWN):
                if qi + 1 < NOWN:
                    front(qi + 1)
                rest(qi)
            P.emit_segment(last=True)
            nc._est_log = P.est_log
    return nc


def make_in_maps(x, c, w_ada, b_ada, norm1_g, w_in, q_norm_g, k_norm_g, lru_conv_w, lru_conv_b,
                 lru_wa, lru_ba, lru_wx, lru_bx, lru_lambda, lru_out_g, attn_out_g, w_out,
                 norm2_g, w_up, ffn_conv_w, ffn_conv_b, w_down):
    f = lambda a: np.ascontiguousarray(np.asarray(a, dtype=np.float32))
    x = f(x); c = f(c)

    def fm(v, nch):
        return f(np.asarray(v, np.float32).reshape(nch, 128).T)

    def bd(w):
        w = np.asarray(w, np.float32)
        o = np.zeros((128, 4, 128), np.float32)
        for ch in range(4):
            for s_ in range(2):
                o[s_ * 64:(s_ + 1) * 64, ch, s_ * 64:(s_ + 1) * 64] = w[ch * 2 + s_]
        return o

    shared = {
        "w_ada": f(w_ada[0]), "b_ada": f(b_ada[0]).reshape(1, -1),
        "n1g": fm(norm1_g[0], 8), "n2g": fm(norm2_g[0], 8),
        "mixg": fm(np.concatenate([np.asarray(lru_out_g[0]), np.asarray(attn_out_g[0])]), 8),
        "w_in": f(w_in[0]),
        "gqp": f(np.tile(np.asarray(q_norm_g[0], np.float32), 2).reshape(128, 1)),
        "gkp": f(np.tile(np.asarray(k_norm_g[0], np.float32), 2).reshape(128, 1)),
        "cw": f(np.asarray(lru_conv_w[0], np.float32).reshape(4, 4, 128).transpose(2, 1, 0)),
        "cb": fm(lru_conv_b[0], 4),
        "wabd": bd(lru_wa[0]), "wxbd": bd(lru_wx[0]),
        "lba": fm(lru_ba[0], 4), "lbx": fm(lru_bx[0], 4), "lam": fm(lru_lambda[0], 4),
        "w_out": f(w_out[0]), "w_up": f(w_up[0]),
        "fcw": f(np.asarray(ffn_conv_w[0], np.float32).reshape(3, 44, 128).transpose(2, 1, 0)),
        "fcb": fm(ffn_conv_b[0], 44),
        "w_down": f(w_down[0]),
        "identb": np.eye(128, dtype=np.float32).astype(ml_dtypes.bfloat16),
        "identf": np.eye(128, dtype=np.float32),
        "onesb": np.ones((128, 128), np.float32).astype(ml_dtypes.bfloat16),
    }
    ohm = np.zeros((32, S), np.float32)
    for j in range(32):
        ohm[j, j * 256:(j + 1) * 256] = NEG
    shared["oh"] = ohm.astype(ml_dtypes.bfloat16)
    kk = np.arange(128)[:, None]
    qq = np.arange(128)[None, :]
    tri = np.where(kk <= qq, 0.0, NEG).astype(np.float32)
    md = np.zeros((128, 2, 256), np.float32)
    md[:, 0, 0:128] = tri
    md[:, 1, 0:128] = NEG
    md[:, 1, 128:256] = tri
    shared["maskd"] = md.astype(ml_dtypes.bfloat16)
    in_maps = []
    for core in range(8):
        b, half = core // 2, core % 2
        if half == 1:
            xsv = x[b]
        else:
            xsv = np.concatenate([np.zeros((4096, D), np.float32), x[b, :4096]], axis=0)
        pb = np.full((NOWN, 32), -1e30, np.float32)
        for qi in range(NOWN):
            bi = OWN0 + qi
            lo = 16 if half == 0 else 0
            if bi > lo:
                pb[qi, lo:bi] = 0.0
        m = dict(shared)
        m["xs"] = f(xsv)
        m["cT"] = fm(c[b], 8)
        m["flag"] = np.full((128, 1), float(half), np.float32)
        m["pastb"] = f(np.broadcast_to(pb[None], (128, NOWN, 32)))
        in_maps.append(m)
    return in_maps


_NC = {}


def kernel(**inputs):
    in_maps = make_in_maps(**inputs)
    if "nc" not in _NC:
        _NC["nc"] = build(False)
    res = run_bass_kernel_spmd(_NC["nc"], in_maps, core_ids=list(range(8)))
    outp = np.zeros((4, S, D), np.float32)
    for core in range(8):
        b, half = core // 2, core % 2
        outp[b, half * 4096:(half + 1) * 4096] = res.results[core]["out"]
    return outp
```
